# Optimizing a Trainium2 kernel written in Bass

```python
import math
import jax, jax.numpy as jnp
from jax import lax
import numpy as np

D_MODEL = 4096
BATCH = 1
SEQ = 8192
DEPTH = 1

CHUNK = 64
GMLP_BLOCK = 128
GMLP_GROUPS = 16
GMLP_WIDTH = D_MODEL // 2
GMLP_GROUP_DIM = GMLP_WIDTH // GMLP_GROUPS
SB_HEAD_DIM = 128
SB_HEADS = (D_MODEL // 2) // SB_HEAD_DIM
SB_WIDTH = SB_HEADS * SB_HEAD_DIM
Q_BLOCK = 128
D_FF = 4 * D_MODEL
N_MOD = 6
EPS = 1e-6
IN_COLS = 2 * GMLP_WIDTH + 3 * SB_WIDTH + 2 * D_MODEL

kernel_name = "hybrid_gmlp_stickbreaking_block"


def rms_norm(x, g):
    xf = x.astype(jnp.float32)
    inv = lax.rsqrt(jnp.mean(xf * xf, axis=-1, keepdims=True) + EPS)
    return (xf * inv * g.astype(jnp.float32)).astype(x.dtype)


def layer_norm(x, g):
    xf = x.astype(jnp.float32)
    mu = jnp.mean(xf, axis=-1, keepdims=True)
    xc = xf - mu
    inv = lax.rsqrt(jnp.mean(xc * xc, axis=-1, keepdims=True) + EPS)
    return (xc * inv * g.astype(jnp.float32)).astype(x.dtype)


def gmlp_branch(zuv, g_v, w_s, b_s):
    B, S, _ = zuv.shape
    zuv = jax.nn.gelu(zuv, approximate=False)
    u, v = jnp.split(zuv, 2, axis=-1)
    v = layer_norm(v, g_v)
    nblk = S // GMLP_BLOCK
    v = v.reshape(B, nblk, GMLP_BLOCK, GMLP_GROUPS, GMLP_GROUP_DIM)
    pos = jnp.arange(GMLP_BLOCK)
    mask = (pos[:, None] // CHUNK) >= (pos[None, :] // CHUNK)
    w = jnp.where(mask[None], w_s, jnp.zeros((), w_s.dtype))
    y = jnp.einsum('gts,bnsgc->bntgc', w, v) + b_s.T[None, None, :, :, None]
    y = y.reshape(B, S, GMLP_WIDTH)
    return u * y


def stick_breaking_branch(q, k, v):
    B, S, _ = q.shape
    to_heads = lambda t: t.reshape(B, S, SB_HEADS, SB_HEAD_DIM).transpose(0, 2, 1, 3)
    qh = to_heads(q).astype(jnp.float32)
    kh = to_heads(k).astype(jnp.float32)
    vh = to_heads(v).astype(jnp.float32)
    nb = S // Q_BLOCK
    qb = qh.reshape(B, SB_HEADS, nb, Q_BLOCK, SB_HEAD_DIM).transpose(2, 0, 1, 3, 4)
    key_pos = jnp.arange(S)
    scale = 1.0 / math.sqrt(SB_HEAD_DIM)

    def one_block(args):
        q_blk, blk = args
        z = jnp.einsum('bhqd,bhkd->bhqk', q_blk, kh) * scale
        qpos = blk * Q_BLOCK + jnp.arange(Q_BLOCK)
        mask = key_pos[None, :] < qpos[:, None]
        log_beta = jax.nn.log_sigmoid(z)
        log_one_minus = jnp.where(mask, jax.nn.log_sigmoid(-z), 0.0)
        between = lax.cumsum(log_one_minus, axis=3, reverse=True) - log_one_minus
        a = jnp.where(mask, jnp.exp(log_beta + between), 0.0)
        return jnp.einsum('bhqk,bhkd->bhqd', a, vh)

    o = lax.map(one_block, (qb, jnp.arange(nb)))
    o = o.transpose(1, 0, 3, 2, 4).reshape(B, S, SB_WIDTH)
    return o.astype(q.dtype)


def setup_inputs(seed: int = 0) -> dict:
    key = jax.random.key(seed)
    ks = jax.random.split(key, 20)
    f32 = jnp.float32
    nrm = lambda k, shape, s: jax.random.normal(k, shape, f32) * s
    L = DEPTH
    return {
        "x": nrm(ks[0], (BATCH, SEQ, D_MODEL), 1.0),
        "c": nrm(ks[1], (BATCH, D_MODEL), 1.0),
        "w_ada": nrm(ks[2], (L, D_MODEL, N_MOD * D_MODEL), 0.2 * D_MODEL ** -0.5),
        "b_ada": nrm(ks[3], (L, N_MOD * D_MODEL), 0.05),
        "g_pre_mix": 1.0 + nrm(ks[4], (L, D_MODEL), 0.05),
        "w_in": nrm(ks[5], (L, D_MODEL, IN_COLS), D_MODEL ** -0.5),
        "g_v": 1.0 + nrm(ks[6], (L, GMLP_WIDTH), 0.05),
        "w_s": nrm(ks[7], (L, GMLP_GROUPS, GMLP_BLOCK, GMLP_BLOCK), GMLP_BLOCK ** -0.5),
        "b_s": 1.0 + nrm(ks[8], (L, GMLP_GROUPS, GMLP_BLOCK), 0.1),
        "w_proj_a": nrm(ks[9], (L, GMLP_WIDTH, D_MODEL), GMLP_WIDTH ** -0.5),
        "w_proj_b": nrm(ks[10], (L, SB_WIDTH, D_MODEL), SB_WIDTH ** -0.5),
        "w_o": nrm(ks[11], (L, D_MODEL, D_MODEL), D_MODEL ** -0.5),
        "g_post_mix": 1.0 + nrm(ks[12], (L, D_MODEL), 0.05),
        "g_pre_mlp": 1.0 + nrm(ks[13], (L, D_MODEL), 0.05),
        "w_ff1": nrm(ks[14], (L, D_MODEL, D_FF), D_MODEL ** -0.5),
        "w_ff2": nrm(ks[15], (L, D_FF, D_MODEL), D_FF ** -0.5),
        "g_post_mlp": 1.0 + nrm(ks[16], (L, D_MODEL), 0.05),
    }


def reference(x, c, w_ada, b_ada, g_pre_mix, w_in, g_v, w_s, b_s, w_proj_a,
              w_proj_b, w_o, g_post_mix, g_pre_mlp, w_ff1, w_ff2, g_post_mlp):
    cut = [int(i) for i in np.cumsum([2 * GMLP_WIDTH, SB_WIDTH, SB_WIDTH, SB_WIDTH, D_MODEL])]
    for l in range(DEPTH):
        mod = (jax.nn.silu(c) @ w_ada[l] + b_ada[l])[:, None, :]
        sh1, sc1, gt1, sh2, sc2, gt2 = jnp.split(mod, N_MOD, axis=-1)

        h = rms_norm(x, g_pre_mix[l]) * (1.0 + sc1) + sh1
        proj = h @ w_in[l]
        zuv, q, k, v, gate_a, gate_b = jnp.split(proj, cut, axis=-1)
        y_a = gmlp_branch(zuv, g_v[l], w_s[l], b_s[l]) @ w_proj_a[l]
        y_b = stick_breaking_branch(q, k, v) @ w_proj_b[l]
        mix = (jax.nn.sigmoid(gate_a) * y_a + jax.nn.sigmoid(gate_b) * y_b) @ w_o[l]
        x = x + gt1 * rms_norm(mix, g_post_mix[l])

        h2 = rms_norm(x, g_pre_mlp[l]) * (1.0 + sc2) + sh2
        ff = jnp.square(jax.nn.relu(h2 @ w_ff1[l])) @ w_ff2[l]
        x = x + gt2 * rms_norm(ff, g_post_mlp[l])
    return x
```

```python
import os
from contextlib import ExitStack
import numpy as np
import concourse.bass as bass
import concourse.mybir as mybir
from concourse.bass_utils import run_bass_kernel_spmd

F32 = mybir.dt.float32
BF16 = mybir.dt.bfloat16
AF = mybir.ActivationFunctionType
ALU = mybir.AluOpType
EPS = 1e-6
ENGS = ("sync", "scalar", "gpsimd", "vector", "tensor")


class Cfg:
    def __init__(self, D=4096, S=8192, NC=8):
        self.D, self.S, self.NC = D, S, NC
        self.KC = D // 128
        self.TO = S // NC
        self.NTO = self.TO // 128
        self.NTA = S // 128
        self.GW = D // 2
        self.NH = self.GW // 128
        self.DFF = 4 * D
        self.NMOD = 6 * D
        self.INC = 2 * self.GW + 3 * self.GW + 2 * D


class CS:
    def __init__(self, h):
        self.h = h
        self.n = 0


class Prog:
    def __init__(self, nc):
        self.nc = nc
        self.q = {e: [] for e in ENGS}
        self.waited = {e: {} for e in ENGS}
        self.nblk = 0

    def op(self, eng, fn, waits=(), sig=None, amt=1):
        tok = None
        if sig is not None:
            sig.n += amt
            tok = (sig, sig.n)
        ws = []
        seen = self.waited[eng]
        stack = list(waits)
        while stack:
            w = stack.pop()
            if w is None:
                continue
            if isinstance(w, list):
                stack.extend(w)
                continue
            s, v = w
            if seen.get(id(s), 0) >= v:
                continue
            seen[id(s)] = v
            ws.append((s, v))
        self.q[eng].append((fn, ws, sig, amt))
        return tok

    def dma(self, eng, out, in_, waits=(), sig=None, **kw):
        return self.op(eng, lambda e: e.dma_start(out=out, in_=in_, **kw), waits, sig, 16)

    def flush(self, name=None):
        self.nblk += 1
        name = name or f"blk{self.nblk}"
        with self.nc.Block(name) as block:
            for eng in ENGS:
                items = self.q[eng]
                if not items:
                    continue

                def body(e, items=items):
                    for fn, ws, sig, amt in items:
                        for (s, v) in ws:
                            e.wait_ge(s.h, v)
                        if fn is None:
                            continue
                        ins = fn(e)
                        if sig is not None:
                            ins.then_inc(sig.h, amt)

                getattr(block, eng)(body)
        self.q = {e: [] for e in ENGS}


def build_nc(cfg, stop_after=None, debug=False):
    D, S, KC, TO, NTO, NTA, GW, NH, DFF = cfg.D, cfg.S, cfg.KC, cfg.TO, cfg.NTO, cfg.NTA, cfg.GW, cfg.NH, cfg.DFF
    GKC = GW // 128
    nc = bass.Bass("TRN2", target_bir_lowering=False)
    P = Prog(nc)

    def din(name, shape, dt=F32):
        return nc.dram_tensor(name, list(shape), dt, kind="ExternalInput").ap()

    def dscr(name, shape, dt):
        return nc.dram_tensor(name, list(shape), dt).ap()

    xr = din("xr", [S, D])
    xo = din("xo", [TO, D])
    cv = din("cv", [128, KC])
    wada = din("wada", [D, cfg.NMOD])
    bada = din("bada", [1, cfg.NMOD])
    win = din("win", [D, cfg.INC])
    gpre = din("gpre", [1, D])
    gv = din("gv", [1, GW])
    wsT = din("wsT", [128, NH, 128])
    bsr = din("bsr", [128, NH])
    wpa = din("wpa", [GW, D])
    wpb = din("wpb", [GW, D])
    wo = din("wo", [D, D])
    gpost = din("gpost", [1, D])
    gpre2 = din("gpre2", [1, D])
    wf1 = din("wf1", [D, DFF])
    wf2 = din("wf2", [DFF, D])
    gpost2 = din("gpost2", [1, D])
    identf = din("identf", [128, 128])
    kpos = din("kpos", [1, S])
    qpos = din("qpos", [128, NTO])
    y = nc.dram_tensor("y", [TO, D], F32, kind="ExternalOutput").ap()

    dbg = {}

    def dout(name, src, dt=F32):
        if not debug:
            return
        a = nc.dram_tensor("dbg_" + name, list(src.shape), dt, kind="ExternalOutput").ap()
        dbg[name] = (a, src)

    mod_d = dscr("mod_d", [1, cfg.NMOD], F32)
    hT_d = dscr("hT_d", [NTA // 4, 128, KC, 512], BF16)
    kT_d = dscr("kT_d", [NH, 128, S], BF16)
    v_d = dscr("v_d", [S, GW], BF16)
    qT_d = dscr("qT_d", [NH, 128, TO], BF16)
    u_d = dscr("u_d", [TO, GW], BF16)
    vr_d = dscr("vr_d", [TO, GW], F32)
    sga_d = dscr("sga_d", [TO, D], BF16)
    sgb_d = dscr("sgb_d", [TO, D], BF16)
    gm_d = dscr("gm_d", [TO, GW], BF16)
    o_d = dscr("o_d", [TO, GW], BF16)
    t1_d = dscr("t1_d", [TO, D], F32)
    mix_d = dscr("mix_d", [TO, D], BF16)
    mo_d = dscr("mo_d", [TO, D], F32)
    x1_d = dscr("x1_d", [TO, D], F32)
    f_d = dscr("f_d", [TO, DFF], BF16)
    acc_d = dscr("acc_d", [TO, D], F32)

    with ExitStack() as es:
        def sem(name):
            return CS(es.enter_context(nc.semaphore(name)))

        S_pe, S_act, S_dve = sem("s_pe"), sem("s_act"), sem("s_dve")
        S_ld = sem("s_ld")
        S_st = sem("s_st")
        ring_sems = [sem(f"rs{i}") for i in range(24)]
        psf = [es.enter_context(nc.psum_tensor(f"psf{i}", [128, 512], F32)) for i in range(6)]
        psbs = [es.enter_context(nc.psum_tensor(f"psb{i}", [128, 1024], BF16)) for i in range(2)]
        ident_f = es.enter_context(nc.sbuf_tensor("ident_f", [128, 128], F32))
        ident_b = es.enter_context(nc.sbuf_tensor("ident_b", [128, 128], BF16))
        eps_t = es.enter_context(nc.sbuf_tensor("eps_t", [128, 1], F32))

        state = {"rs": 0}
        PSF = {}
        PI = [0]

        def next_bank():
            b = 2 + PI[0] % 4
            PI[0] += 1
            return b

        def new_ring_sem():
            assert state["rs"] < len(ring_sems), "out of ring semaphores in this stage"
            s_ = ring_sems[state["rs"]]
            state["rs"] += 1
            return s_

        stagers = []

        def wait_stagers():
            toks = []
            for s_ in stagers:
                for fr in s_.ring.free:
                    toks.extend(fr)
            P.op("sync", None, toks)
            stagers.clear()

        def end_stage(name):
            wait_stagers()
            P.op("sync", None, [(S_st, S_st.n), (S_ld, S_ld.n)])
            P.flush(name)
            state["rs"] = 0
            if os.environ.get("MK_VERBOSE"):
                print("stage", name, "sem counts pe/act/dve/ld/st", S_pe.n, S_act.n, S_dve.n, S_ld.n, S_st.n,
                      "ring max", max(r.n for r in ring_sems), flush=True)
            return stop_after == name

        def finish():
            for name, (a, src) in dbg.items():
                P.dma("sync", a, src, [], S_st)
            P.op("sync", None, [(S_st, S_st.n)])
            P.flush("fin")
            return nc

        class Ring:
            def __init__(self, st, name, n, shape, dt):
                self.bufs = [st.enter_context(nc.sbuf_tensor(f"{name}{i}", list(shape), dt)) for i in range(n)]
                self.sems = [new_ring_sem() for _ in range(n)]
                self.free = [[] for _ in range(n)]
                self.i = 0

            def next(self):
                k = self.i % len(self.bufs)
                self.i += 1
                fr = self.free[k]
                self.free[k] = []
                return k, self.bufs[k], self.sems[k], fr

        class Stager:
            def __init__(self, st, name, n, shape, dt):
                self.ring = Ring(st, name, n, shape, dt)

            def get(self):
                k, t, s_, fr = self.ring.next()
                self.k, self.s = k, s_
                return t, fr

            def store(self, dst, src, waits):
                tok = P.dma("sync", dst, src, waits, self.s)
                self.ring.free[self.k] = [tok]
                return tok

        def mk_stager(st, name, n, shape, dt):
            s_ = Stager(st, name, n, shape, dt)
            stagers.append(s_)
            return s_

        def bcast_load(st, name, src_row, n, dt=F32, waits=()):
            t = st.enter_context(nc.sbuf_tensor(name, [128, n], dt))
            tok = P.dma("sync", t[:], src_row.partition_broadcast(128), list(waits), S_ld)
            return t, tok

        def frontend(st, src_d, ntiles, mb, shb, emit_hT, tag):
            xring = Ring(st, tag + "x", 2, [128, D], F32)
            hring = Ring(st, tag + "h", 2, [128, D], F32)
            junk = st.enter_context(nc.sbuf_tensor(tag + "junk", [128, D], BF16))
            ssq = st.enter_context(nc.sbuf_tensor(tag + "ssq", [128, ntiles], F32))
            std = st.enter_context(nc.sbuf_tensor(tag + "std", [128, ntiles], F32))
            inv = st.enter_context(nc.sbuf_tensor(tag + "inv", [128, ntiles], F32))
            psfree = [None, None]
            pi = 0
            for tt in range(ntiles):
                k, xt, xs, xfree = xring.next()
                t_ld = P.dma("sync", xt[:], src_d[tt * 128:(tt + 1) * 128, :], xfree, xs)
                t_sq = P.op("scalar", (lambda e, xt=xt, tt=tt: e.activation(
                    out=junk[:], in_=xt[:], func=AF.Square, accum_out=ssq[:, tt:tt + 1])), [t_ld], S_act)
                t_sd = P.op("scalar", (lambda e, tt=tt: e.activation(
                    out=std[:, tt:tt + 1], in_=ssq[:, tt:tt + 1], func=AF.Sqrt, scale=1.0 / D, bias=eps_t[:, 0:1])),
                    [t_sq], S_act)
                t_iv = P.op("vector", (lambda e, tt=tt: e.reciprocal(out=inv[:, tt:tt + 1], in_=std[:, tt:tt + 1])),
                            [t_sd], S_dve)
                hk, ht, _, hfree = hring.next()
                t_h0 = P.op("vector", (lambda e, xt=xt, ht=ht, tt=tt: e.scalar_tensor_tensor(
                    out=ht[:], in0=xt[:], scalar=inv[:, tt:tt + 1], in1=mb[:], op0=ALU.mult, op1=ALU.mult)),
                    [t_ld, t_iv] + hfree, S_dve)
                t_h = P.op("vector", (lambda e, ht=ht: e.tensor_tensor(out=ht[:], in0=ht[:], in1=shb[:], op=ALU.add)),
                           [t_h0], S_dve)
                xring.free[k] = [t_h, t_sq]
                t_last = None
                for kc0 in range(0, KC, 4):
                    nk = min(4, KC - kc0)
                    b = pi % 2
                    pi += 1
                    for j in range(nk):
                        t_last = P.op("tensor", (lambda e, b=b, j=j, ht=ht, kc=kc0 + j: e.transpose(
                            out=psf[b][:, j * 128:(j + 1) * 128], in_=ht[:, kc * 128:(kc + 1) * 128], identity=ident_f[:])),
                            [t_h, psfree[b]], S_pe if j == nk - 1 else None)
                    psfree[b] = emit_hT(tt, kc0, nk, psf[b], t_last)
                hring.free[hk] = [t_last]

        def load_T(st, src_d, K, actT, waits, tag):
            ring = Ring(st, tag + "ld", 2, [128, K], BF16)
            half_free = [None, None]
            hi = 0
            for tt in range(NTO):
                k, t, s_, fr = ring.next()
                t_ld = P.dma("sync", t[:], src_d[tt * 128:(tt + 1) * 128, :], fr + list(waits), s_)
                t_last = None
                for kc0 in range(0, K // 128, 4):
                    nk = min(4, K // 128 - kc0)
                    b = hi % 2
                    hi += 1
                    for j in range(nk):
                        t_last = P.op("tensor", (lambda e, b=b, j=j, t=t, kc=kc0 + j: e.transpose(
                            out=psbs[b][:, j * 128:(j + 1) * 128], in_=t[:, kc * 128:(kc + 1) * 128],
                            identity=ident_b[:])), [t_ld, half_free[b]], S_pe if j == nk - 1 else None)
                    half_free[b] = P.op("scalar", (lambda e, b=b, nk=nk, kc0=kc0, tt=tt: e.activation(
                        out=actT[:, kc0:kc0 + nk, tt * 128:(tt + 1) * 128],
                        in_=psbs[b][:, 0:nk * 128].rearrange("p (k t) -> p k t", k=nk), func=AF.Copy)),
                        [t_last], S_act)
                ring.free[k] = [t_last]

        def gemm(wring, actT, kcn, w_d, row0, col0, ncols, epilogue, cgw=512):
            wv = w_d[row0:row0 + kcn * 128, :].rearrange("(kc p) n -> p kc n", p=128)
            psfree = PSF
            for c0 in range(0, ncols, cgw):
                cw = min(cgw, ncols - c0)
                k, wt, ws_, wfree = wring.next()
                t_w = P.dma("gpsimd", wt[:, :, 0:cw], wv[:, :, col0 + c0:col0 + c0 + cw], wfree, ws_)
                t_last = None
                for tt in range(NTO):
                    b = next_bank()
                    for kc in range(kcn):
                        t_last = P.op("tensor", (lambda e, b=b, kc=kc, wt=wt, cw=cw, tt=tt: e.matmul(
                            psf[b][:, 0:cw], lhsT=actT[:, kc, tt * 128:(tt + 1) * 128], rhs=wt[:, kc, 0:cw],
                            start=(kc == 0), stop=(kc == kcn - 1))),
                            [t_w, psfree.get(b)], S_pe if kc == kcn - 1 else None)
                    psfree[b] = epilogue(tt, c0, cw, psf[b], t_last)
                wring.free[k] = [t_last]

        with ExitStack() as st:
            cvt = st.enter_context(nc.sbuf_tensor("cvt", [128, KC], F32))
            s_bf = st.enter_context(nc.sbuf_tensor("s_bf", [128, KC], BF16))
            P.dma("sync", ident_f[:], identf[:, :], [], S_ld)
            t_ld0 = P.dma("sync", cvt[:], cv[:, :], [], S_ld)
            P.op("vector", lambda e: e.tensor_copy(out=ident_b[:], in_=ident_f[:]), [t_ld0], S_dve)
            P.op("vector", lambda e: e.memset(eps_t[:], EPS), [], S_dve)
            t_s = P.op("scalar", lambda e: e.activation(out=s_bf[:], in_=cvt[:], func=AF.Silu), [t_ld0], S_act)
            wring = Ring(st, "wada", 2, [128, KC, 512], BF16)
            bring = Ring(st, "bada", 2, [1, 512], F32)
            mstg = mk_stager(st, "modst", 2, [1, 512], F32)
            wv = wada.rearrange("(kc p) n -> p kc n", p=128)
            psfree = [None, None]
            for cg in range(cfg.NMOD // 512):
                k, wt, ws_, wfree = wring.next()
                t_w = P.dma("gpsimd", wt[:], wv[:, :, cg * 512:(cg + 1) * 512], wfree, ws_)
                bk, bt, bs_, bfree = bring.next()
                t_b = P.dma("sync", bt[:], bada[0:1, cg * 512:(cg + 1) * 512], bfree, bs_)
                b = cg % 2
                tk = None
                for kc in range(KC):
                    tk = P.op("tensor", (lambda e, b=b, kc=kc, wt=wt: e.matmul(
                        psf[b][0:1, :], lhsT=s_bf[:, kc:kc + 1], rhs=wt[:, kc, :],
                        start=(kc == 0), stop=(kc == KC - 1))), [t_w, t_s, psfree[b]], S_pe if kc == KC - 1 else None)
                wring.free[k] = [tk]
                mt, mfr = mstg.get()
                t_ev = P.op("vector", (lambda e, b=b, mt=mt, bt=bt: e.tensor_tensor(
                    out=mt[0:1, :], in0=psf[b][0:1, :], in1=bt[0:1, :], op=ALU.add)), [tk, t_b] + mfr, S_dve)
                psfree[b] = t_ev
                bring.free[bk] = [t_ev]
                mstg.store(mod_d[0:1, cg * 512:(cg + 1) * 512], mt[0:1, :], [t_ev])
            dout("mod", mod_d)
            if end_stage("s0"):
                return finish()

        def modrow_ap(i):
            return mod_d[0:1, i * D:(i + 1) * D]

        def mod_tiles(st, gsrc, i_sc, i_sh, tag):
            mb, _ = bcast_load(st, tag + "mb", modrow_ap(i_sc), D)
            shb, _ = bcast_load(st, tag + "shb", modrow_ap(i_sh), D)
            gb, t3 = bcast_load(st, tag + "gb", gsrc[0:1, :], D)
            t_m = P.op("vector", lambda e: e.scalar_tensor_tensor(
                out=mb[:], in0=mb[:], scalar=1.0, in1=gb[:], op0=ALU.add, op1=ALU.mult), [t3], S_dve)
            return mb, shb, t_m

        with ExitStack() as st:
            mb, shb, t_m = mod_tiles(st, gpre, 1, 0, "s1")
            stg = mk_stager(st, "s1stg", 2, [128, KC, 512], BF16)
            cur = {}

            def emit1(tt, kc0, nk, ps, ready):
                if tt % 4 == 0 and kc0 == 0:
                    cur["t"], cur["fr"] = stg.get()
                t = cur["t"]
                tok = P.op("scalar", (lambda e, t=t, tt=tt, kc0=kc0, nk=nk, ps=ps: e.activation(
                    out=t[:, kc0:kc0 + nk, (tt % 4) * 128:(tt % 4 + 1) * 128],
                    in_=ps[:, 0:nk * 128].rearrange("p (k t) -> p k t", k=nk), func=AF.Copy)),
                    [ready] + cur["fr"], S_act)
                cur["fr"] = []
                if tt % 4 == 3 and kc0 + nk == KC:
                    stg.store(hT_d[tt // 4], t[:], [tok])
                return tok

            P.op("vector", None, [t_m])
            frontend(st, xr, NTA, mb, shb, emit1, "s1")
            dout("hT", hT_d, BF16)
            if end_stage("s1"):
                return finish()

        with ExitStack() as st:
            hring = Ring(st, "s2h", 2, [128, KC, 512], BF16)
            wring = Ring(st, "s2w", 2, [128, KC, 512], BF16)
            stg = mk_stager(st, "s2stg", 3, [128, 512], BF16)
            wv = win.rearrange("(kc p) n -> p kc n", p=128)
            kcol0 = 2 * GW + GW
            vcol0 = 2 * GW + 2 * GW
            psfree = PSF
            for isv in (0, 1):
                for c0 in range(0, GW, 512):
                    cw = min(512, GW - c0)
                    wk, wt, ws_, wfree = wring.next()
                    cbase = (vcol0 if isv else kcol0) + c0
                    t_w = P.dma("gpsimd", wt[:, :, 0:cw], wv[:, :, cbase:cbase + cw], wfree, ws_)
                    t_last = None
                    for tg in range(NTA // 4):
                        hk, ht, hs_, hfree = hring.next()
                        t_h = P.dma("sync", ht[:], hT_d[tg], hfree, hs_)
                        if not isv:
                            for hh in range(cw // 128):
                                b = next_bank()
                                for kc in range(KC):
                                    t_last = P.op("tensor", (lambda e, b=b, kc=kc, wt=wt, ht=ht, hh=hh: e.matmul(
                                        psf[b][:, :], lhsT=wt[:, kc, hh * 128:(hh + 1) * 128], rhs=ht[:, kc, :],
                                        start=(kc == 0), stop=(kc == KC - 1))), [t_w, t_h, psfree.get(b)],
                                        S_pe if kc == KC - 1 else None)
                                sg_t, fr = stg.get()
                                tok = P.op("scalar", (lambda e, sg_t=sg_t, b=b: e.activation(
                                    out=sg_t[:], in_=psf[b][:, :], func=AF.Copy)), [t_last] + fr, S_act)
                                psfree[b] = tok
                                head = (c0 // 128) + hh
                                stg.store(kT_d[head][:, tg * 512:(tg + 1) * 512], sg_t[:], [tok])
                        else:
                            for t4 in range(4):
                                b = next_bank()
                                for kc in range(KC):
                                    t_last = P.op("tensor", (lambda e, b=b, kc=kc, wt=wt, ht=ht, t4=t4, cw=cw: e.matmul(
                                        psf[b][:, 0:cw], lhsT=ht[:, kc, t4 * 128:(t4 + 1) * 128], rhs=wt[:, kc, 0:cw],
                                        start=(kc == 0), stop=(kc == KC - 1))), [t_w, t_h, psfree.get(b)],
                                        S_pe if kc == KC - 1 else None)
                                sg_t, fr = stg.get()
                                tok = P.op("vector", (lambda e, sg_t=sg_t, b=b, cw=cw: e.tensor_copy(
                                    out=sg_t[:, 0:cw], in_=psf[b][:, 0:cw])), [t_last] + fr, S_dve)
                                psfree[b] = tok
                                r0 = tg * 512 + t4 * 128
                                stg.store(v_d[r0:r0 + 128, c0:c0 + cw], sg_t[:, 0:cw], [tok])
                        hring.free[hk] = [t_last]
                    wring.free[wk] = [t_last]
            dout("kT", kT_d, BF16)
            dout("v", v_d, BF16)
            if end_stage("s2"):
                return finish()

        with ExitStack() as st:
            hT_own = st.enter_context(nc.sbuf_tensor("hT_own", [128, KC, TO], BF16))
            with ExitStack() as st2:
                mb, shb, t_m = mod_tiles(st2, gpre, 1, 0, "s3")

                def emit3(tt, kc0, nk, ps, ready):
                    return P.op("scalar", (lambda e, tt=tt, kc0=kc0, nk=nk, ps=ps: e.activation(
                        out=hT_own[:, kc0:kc0 + nk, tt * 128:(tt + 1) * 128],
                        in_=ps[:, 0:nk * 128].rearrange("p (k t) -> p k t", k=nk), func=AF.Copy)), [ready], S_act)

                P.op("vector", None, [t_m])
                frontend(st2, xo, NTO, mb, shb, emit3, "s3")
                if end_stage("s3a"):
                    return finish()
            stg_b = mk_stager(st, "s3sb", 3, [128, 512], BF16)
            stg_f = mk_stager(st, "s3sf", 3, [128, 512], F32)

            def ep_act(func, dst_d, stg):
                def ep(tt, c0, cw, ps, ready):
                    t, fr = stg.get()
                    tok = P.op("scalar", (lambda e, t=t, ps=ps, cw=cw: e.activation(
                        out=t[:, 0:cw], in_=ps[:, 0:cw], func=func)), [ready] + fr, S_act)
                    stg.store(dst_d[tt * 128:(tt + 1) * 128, c0:c0 + cw], t[:, 0:cw], [tok])
                    return tok
                return ep

            wring = Ring(st, "s3w", 2, [128, KC, 512], BF16)
            s3n = int(os.environ.get("MK_S3N", "5"))
            if s3n >= 1 and not os.environ.get("MK_SKIPU"):
                gemm(wring, hT_own, KC, win, 0, 0, GW, ep_act(AF.Gelu, u_d, stg_b))
            if s3n >= 2:
                gemm(wring, hT_own, KC, win, 0, GW, GW, ep_act(AF.Gelu, vr_d, stg_f))
            if s3n >= 3:
                gemm(wring, hT_own, KC, win, 0, 2 * GW + 3 * GW, D, ep_act(AF.Sigmoid, sga_d, stg_b))
            if s3n >= 4:
                gemm(wring, hT_own, KC, win, 0, 2 * GW + 3 * GW + D, D, ep_act(AF.Sigmoid, sgb_d, stg_b))
            wv = win.rearrange("(kc p) n -> p kc n", p=128)
            psfree = PSF
            for hh in range(NH if s3n >= 5 else 0):
                wk, wt, ws_, wfree = wring.next()
                cb = 2 * GW + hh * 128
                t_w = P.dma("gpsimd", wt[:, :, 0:128], wv[:, :, cb:cb + 128], wfree, ws_)
                t_last = None
                for t0 in range(0, TO, 512):
                    tw = min(512, TO - t0)
                    b = next_bank()
                    for kc in range(KC):
                        t_last = P.op("tensor", (lambda e, b=b, kc=kc, wt=wt, t0=t0, tw=tw: e.matmul(
                            psf[b][:, 0:tw], lhsT=wt[:, kc, 0:128], rhs=hT_own[:, kc, t0:t0 + tw],
                            start=(kc == 0), stop=(kc == KC - 1))), [t_w, psfree.get(b)],
                            S_pe if kc == KC - 1 else None)
                    t, fr = stg_b.get()
                    tok = P.op("scalar", (lambda e, t=t, b=b, tw=tw: e.activation(
                        out=t[:, 0:tw], in_=psf[b][:, 0:tw], func=AF.Copy, scale=float(1.0 / np.sqrt(128.0)))),
                        [t_last] + fr, S_act)
                    psfree[b] = tok
                    stg_b.store(qT_d[hh][:, t0:t0 + tw], t[:, 0:tw], [tok])
                wring.free[wk] = [t_last]
            dout("u", u_d, BF16)
            dout("vr", vr_d, F32)
            dout("qT", qT_d, BF16)
            dout("sga", sga_d, BF16)
            if end_stage("s3"):
                return finish()

        with ExitStack() as st:
            gvb, t_g = bcast_load(st, "s4gv", gv[0:1, :], GW)
            wst = st.enter_context(nc.sbuf_tensor("s4ws", [128, NH, 128], F32))
            wsb = st.enter_context(nc.sbuf_tensor("s4wsb", [128, NH, 128], BF16))
            bst = st.enter_context(nc.sbuf_tensor("s4bs", [128, NH], F32))
            P.dma("sync", wst[:], wsT[:, :, :], [], S_ld)
            t_b = P.dma("sync", bst[:], bsr[:, :], [], S_ld)
            t_g = t_b
            t_ms = P.op("vector", lambda e: e.memset(wst[0:64, :, 64:128], 0.0), [t_b], S_dve)
            t_wb = P.op("vector", lambda e: e.tensor_copy(out=wsb[:], in_=wst[:]), [t_ms], S_dve)
            vring = Ring(st, "s4v", 2, [128, GW], F32)
            uring = Ring(st, "s4u", 2, [128, GW], BF16)
            vnr = Ring(st, "s4vn", 2, [128, GW], BF16)
            stg = mk_stager(st, "s4stg", 2, [128, GW], BF16)
            stats = st.enter_context(nc.sbuf_tensor("s4stats", [128, 8 * NTO], F32))
            junk = st.enter_context(nc.sbuf_tensor("s4junk", [128, GW], BF16))
            psfree = PSF
            for tt in range(NTO):
                vk, vt, vs_, vfree = vring.next()
                t_v = P.dma("sync", vt[:], vr_d[tt * 128:(tt + 1) * 128, :], vfree, vs_)
                uk, ut, us_, ufree = uring.next()
                t_u = P.dma("sync", ut[:], u_d[tt * 128:(tt + 1) * 128, :], ufree, us_)
                sc = stats[:, tt * 8:(tt + 1) * 8]
                t_a1 = P.op("scalar", (lambda e, vt=vt, sc=sc: e.activation(
                    out=junk[:], in_=vt[:], func=AF.Copy, scale=1.0 / GW, accum_out=sc[:, 0:1])), [t_v], S_act)
                t_a2 = P.op("scalar", (lambda e, vt=vt, sc=sc: e.activation(
                    out=junk[:], in_=vt[:], func=AF.Square, scale=float(1.0 / np.sqrt(GW)), accum_out=sc[:, 1:2])),
                    [t_a1], S_act)
                t_x = P.op("vector", (lambda e, sc=sc: e.tensor_tensor(
                    out=sc[:, 2:3], in0=sc[:, 0:1], in1=sc[:, 0:1], op=ALU.mult)), [t_a2], S_dve)
                t_var = P.op("vector", (lambda e, sc=sc: e.tensor_tensor(
                    out=sc[:, 3:4], in0=sc[:, 1:2], in1=sc[:, 2:3], op=ALU.subtract)), [t_x], S_dve)
                t_sd = P.op("scalar", (lambda e, sc=sc: e.activation(
                    out=sc[:, 4:5], in_=sc[:, 3:4], func=AF.Sqrt, bias=eps_t[:, 0:1])), [t_var], S_act)
                t_iv = P.op("vector", (lambda e, sc=sc: e.reciprocal(out=sc[:, 5:6], in_=sc[:, 4:5])), [t_sd], S_dve)
                t_n0 = P.op("vector", (lambda e, vt=vt, sc=sc: e.tensor_scalar(
                    out=vt[:], in0=vt[:], scalar1=sc[:, 0:1], scalar2=sc[:, 5:6], op0=ALU.subtract, op1=ALU.mult)),
                    [t_iv], S_dve)
                nk, vn, _, nfree = vnr.next()
                t_vn = P.op("vector", (lambda e, vt=vt, vn=vn: e.tensor_tensor(
                    out=vn[:], in0=vt[:], in1=gvb[:], op=ALU.mult)), [t_g, t_n0] + nfree, S_dve)
                vring.free[vk] = [t_vn]
                gt_, gfr = stg.get()
                t_last = None
                tok = None
                for g in range(NH):
                    b = next_bank()
                    t_last = P.op("tensor", (lambda e, b=b, g=g, vn=vn: e.matmul(
                        psf[b][:, 0:128], lhsT=wsb[:, g, :], rhs=vn[:, g * 128:(g + 1) * 128], start=True, stop=True)),
                        [t_vn, t_wb, psfree.get(b)], S_pe)
                    tok = P.op("vector", (lambda e, b=b, g=g, ut=ut, gt_=gt_: e.scalar_tensor_tensor(
                        out=gt_[:, g * 128:(g + 1) * 128], in0=psf[b][:, 0:128], scalar=bst[:, g:g + 1],
                        in1=ut[:, g * 128:(g + 1) * 128], op0=ALU.add, op1=ALU.mult)), [t_last, t_u] + gfr, S_dve)
                    gfr = []
                    psfree[b] = tok
                vnr.free[nk] = [t_last]
                uring.free[uk] = [tok]
                stg.store(gm_d[tt * 128:(tt + 1) * 128, :], gt_[:], [tok])
            dout("gm", gm_d, BF16)
            if end_stage("s4"):
                return finish()

        with ExitStack() as st:
            kpb, t_kp = bcast_load(st, "s5kp", kpos[0:1, :], S)
            qp = st.enter_context(nc.sbuf_tensor("s5qp", [128, NTO], F32))
            t_qp = P.dma("sync", qp[:], qpos[:, :], [], S_ld)
            t_c = [t_kp, t_qp]
            qT = st.enter_context(nc.sbuf_tensor("s5qT", [128, TO], BF16))
            kring = Ring(st, "s5k", 2, [128, S], BF16)
            vring = Ring(st, "s5v", 2, [128, NTA, 128], BF16)
            ering = Ring(st, "s5e", 2, [128, 512], F32)
            lring = Ring(st, "s5l", 2, [128, 512], F32)
            cring = Ring(st, "s5c", 3, [128, 512], F32)
            wring = Ring(st, "s5w", 2, [128, 512], F32)
            aring = Ring(st, "s5a", 2, [128, 512], BF16)
            atring = Ring(st, "s5at", 2, [128, 4, 128], BF16)
            zeros = st.enter_context(nc.sbuf_tensor("s5z", [128, 512], F32))
            carry0 = st.enter_context(nc.sbuf_tensor("s5carry0", [128, 1], F32))
            P.op("vector", lambda e: e.memset(zeros[:], 0.0), [], S_dve)
            t_z0 = P.op("vector", lambda e: e.memset(carry0[:], 0.0), [], S_dve)
            ostg = mk_stager(st, "s5o", 2, [128, 128], BF16)
            qs = new_ring_sem()
            q_free = []
            zfree = PSF
            ofree = {}
            oi = 0
            tfree = [None, None]
            ti = 0
            NCH = S // 512
            for hh in range(NH):
                t_q = P.dma("sync", qT[:], qT_d[hh], q_free, qs)
                kk, kt, ks_, kfree = kring.next()
                t_k = P.dma("sync", kt[:], kT_d[hh], kfree, ks_)
                vk, vt, vs_, vfree = vring.next()
                vsrc = v_d[:, hh * 128:(hh + 1) * 128].rearrange("(b s) d -> s b d", s=128)
                nsp = max(1, NTA // 16)
                for sp in range(nsp):
                    b0, b1 = sp * NTA // nsp, (sp + 1) * NTA // nsp
                    t_v = P.dma("sync", vt[:, b0:b1, :], vsrc[:, b0:b1, :], vfree, vs_)
                last_pe = None
                for qb in range(NTO):
                    ob = oi % 2
                    oi += 1
                    prev_c = None
                    t_prev_sc = t_z0
                    n_av = 0
                    for ch in range(NCH):
                        zb = next_bank()
                        t_z = P.op("tensor", (lambda e, zb=zb, qb=qb, kt=kt, ch=ch: e.matmul(
                            psf[zb][:, :], lhsT=qT[:, qb * 128:(qb + 1) * 128], rhs=kt[:, ch * 512:(ch + 1) * 512],
                            start=True, stop=True)), [t_q, t_k, zfree.get(zb)], S_pe)
                        ek, et, _, efree = ering.next()
                        t_e = P.op("scalar", (lambda e, et=et, zb=zb: e.activation(
                            out=et[:], in_=psf[zb][:, :], func=AF.Exp)), [t_z] + efree, S_act)
                        zfree[zb] = t_e
                        t_em = P.op("vector", (lambda e, et=et, ch=ch, qb=qb: e.scalar_tensor_tensor(
                            out=et[:], in0=kpb[:, ch * 512:(ch + 1) * 512], scalar=qp[:, qb:qb + 1], in1=et[:],
                            op0=ALU.is_gt, op1=ALU.mult)), [t_e] + t_c, S_dve)
                        lk, lt, _, lfree = lring.next()
                        t_l = P.op("scalar", (lambda e, et=et, lt=lt: e.activation(
                            out=lt[:], in_=et[:], func=AF.Ln, bias=1.0)), [t_em] + lfree, S_act)
                        ck, ct, _, cfree = cring.next()
                        init = carry0[:, 0:1] if prev_c is None else prev_c[:, 511:512]
                        t_sc = P.op("vector", (lambda e, lt=lt, ct=ct, init=init: e.tensor_tensor_scan(
                            out=ct[:], data0=lt[:], data1=zeros[:], initial=init, op0=ALU.add, op1=ALU.add)),
                            [t_l, t_prev_sc] + cfree, S_dve)
                        t_prev_sc = t_sc
                        lring.free[lk] = [t_sc]
                        prev_c = ct
                        wk, wt, _, wfree = wring.next()
                        t_w = P.op("scalar", (lambda e, ct=ct, wt=wt: e.activation(
                            out=wt[:], in_=ct[:], func=AF.Exp, scale=-1.0)), [t_sc] + wfree, S_act)
                        ak, at, _, afree = aring.next()
                        t_a = P.op("vector", (lambda e, et=et, wt=wt, at=at: e.tensor_tensor(
                            out=at[:], in0=et[:], in1=wt[:], op=ALU.mult)), [t_w] + afree, S_dve)
                        ering.free[ek] = [t_a]
                        wring.free[wk] = [t_a]
                        cring.free[ck] = [t_w]
                        tb = ti % 2
                        ti += 1
                        t_tr = None
                        for j in range(4):
                            t_tr = P.op("tensor", (lambda e, tb=tb, j=j, at=at: e.transpose(
                                out=psbs[tb][:, j * 128:(j + 1) * 128], in_=at[:, j * 128:(j + 1) * 128],
                                identity=ident_b[:])), [t_a, tfree[tb]], S_pe if j == 3 else None)
                        aring.free[ak] = [t_tr]
                        tk_, att, _, atfree = atring.next()
                        t_at = P.op("scalar", (lambda e, tb=tb, att=att: e.activation(
                            out=att[:], in_=psbs[tb][:, 0:512].rearrange("p (j t) -> p j t", j=4),
                            func=AF.Copy)), [t_tr] + atfree, S_act)
                        tfree[tb] = t_at
                        for j in range(4):
                            last_pe = P.op("tensor", (lambda e, ob=ob, j=j, att=att, vt=vt, ch=ch, n_av=n_av: e.matmul(
                                psf[ob][:, 0:128], lhsT=att[:, j, :], rhs=vt[:, ch * 4 + j, :],
                                start=(n_av == 0), stop=(n_av == NCH * 4 - 1))),
                                [t_at, t_v, ofree.get(ob) if n_av == 0 else None],
                                S_pe if (j == 3) else None)
                            n_av += 1
                        atring.free[tk_] = [last_pe]
                    ot, ofr = ostg.get()
                    t_o = P.op("vector", (lambda e, ot=ot, ob=ob: e.tensor_copy(out=ot[:], in_=psf[ob][:, 0:128])),
                               [last_pe] + ofr, S_dve)
                    ofree[ob] = t_o
                    ostg.store(o_d[qb * 128:(qb + 1) * 128, hh * 128:(hh + 1) * 128], ot[:], [t_o])
                q_free = [last_pe]
                kring.free[kk] = [last_pe]
                vring.free[vk] = [last_pe]
            dout("o", o_d, BF16)
            if end_stage("s5"):
                return finish()

        for which in (() if os.environ.get('MK_SKIP6') else (0, 1)):
            with ExitStack() as st:
                actT = st.enter_context(nc.sbuf_tensor(f"s6act{which}", [128, GKC, TO], BF16))
                src = gm_d if which == 0 else o_d
                with ExitStack() as st2:
                    load_T(st2, src, GW, actT, [], f"s6l{which}")
                    if end_stage(f"s6l{which}"):
                        return finish()
                sgr = Ring(st, f"s6sg{which}", 3, [128, 512], BF16)
                t1r = Ring(st, f"s6t1{which}", 3, [128, 512], F32)
                stg_f = mk_stager(st, f"s6sf{which}", 3, [128, 512], F32)
                stg_b = mk_stager(st, f"s6sb{which}", 3, [128, 512], BF16)
                sg_src = sga_d if which == 0 else sgb_d

                def ep6(tt, c0, cw, ps, ready, which=which, sgr=sgr, t1r=t1r, stg_f=stg_f, stg_b=stg_b, sg_src=sg_src):
                    ep6m = int(os.environ.get("MK_EP6", "2"))
                    if ep6m == 0:
                        t, fr = stg_f.get()
                        tok = P.op("scalar", (lambda e, t=t, ps=ps, cw=cw: e.activation(
                            out=t[:, 0:cw], in_=ps[:, 0:cw], func=AF.Copy)), [ready] + fr, S_act)
                        stg_f.store(t1_d[tt * 128:(tt + 1) * 128, c0:c0 + cw], t[:, 0:cw], [tok])
                        return tok
                    gk, gt_, gs_, gfree = sgr.next()
                    t_g = P.dma("sync", gt_[:, 0:cw], sg_src[tt * 128:(tt + 1) * 128, c0:c0 + cw], gfree, gs_)
                    if ep6m == 1:
                        t, fr = stg_f.get()
                        tok = P.op("scalar", (lambda e, t=t, ps=ps, cw=cw: e.activation(
                            out=t[:, 0:cw], in_=ps[:, 0:cw], func=AF.Copy)), [ready, t_g] + fr, S_act)
                        sgr.free[gk] = [tok]
                        stg_f.store(t1_d[tt * 128:(tt + 1) * 128, c0:c0 + cw], t[:, 0:cw], [tok])
                        return tok
                    if which == 0:
                        t, fr = stg_f.get()
                        tok = P.op("vector", (lambda e, t=t, ps=ps, gt_=gt_, cw=cw: e.tensor_tensor(
                            out=t[:, 0:cw], in0=ps[:, 0:cw], in1=gt_[:, 0:cw], op=ALU.mult)), [ready, t_g] + fr, S_dve)
                        sgr.free[gk] = [tok]
                        stg_f.store(t1_d[tt * 128:(tt + 1) * 128, c0:c0 + cw], t[:, 0:cw], [tok])
                        return tok
                    k1, t1, s1, f1 = t1r.next()
                    t_1 = P.dma("sync", t1[:, 0:cw], t1_d[tt * 128:(tt + 1) * 128, c0:c0 + cw], f1, s1)
                    t, fr = stg_b.get()
                    tok_a = P.op("vector", (lambda e, ps=ps, gt_=gt_, cw=cw: e.tensor_tensor(
                        out=gt_[:, 0:cw], in0=ps[:, 0:cw], in1=gt_[:, 0:cw], op=ALU.mult)), [ready, t_g], S_dve)
                    tok_b = P.op("vector", (lambda e, t=t, t1=t1, gt_=gt_, cw=cw: e.tensor_tensor(
                        out=t[:, 0:cw], in0=gt_[:, 0:cw], in1=t1[:, 0:cw], op=ALU.add)), [t_1, tok_a] + fr, S_dve)
                    sgr.free[gk] = [tok_b]
                    t1r.free[k1] = [tok_b]
                    stg_b.store(mix_d[tt * 128:(tt + 1) * 128, c0:c0 + cw], t[:, 0:cw], [tok_b])
                    return tok_a

                wring = Ring(st, f"s6w{which}", 2, [128, GKC, 512], BF16)
                gemm(wring, actT, GKC, wpa if which == 0 else wpb, 0, 0, D, ep6)
                if which == 1:
                    dout("mix", mix_d, BF16)
                if end_stage(f"s6{which}"):
                    return finish()

        with ExitStack() as st:
            actT = st.enter_context(nc.sbuf_tensor("s8act", [128, KC, TO], BF16))
            with ExitStack() as st2:
                load_T(st2, (sga_d if os.environ.get("MK_S8SRC") else mix_d)[:, 0:int(os.environ.get("MK_LTK", D))], int(os.environ.get("MK_LTK", D)), actT, [], "s8l")
                if end_stage("s8l"):
                    return finish()
            stg_f = mk_stager(st, "s8sf", 3, [128, 512], F32)

            def ep8(tt, c0, cw, ps, ready):
                t, fr = stg_f.get()
                tok = P.op("scalar", (lambda e, t=t, ps=ps, cw=cw: e.activation(
                    out=t[:, 0:cw], in_=ps[:, 0:cw], func=AF.Copy)), [ready] + fr, S_act)
                stg_f.store(mo_d[tt * 128:(tt + 1) * 128, c0:c0 + cw], t[:, 0:cw], [tok])
                return tok

            wring = Ring(st, "s8w", 2, [128, KC, 512], BF16)
            gemm(wring, actT, KC, wo, 0, 0, D, ep8)
            dout("mo", mo_d, F32)
            if end_stage("s8"):
                return finish()

        def post_norm_residual(st, br_d, res_d, gsrc, i_gt, dst_d, tag):
            gtb, _ = bcast_load(st, tag + "gt", modrow_ap(i_gt), D)
            gb, t3 = bcast_load(st, tag + "g", gsrc[0:1, :], D)
            t_gg = P.op("vector", lambda e: e.tensor_tensor(out=gtb[:], in0=gtb[:], in1=gb[:], op=ALU.mult), [t3], S_dve)
            bring = Ring(st, tag + "b", 2, [128, D], F32)
            rring = Ring(st, tag + "r", 2, [128, D], F32)
            junk = st.enter_context(nc.sbuf_tensor(tag + "junk", [128, D], BF16))
            stt = st.enter_context(nc.sbuf_tensor(tag + "st", [128, 4 * NTO], F32))
            stg = mk_stager(st, tag + "o", 2, [128, D], F32)
            for tt in range(NTO):
                bk, bt, bs_, bfree = bring.next()
                t_b = P.dma("sync", bt[:], br_d[tt * 128:(tt + 1) * 128, :], bfree, bs_)
                rk, rt, rs_, rfree = rring.next()
                t_r = P.dma("sync", rt[:], res_d[tt * 128:(tt + 1) * 128, :], rfree, rs_)
                sc = stt[:, tt * 4:(tt + 1) * 4]
                t_q = P.op("scalar", (lambda e, bt=bt, sc=sc: e.activation(
                    out=junk[:], in_=bt[:], func=AF.Square, accum_out=sc[:, 0:1])), [t_b], S_act)
                t_sd = P.op("scalar", (lambda e, sc=sc: e.activation(
                    out=sc[:, 1:2], in_=sc[:, 0:1], func=AF.Sqrt, scale=1.0 / D, bias=eps_t[:, 0:1])), [t_q], S_act)
                t_iv = P.op("vector", (lambda e, sc=sc: e.reciprocal(out=sc[:, 2:3], in_=sc[:, 1:2])), [t_sd], S_dve)
                t_m = P.op("vector", (lambda e, bt=bt, sc=sc: e.scalar_tensor_tensor(
                    out=bt[:], in0=bt[:], scalar=sc[:, 2:3], in1=gtb[:], op0=ALU.mult, op1=ALU.mult)),
                    [t_gg, t_iv], S_dve)
                ot, ofr = stg.get()
                tok = P.op("vector", (lambda e, bt=bt, rt=rt, ot=ot: e.tensor_tensor(
                    out=ot[:], in0=bt[:], in1=rt[:], op=ALU.add)), [t_r, t_m] + ofr, S_dve)
                bring.free[bk] = [tok]
                rring.free[rk] = [tok]
                stg.store(dst_d[tt * 128:(tt + 1) * 128, :], ot[:], [tok])

        with ExitStack() as st:
            post_norm_residual(st, mo_d, xo, gpost, 2, x1_d, "s9")
            dout("x1", x1_d, F32)
            if end_stage("s9"):
                return finish()

        with ExitStack() as st:
            h2T = st.enter_context(nc.sbuf_tensor("h2T", [128, KC, TO], BF16))
            with ExitStack() as st2:
                mb, shb, t_m = mod_tiles(st2, gpre2, 4, 3, "s10")

                def emit10(tt, kc0, nk, ps, ready):
                    return P.op("scalar", (lambda e, tt=tt, kc0=kc0, nk=nk, ps=ps: e.activation(
                        out=h2T[:, kc0:kc0 + nk, tt * 128:(tt + 1) * 128],
                        in_=ps[:, 0:nk * 128].rearrange("p (k t) -> p k t", k=nk), func=AF.Copy)), [ready], S_act)

                P.op("vector", None, [t_m])
                frontend(st2, x1_d, NTO, mb, shb, emit10, "s10")
                if end_stage("s10a"):
                    return finish()
            rr = Ring(st, "s10r", 3, [128, 512], F32)
            stg_b = mk_stager(st, "s10sb", 3, [128, 512], BF16)

            def ep10(tt, c0, cw, ps, ready):
                rk, rt, _, rfree = rr.next()
                tok = P.op("scalar", (lambda e, rt=rt, ps=ps, cw=cw: e.activation(
                    out=rt[:, 0:cw], in_=ps[:, 0:cw], func=AF.Relu)), [ready] + rfree, S_act)
                t, fr = stg_b.get()
                tok2 = P.op("vector", (lambda e, t=t, rt=rt, cw=cw: e.tensor_tensor(
                    out=t[:, 0:cw], in0=rt[:, 0:cw], in1=rt[:, 0:cw], op=ALU.mult)), [tok] + fr, S_dve)
                rr.free[rk] = [tok2]
                stg_b.store(f_d[tt * 128:(tt + 1) * 128, c0:c0 + cw], t[:, 0:cw], [tok2])
                return tok

            wring = Ring(st, "s10w", 2, [128, KC, 512], BF16)
            gemm(wring, h2T, KC, wf1, 0, 0, DFF, ep10)
            dout("f", f_d, BF16)
            if end_stage("s10"):
                return finish()

        for kb in range(DFF // D):
            with ExitStack() as st:
                actT = st.enter_context(nc.sbuf_tensor(f"s11act{kb}", [128, KC, TO], BF16))
                with ExitStack() as st2:
                    load_T(st2, f_d[:, kb * D:(kb + 1) * D], D, actT, [], f"s11l{kb}")
                    if end_stage(f"s11l{kb}"):
                        return finish()
                ar = Ring(st, f"s11a{kb}", 3, [128, 512], F32)
                stg_f = mk_stager(st, f"s11sf{kb}", 3, [128, 512], F32)

                def ep11(tt, c0, cw, ps, ready, kb=kb, ar=ar, stg_f=stg_f):
                    t, fr = stg_f.get()
                    if kb == 0:
                        tok = P.op("scalar", (lambda e, t=t, ps=ps, cw=cw: e.activation(
                            out=t[:, 0:cw], in_=ps[:, 0:cw], func=AF.Copy)), [ready] + fr, S_act)
                    else:
                        ak, at, as_, afree = ar.next()
                        t_a = P.dma("sync", at[:, 0:cw], acc_d[tt * 128:(tt + 1) * 128, c0:c0 + cw], afree, as_)
                        tok = P.op("vector", (lambda e, t=t, ps=ps, at=at, cw=cw: e.tensor_tensor(
                            out=t[:, 0:cw], in0=ps[:, 0:cw], in1=at[:, 0:cw], op=ALU.add)), [ready, t_a] + fr, S_dve)
                        ar.free[ak] = [tok]
                    stg_f.store(acc_d[tt * 128:(tt + 1) * 128, c0:c0 + cw], t[:, 0:cw], [tok])
                    return tok

                wring = Ring(st, f"s11w{kb}", 2, [128, KC, 512], BF16)
                gemm(wring, actT, KC, wf2, kb * D, 0, D, ep11)
                if end_stage(f"s11{kb}"):
                    return finish()

        with ExitStack() as st:
            post_norm_residual(st, acc_d, x1_d, gpost2, 5, y, "s12")
            end_stage("s12")
        return finish()


def prep_inputs(cfg, x, c, w_ada, b_ada, g_pre_mix, w_in, g_v, w_s, b_s, w_proj_a, w_proj_b, w_o,
                g_post_mix, g_pre_mlp, w_ff1, w_ff2, g_post_mlp):
    f = lambda a: np.ascontiguousarray(np.asarray(a, dtype=np.float32))
    xr = f(np.asarray(x)[0, ::-1, :])
    common = {
        "xr": xr,
        "cv": f(np.asarray(c)[0].reshape(cfg.KC, 128).T),
        "wada": f(np.asarray(w_ada)[0]),
        "bada": f(np.asarray(b_ada)[0][None, :]),
        "win": f(np.asarray(w_in)[0]),
        "gpre": f(np.asarray(g_pre_mix)[0][None, :]),
        "gv": f(np.asarray(g_v)[0][None, :]),
        "wsT": f(np.asarray(w_s)[0][:, ::-1, ::-1].transpose(2, 0, 1)),
        "bsr": f(np.asarray(b_s)[0][:, ::-1].T),
        "wpa": f(np.asarray(w_proj_a)[0]),
        "wpb": f(np.asarray(w_proj_b)[0]),
        "wo": f(np.asarray(w_o)[0]),
        "gpost": f(np.asarray(g_post_mix)[0][None, :]),
        "gpre2": f(np.asarray(g_pre_mlp)[0][None, :]),
        "wf1": f(np.asarray(w_ff1)[0]),
        "wf2": f(np.asarray(w_ff2)[0]),
        "gpost2": f(np.asarray(g_post_mlp)[0][None, :]),
        "identf": np.eye(128, dtype=np.float32),
        "kpos": np.arange(cfg.S, dtype=np.float32)[None, :],
    }
    in_maps = []
    for cid in range(cfg.NC):
        m = dict(common)
        m["xo"] = f(xr[cid * cfg.TO:(cid + 1) * cfg.TO])
        m["qpos"] = f((cid * cfg.TO + np.arange(cfg.TO)).reshape(cfg.NTO, 128).T)
        in_maps.append(m)
    return in_maps


_CACHE = {}


def run(cfg, inputs, stop_after=None, debug=False, trace=False):
    key = (cfg.D, cfg.S, cfg.NC, stop_after, debug)
    if key not in _CACHE:
        _CACHE[key] = build_nc(cfg, stop_after, debug)
    nc = _CACHE[key]
    in_maps = prep_inputs(cfg, **inputs)
    return run_bass_kernel_spmd(nc, in_maps, core_ids=list(range(cfg.NC)), trace=trace)


def kernel(**inputs):
    cfg = Cfg()
    res = run(cfg, inputs)
    yr = np.concatenate([np.asarray(r["y"]) for r in res.results], axis=0)
    return np.ascontiguousarray(yr[::-1])[None].astype(np.float32)
```

```python
import os
from contextlib import ExitStack
import numpy as np
import concourse.bass as bass
import concourse.mybir as mybir
from concourse.bass_utils import run_bass_kernel_spmd

F32 = mybir.dt.float32
BF16 = mybir.dt.bfloat16
AF = mybir.ActivationFunctionType
ALU = mybir.AluOpType
EPS = 1e-6
ENGS = ("sync", "scalar", "gpsimd", "vector", "tensor")


class Cfg:
    def __init__(self, D=4096, S=8192, NC=8):
        self.D, self.S, self.NC = D, S, NC
        self.KC = D // 128
        self.TO = S // NC
        self.NTO = self.TO // 128
        self.NTA = S // 128
        self.GW = D // 2
        self.NH = self.GW // 128
        self.DFF = 4 * D
        self.NMOD = 6 * D
        self.INC = 2 * self.GW + 3 * self.GW + 2 * D


class CS:
    def __init__(self, h):
        self.h = h
        self.n = 0


class Prog:
    def __init__(self, nc):
        self.nc = nc
        self.q = {e: [] for e in ENGS}
        self.waited = {e: {} for e in ENGS}
        self.nblk = 0

    def op(self, eng, fn, waits=(), sig=None, amt=1):
        tok = None
        if sig is not None:
            sig.n += amt
            tok = (sig, sig.n)
        ws = []
        seen = self.waited[eng]
        stack = list(waits)
        while stack:
            w = stack.pop()
            if w is None:
                continue
            if isinstance(w, list):
                stack.extend(w)
                continue
            s, v = w
            if seen.get(id(s), 0) >= v:
                continue
            seen[id(s)] = v
            ws.append((s, v))
        self.q[eng].append((fn, ws, sig, amt))
        return tok

    def dma(self, eng, out, in_, waits=(), sig=None, **kw):
        return self.op(eng, lambda e: e.dma_start(out=out, in_=in_, **kw), waits, sig, 16)

    def flush(self, name=None):
        self.nblk += 1
        name = name or f"blk{self.nblk}"
        with self.nc.Block(name) as block:
            for eng in ENGS:
                items = self.q[eng]
                if not items:
                    continue

                def body(e, items=items):
                    for fn, ws, sig, amt in items:
                        for (s, v) in ws:
                            e.wait_ge(s.h, v)
                        if fn is None:
                            continue
                        ins = fn(e)
                        if sig is not None:
                            ins.then_inc(sig.h, amt)

                getattr(block, eng)(body)
        self.q = {e: [] for e in ENGS}


def build_nc(cfg, stop_after=None, debug=False):
    D, S, KC, TO, NTO, NTA, GW, NH, DFF = cfg.D, cfg.S, cfg.KC, cfg.TO, cfg.NTO, cfg.NTA, cfg.GW, cfg.NH, cfg.DFF
    GKC = GW // 128
    nc = bass.Bass("TRN2", target_bir_lowering=False)
    P = Prog(nc)

    def din(name, shape, dt=F32):
        return nc.dram_tensor(name, list(shape), dt, kind="ExternalInput").ap()

    def dscr(name, shape, dt):
        return nc.dram_tensor(name, list(shape), dt).ap()

    xr = din("xr", [S, D])
    xo = din("xo", [TO, D])
    cv = din("cv", [128, KC])
    wada = din("wada", [D, cfg.NMOD])
    bada = din("bada", [1, cfg.NMOD])
    win = din("win", [D, cfg.INC])
    gpre = din("gpre", [1, D])
    gv = din("gv", [1, GW])
    wsT = din("wsT", [128, NH, 128])
    bsr = din("bsr", [128, NH])
    wpa = din("wpa", [GW, D])
    wpb = din("wpb", [GW, D])
    wo = din("wo", [D, D])
    gpost = din("gpost", [1, D])
    gpre2 = din("gpre2", [1, D])
    wf1 = din("wf1", [D, DFF])
    wf2 = din("wf2", [DFF, D])
    gpost2 = din("gpost2", [1, D])
    identf = din("identf", [128, 128])
    kpos = din("kpos", [1, S])
    qpos = din("qpos", [128, NTO])
    y = nc.dram_tensor("y", [TO, D], F32, kind="ExternalOutput").ap()

    dbg = {}

    def dout(name, src, dt=F32):
        if not debug:
            return
        a = nc.dram_tensor("dbg_" + name, list(src.shape), dt, kind="ExternalOutput").ap()
        dbg[name] = (a, src)

    mod_d = dscr("mod_d", [1, cfg.NMOD], F32)
    hT_d = dscr("hT_d", [NTA // 4, 128, KC, 512], BF16)
    kT_d = dscr("kT_d", [NH, 128, S], BF16)
    v_d = dscr("v_d", [S, GW], BF16)
    qT_d = dscr("qT_d", [NH, 128, TO], BF16)
    u_d = dscr("u_d", [TO, GW], BF16)
    vr_d = dscr("vr_d", [TO, GW], F32)
    sga_d = dscr("sga_d", [TO, D], BF16)
    sgb_d = dscr("sgb_d", [TO, D], BF16)
    gm_d = dscr("gm_d", [TO, GW], BF16)
    o_d = dscr("o_d", [TO, GW], BF16)
    t1_d = dscr("t1_d", [TO, D], F32)
    mix_d = dscr("mix_d", [TO, D], BF16)
    mo_d = dscr("mo_d", [TO, D], F32)
    x1_d = dscr("x1_d", [TO, D], F32)
    f_d = dscr("f_d", [TO, DFF], BF16)
    acc_d = dscr("acc_d", [TO, D], F32)

    with ExitStack() as es:
        def sem(name):
            return CS(es.enter_context(nc.semaphore(name)))

        S_pe, S_act, S_dve = sem("s_pe"), sem("s_act"), sem("s_dve")
        S_ld = sem("s_ld")
        S_st = sem("s_st")
        ring_sems = [sem(f"rs{i}") for i in range(24)]
        psf = [es.enter_context(nc.psum_tensor(f"psf{i}", [128, 512], F32)) for i in range(6)]
        psbs = [es.enter_context(nc.psum_tensor(f"psb{i}", [128, 1024], BF16)) for i in range(2)]
        ident_f = es.enter_context(nc.sbuf_tensor("ident_f", [128, 128], F32))
        ident_b = es.enter_context(nc.sbuf_tensor("ident_b", [128, 128], BF16))
        eps_t = es.enter_context(nc.sbuf_tensor("eps_t", [128, 1], F32))

        state = {"rs": 0}
        PSF = {}
        PI = [0]

        def next_bank():
            b = 2 + PI[0] % 4
            PI[0] += 1
            return b

        def new_ring_sem():
            assert state["rs"] < len(ring_sems), "out of ring semaphores in this stage"
            s_ = ring_sems[state["rs"]]
            state["rs"] += 1
            return s_

        stagers = []

        def wait_stagers():
            toks = []
            for s_ in stagers:
                for fr in s_.ring.free:
                    toks.extend(fr)
            P.op("sync", None, toks)
            stagers.clear()

        def end_stage(name):
            wait_stagers()
            P.op("sync", None, [(S_st, S_st.n), (S_ld, S_ld.n)])
            P.flush(name)
            state["rs"] = 0
            if os.environ.get("MK_VERBOSE"):
                print("stage", name, "sem counts pe/act/dve/ld/st", S_pe.n, S_act.n, S_dve.n, S_ld.n, S_st.n,
                      "ring max", max(r.n for r in ring_sems), flush=True)
            return stop_after == name

        def finish():
            for name, (a, src) in dbg.items():
                P.dma("sync", a, src, [], S_st)
            P.op("sync", None, [(S_st, S_st.n)])
            P.flush("fin")
            return nc

        class Ring:
            def __init__(self, st, name, n, shape, dt, sems=True):
                self.bufs = [st.enter_context(nc.sbuf_tensor(f"{name}{i}", list(shape), dt)) for i in range(n)]
                self.sems = [new_ring_sem() if sems else None for _ in range(n)]
                self.free = [[] for _ in range(n)]
                self.i = 0

            def next(self):
                k = self.i % len(self.bufs)
                self.i += 1
                fr = self.free[k]
                self.free[k] = []
                return k, self.bufs[k], self.sems[k], fr

        class Stager:
            def __init__(self, st, name, n, shape, dt):
                self.ring = Ring(st, name, n, shape, dt)

            def get(self):
                k, t, s_, fr = self.ring.next()
                self.k, self.s = k, s_
                return t, fr

            def store(self, dst, src, waits):
                tok = P.dma("sync", dst, src, waits, self.s)
                self.ring.free[self.k] = [tok]
                return tok

        def mk_stager(st, name, n, shape, dt):
            s_ = Stager(st, name, n, shape, dt)
            stagers.append(s_)
            return s_

        def bcast_load(st, name, src_row, n, dt=F32, waits=()):
            t = st.enter_context(nc.sbuf_tensor(name, [128, n], dt))
            tok = P.dma("sync", t[:], src_row.partition_broadcast(128), list(waits), S_ld)
            return t, tok

        def frontend(st, src_d, ntiles, mb, shb, emit_hT, tag):
            xring = Ring(st, tag + "x", 2, [128, D], F32)
            hring = Ring(st, tag + "h", 2, [128, D], F32)
            junk = st.enter_context(nc.sbuf_tensor(tag + "junk", [128, D], BF16))
            ssq = st.enter_context(nc.sbuf_tensor(tag + "ssq", [128, ntiles], F32))
            std = st.enter_context(nc.sbuf_tensor(tag + "std", [128, ntiles], F32))
            inv = st.enter_context(nc.sbuf_tensor(tag + "inv", [128, ntiles], F32))
            psfree = [None, None]
            pi = 0
            for tt in range(ntiles):
                k, xt, xs, xfree = xring.next()
                t_ld = P.dma("sync", xt[:], src_d[tt * 128:(tt + 1) * 128, :], xfree, xs)
                t_sq = P.op("scalar", (lambda e, xt=xt, tt=tt: e.activation(
                    out=junk[:], in_=xt[:], func=AF.Square, accum_out=ssq[:, tt:tt + 1])), [t_ld], S_act)
                t_sd = P.op("scalar", (lambda e, tt=tt: e.activation(
                    out=std[:, tt:tt + 1], in_=ssq[:, tt:tt + 1], func=AF.Sqrt, scale=1.0 / D, bias=eps_t[:, 0:1])),
                    [t_sq], S_act)
                t_iv = P.op("vector", (lambda e, tt=tt: e.reciprocal(out=inv[:, tt:tt + 1], in_=std[:, tt:tt + 1])),
                            [t_sd], S_dve)
                hk, ht, _, hfree = hring.next()
                t_h0 = P.op("vector", (lambda e, xt=xt, ht=ht, tt=tt: e.scalar_tensor_tensor(
                    out=ht[:], in0=xt[:], scalar=inv[:, tt:tt + 1], in1=mb[:], op0=ALU.mult, op1=ALU.mult)),
                    [t_ld, t_iv] + hfree, S_dve)
                t_h = P.op("vector", (lambda e, ht=ht: e.tensor_tensor(out=ht[:], in0=ht[:], in1=shb[:], op=ALU.add)),
                           [t_h0], S_dve)
                xring.free[k] = [t_h, t_sq]
                t_last = None
                for kc0 in range(0, KC, 4):
                    nk = min(4, KC - kc0)
                    b = pi % 2
                    pi += 1
                    for j in range(nk):
                        t_last = P.op("tensor", (lambda e, b=b, j=j, ht=ht, kc=kc0 + j: e.transpose(
                            out=psf[b][:, j * 128:(j + 1) * 128], in_=ht[:, kc * 128:(kc + 1) * 128], identity=ident_f[:])),
                            [t_h, psfree[b]], S_pe if j == nk - 1 else None)
                    psfree[b] = emit_hT(tt, kc0, nk, psf[b], t_last)
                hring.free[hk] = [t_last]

        def load_T(st, src_d, K, actT, waits, tag):
            ring = Ring(st, tag + "ld", 2, [128, K], BF16)
            half_free = [None, None]
            hi = 0
            for tt in range(NTO):
                k, t, s_, fr = ring.next()
                t_ld = P.dma("sync", t[:], src_d[tt * 128:(tt + 1) * 128, :], fr + list(waits), s_)
                t_last = None
                for kc0 in range(0, K // 128, 4):
                    nk = min(4, K // 128 - kc0)
                    b = hi % 2
                    hi += 1
                    for j in range(nk):
                        t_last = P.op("tensor", (lambda e, b=b, j=j, t=t, kc=kc0 + j: e.transpose(
                            out=psbs[b][:, j * 128:(j + 1) * 128], in_=t[:, kc * 128:(kc + 1) * 128],
                            identity=ident_b[:])), [t_ld, half_free[b]], S_pe if j == nk - 1 else None)
                    half_free[b] = P.op("scalar", (lambda e, b=b, nk=nk, kc0=kc0, tt=tt: e.activation(
                        out=actT[:, kc0:kc0 + nk, tt * 128:(tt + 1) * 128],
                        in_=psbs[b][:, 0:nk * 128].rearrange("p (k t) -> p k t", k=nk), func=AF.Copy)),
                        [t_last], S_act)
                ring.free[k] = [t_last]

        def gemm(wring, actT, kcn, w_d, row0, col0, ncols, epilogue, cgw=512):
            wv = w_d[row0:row0 + kcn * 128, :].rearrange("(kc p) n -> p kc n", p=128)
            psfree = PSF
            for c0 in range(0, ncols, cgw):
                cw = min(cgw, ncols - c0)
                k, wt, ws_, wfree = wring.next()
                t_w = P.dma("gpsimd", wt[:, :, 0:cw], wv[:, :, col0 + c0:col0 + c0 + cw], wfree, ws_)
                t_last = None
                for tt in range(NTO):
                    b = next_bank()
                    for kc in range(kcn):
                        t_last = P.op("tensor", (lambda e, b=b, kc=kc, wt=wt, cw=cw, tt=tt: e.matmul(
                            psf[b][:, 0:cw], lhsT=actT[:, kc, tt * 128:(tt + 1) * 128], rhs=wt[:, kc, 0:cw],
                            start=(kc == 0), stop=(kc == kcn - 1))),
                            [t_w, psfree.get(b)], S_pe if kc == kcn - 1 else None)
                    psfree[b] = epilogue(tt, c0, cw, psf[b], t_last)
                wring.free[k] = [t_last]

        with ExitStack() as st:
            cvt = st.enter_context(nc.sbuf_tensor("cvt", [128, KC], F32))
            s_bf = st.enter_context(nc.sbuf_tensor("s_bf", [128, KC], BF16))
            P.dma("sync", ident_f[:], identf[:, :], [], S_ld)
            t_ld0 = P.dma("sync", cvt[:], cv[:, :], [], S_ld)
            P.op("vector", lambda e: e.tensor_copy(out=ident_b[:], in_=ident_f[:]), [t_ld0], S_dve)
            P.op("vector", lambda e: e.memset(eps_t[:], EPS), [], S_dve)
            t_s = P.op("scalar", lambda e: e.activation(out=s_bf[:], in_=cvt[:], func=AF.Silu), [t_ld0], S_act)
            wring = Ring(st, "wada", 2, [128, KC, 512], BF16)
            bring = Ring(st, "bada", 2, [1, 512], F32)
            mstg = mk_stager(st, "modst", 2, [1, 512], F32)
            wv = wada.rearrange("(kc p) n -> p kc n", p=128)
            psfree = [None, None]
            for cg in range(cfg.NMOD // 512):
                k, wt, ws_, wfree = wring.next()
                t_w = P.dma("gpsimd", wt[:], wv[:, :, cg * 512:(cg + 1) * 512], wfree, ws_)
                bk, bt, bs_, bfree = bring.next()
                t_b = P.dma("sync", bt[:], bada[0:1, cg * 512:(cg + 1) * 512], bfree, bs_)
                b = cg % 2
                tk = None
                for kc in range(KC):
                    tk = P.op("tensor", (lambda e, b=b, kc=kc, wt=wt: e.matmul(
                        psf[b][0:1, :], lhsT=s_bf[:, kc:kc + 1], rhs=wt[:, kc, :],
                        start=(kc == 0), stop=(kc == KC - 1))), [t_w, t_s, psfree[b]], S_pe if kc == KC - 1 else None)
                wring.free[k] = [tk]
                mt, mfr = mstg.get()
                t_ev = P.op("vector", (lambda e, b=b, mt=mt, bt=bt: e.tensor_tensor(
                    out=mt[0:1, :], in0=psf[b][0:1, :], in1=bt[0:1, :], op=ALU.add)), [tk, t_b] + mfr, S_dve)
                psfree[b] = t_ev
                bring.free[bk] = [t_ev]
                mstg.store(mod_d[0:1, cg * 512:(cg + 1) * 512], mt[0:1, :], [t_ev])
            dout("mod", mod_d)
            if end_stage("s0"):
                return finish()

        def modrow_ap(i):
            return mod_d[0:1, i * D:(i + 1) * D]

        def mod_tiles(st, gsrc, i_sc, i_sh, tag):
            mb, _ = bcast_load(st, tag + "mb", modrow_ap(i_sc), D)
            shb, _ = bcast_load(st, tag + "shb", modrow_ap(i_sh), D)
            gb, t3 = bcast_load(st, tag + "gb", gsrc[0:1, :], D)
            t_m = P.op("vector", lambda e: e.scalar_tensor_tensor(
                out=mb[:], in0=mb[:], scalar=1.0, in1=gb[:], op0=ALU.add, op1=ALU.mult), [t3], S_dve)
            return mb, shb, t_m

        with ExitStack() as st:
            mb, shb, t_m = mod_tiles(st, gpre, 1, 0, "s1")
            stg = mk_stager(st, "s1stg", 2, [128, KC, 512], BF16)
            cur = {}

            def emit1(tt, kc0, nk, ps, ready):
                if tt % 4 == 0 and kc0 == 0:
                    cur["t"], cur["fr"] = stg.get()
                t = cur["t"]
                tok = P.op("scalar", (lambda e, t=t, tt=tt, kc0=kc0, nk=nk, ps=ps: e.activation(
                    out=t[:, kc0:kc0 + nk, (tt % 4) * 128:(tt % 4 + 1) * 128],
                    in_=ps[:, 0:nk * 128].rearrange("p (k t) -> p k t", k=nk), func=AF.Copy)),
                    [ready] + cur["fr"], S_act)
                cur["fr"] = []
                if tt % 4 == 3 and kc0 + nk == KC:
                    stg.store(hT_d[tt // 4], t[:], [tok])
                return tok

            P.op("vector", None, [t_m])
            frontend(st, xr, NTA, mb, shb, emit1, "s1")
            dout("hT", hT_d, BF16)
            if end_stage("s1"):
                return finish()

        with ExitStack() as st:
            hring = Ring(st, "s2h", 3, [128, KC, 512], BF16)
            wring = Ring(st, "s2w", 2, [128, KC, 512], BF16)
            stg = mk_stager(st, "s2stg", 3, [128, 512], BF16)
            wv = win.rearrange("(kc p) n -> p kc n", p=128)
            kcol0 = 2 * GW + GW
            vcol0 = 2 * GW + 2 * GW
            psfree = PSF
            for isv in (0, 1):
                for c0 in range(0, GW, 512):
                    cw = min(512, GW - c0)
                    wk, wt, ws_, wfree = wring.next()
                    cbase = (vcol0 if isv else kcol0) + c0
                    t_w = P.dma("gpsimd", wt[:, :, 0:cw], wv[:, :, cbase:cbase + cw], wfree, ws_)
                    t_last = None
                    for tg in range(NTA // 4):
                        hk, ht, hs_, hfree = hring.next()
                        t_h = P.dma("sync", ht[:], hT_d[tg], hfree, hs_)
                        if not isv:
                            for hh in range(cw // 128):
                                b = next_bank()
                                for kc in range(KC):
                                    t_last = P.op("tensor", (lambda e, b=b, kc=kc, wt=wt, ht=ht, hh=hh: e.matmul(
                                        psf[b][:, :], lhsT=wt[:, kc, hh * 128:(hh + 1) * 128], rhs=ht[:, kc, :],
                                        start=(kc == 0), stop=(kc == KC - 1))), [t_w, t_h, psfree.get(b)],
                                        S_pe if kc == KC - 1 else None)
                                sg_t, fr = stg.get()
                                tok = P.op("scalar", (lambda e, sg_t=sg_t, b=b: e.activation(
                                    out=sg_t[:], in_=psf[b][:, :], func=AF.Copy)), [t_last] + fr, S_act)
                                psfree[b] = tok
                                head = (c0 // 128) + hh
                                stg.store(kT_d[head][:, tg * 512:(tg + 1) * 512], sg_t[:], [tok])
                        else:
                            for t4 in range(4):
                                b = next_bank()
                                for kc in range(KC):
                                    t_last = P.op("tensor", (lambda e, b=b, kc=kc, wt=wt, ht=ht, t4=t4, cw=cw: e.matmul(
                                        psf[b][:, 0:cw], lhsT=ht[:, kc, t4 * 128:(t4 + 1) * 128], rhs=wt[:, kc, 0:cw],
                                        start=(kc == 0), stop=(kc == KC - 1))), [t_w, t_h, psfree.get(b)],
                                        S_pe if kc == KC - 1 else None)
                                sg_t, fr = stg.get()
                                tok = P.op("vector", (lambda e, sg_t=sg_t, b=b, cw=cw: e.tensor_copy(
                                    out=sg_t[:, 0:cw], in_=psf[b][:, 0:cw])), [t_last] + fr, S_dve)
                                psfree[b] = tok
                                r0 = tg * 512 + t4 * 128
                                stg.store(v_d[r0:r0 + 128, c0:c0 + cw], sg_t[:, 0:cw], [tok])
                        hring.free[hk] = [t_last]
                    wring.free[wk] = [t_last]
            dout("kT", kT_d, BF16)
            dout("v", v_d, BF16)
            if end_stage("s2"):
                return finish()

        with ExitStack() as st:
            hT_own = st.enter_context(nc.sbuf_tensor("hT_own", [128, KC, TO], BF16))
            with ExitStack() as st2:
                mb, shb, t_m = mod_tiles(st2, gpre, 1, 0, "s3")

                def emit3(tt, kc0, nk, ps, ready):
                    return P.op("scalar", (lambda e, tt=tt, kc0=kc0, nk=nk, ps=ps: e.activation(
                        out=hT_own[:, kc0:kc0 + nk, tt * 128:(tt + 1) * 128],
                        in_=ps[:, 0:nk * 128].rearrange("p (k t) -> p k t", k=nk), func=AF.Copy)), [ready], S_act)

                P.op("vector", None, [t_m])
                frontend(st2, xo, NTO, mb, shb, emit3, "s3")
                if end_stage("s3a"):
                    return finish()
            stg_b = mk_stager(st, "s3sb", 3, [128, 512], BF16)
            stg_f = mk_stager(st, "s3sf", 3, [128, 512], F32)

            def ep_act(func, dst_d, stg):
                def ep(tt, c0, cw, ps, ready):
                    t, fr = stg.get()
                    tok = P.op("scalar", (lambda e, t=t, ps=ps, cw=cw: e.activation(
                        out=t[:, 0:cw], in_=ps[:, 0:cw], func=func)), [ready] + fr, S_act)
                    stg.store(dst_d[tt * 128:(tt + 1) * 128, c0:c0 + cw], t[:, 0:cw], [tok])
                    return tok
                return ep

            wring = Ring(st, "s3w", 2, [128, KC, 512], BF16)
            s3n = int(os.environ.get("MK_S3N", "5"))
            if s3n >= 1 and not os.environ.get("MK_SKIPU"):
                gemm(wring, hT_own, KC, win, 0, 0, GW, ep_act(AF.Gelu, u_d, stg_b))
            if s3n >= 2:
                gemm(wring, hT_own, KC, win, 0, GW, GW, ep_act(AF.Gelu, vr_d, stg_f))
            if s3n >= 3:
                gemm(wring, hT_own, KC, win, 0, 2 * GW + 3 * GW, D, ep_act(AF.Sigmoid, sga_d, stg_b))
            if s3n >= 4:
                gemm(wring, hT_own, KC, win, 0, 2 * GW + 3 * GW + D, D, ep_act(AF.Sigmoid, sgb_d, stg_b))
            wv = win.rearrange("(kc p) n -> p kc n", p=128)
            psfree = PSF
            for hh in range(NH if s3n >= 5 else 0):
                wk, wt, ws_, wfree = wring.next()
                cb = 2 * GW + hh * 128
                t_w = P.dma("gpsimd", wt[:, :, 0:128], wv[:, :, cb:cb + 128], wfree, ws_)
                t_last = None
                for t0 in range(0, TO, 512):
                    tw = min(512, TO - t0)
                    b = next_bank()
                    for kc in range(KC):
                        t_last = P.op("tensor", (lambda e, b=b, kc=kc, wt=wt, t0=t0, tw=tw: e.matmul(
                            psf[b][:, 0:tw], lhsT=wt[:, kc, 0:128], rhs=hT_own[:, kc, t0:t0 + tw],
                            start=(kc == 0), stop=(kc == KC - 1))), [t_w, psfree.get(b)],
                            S_pe if kc == KC - 1 else None)
                    t, fr = stg_b.get()
                    tok = P.op("scalar", (lambda e, t=t, b=b, tw=tw: e.activation(
                        out=t[:, 0:tw], in_=psf[b][:, 0:tw], func=AF.Copy, scale=float(1.0 / np.sqrt(128.0)))),
                        [t_last] + fr, S_act)
                    psfree[b] = tok
                    stg_b.store(qT_d[hh][:, t0:t0 + tw], t[:, 0:tw], [tok])
                wring.free[wk] = [t_last]
            dout("u", u_d, BF16)
            dout("vr", vr_d, F32)
            dout("qT", qT_d, BF16)
            dout("sga", sga_d, BF16)
            if end_stage("s3"):
                return finish()

        with ExitStack() as st:
            gvb, t_g = bcast_load(st, "s4gv", gv[0:1, :], GW)
            wst = st.enter_context(nc.sbuf_tensor("s4ws", [128, NH, 128], F32))
            wsb = st.enter_context(nc.sbuf_tensor("s4wsb", [128, NH, 128], BF16))
            bst = st.enter_context(nc.sbuf_tensor("s4bs", [128, NH], F32))
            P.dma("sync", wst[:], wsT[:, :, :], [], S_ld)
            t_b = P.dma("sync", bst[:], bsr[:, :], [], S_ld)
            t_g = t_b
            t_ms = P.op("vector", lambda e: e.memset(wst[0:64, :, 64:128], 0.0), [t_b], S_dve)
            t_wb = P.op("vector", lambda e: e.tensor_copy(out=wsb[:], in_=wst[:]), [t_ms], S_dve)
            vring = Ring(st, "s4v", 2, [128, GW], F32)
            uring = Ring(st, "s4u", 2, [128, GW], BF16)
            vnr = Ring(st, "s4vn", 2, [128, GW], BF16)
            stg = mk_stager(st, "s4stg", 2, [128, GW], BF16)
            stats = st.enter_context(nc.sbuf_tensor("s4stats", [128, 8 * NTO], F32))
            junk = st.enter_context(nc.sbuf_tensor("s4junk", [128, GW], BF16))
            psfree = PSF
            for tt in range(NTO):
                vk, vt, vs_, vfree = vring.next()
                t_v = P.dma("sync", vt[:], vr_d[tt * 128:(tt + 1) * 128, :], vfree, vs_)
                uk, ut, us_, ufree = uring.next()
                t_u = P.dma("sync", ut[:], u_d[tt * 128:(tt + 1) * 128, :], ufree, us_)
                sc = stats[:, tt * 8:(tt + 1) * 8]
                t_a1 = P.op("scalar", (lambda e, vt=vt, sc=sc: e.activation(
                    out=junk[:], in_=vt[:], func=AF.Copy, scale=1.0 / GW, accum_out=sc[:, 0:1])), [t_v], S_act)
                t_a2 = P.op("scalar", (lambda e, vt=vt, sc=sc: e.activation(
                    out=junk[:], in_=vt[:], func=AF.Square, scale=float(1.0 / np.sqrt(GW)), accum_out=sc[:, 1:2])),
                    [t_a1], S_act)
                t_x = P.op("vector", (lambda e, sc=sc: e.tensor_tensor(
                    out=sc[:, 2:3], in0=sc[:, 0:1], in1=sc[:, 0:1], op=ALU.mult)), [t_a2], S_dve)
                t_var = P.op("vector", (lambda e, sc=sc: e.tensor_tensor(
                    out=sc[:, 3:4], in0=sc[:, 1:2], in1=sc[:, 2:3], op=ALU.subtract)), [t_x], S_dve)
                t_sd = P.op("scalar", (lambda e, sc=sc: e.activation(
                    out=sc[:, 4:5], in_=sc[:, 3:4], func=AF.Sqrt, bias=eps_t[:, 0:1])), [t_var], S_act)
                t_iv = P.op("vector", (lambda e, sc=sc: e.reciprocal(out=sc[:, 5:6], in_=sc[:, 4:5])), [t_sd], S_dve)
                t_n0 = P.op("vector", (lambda e, vt=vt, sc=sc: e.tensor_scalar(
                    out=vt[:], in0=vt[:], scalar1=sc[:, 0:1], scalar2=sc[:, 5:6], op0=ALU.subtract, op1=ALU.mult)),
                    [t_iv], S_dve)
                nk, vn, _, nfree = vnr.next()
                t_vn = P.op("vector", (lambda e, vt=vt, vn=vn: e.tensor_tensor(
                    out=vn[:], in0=vt[:], in1=gvb[:], op=ALU.mult)), [t_g, t_n0] + nfree, S_dve)
                vring.free[vk] = [t_vn]
                gt_, gfr = stg.get()
                t_last = None
                tok = None
                for g in range(NH):
                    b = next_bank()
                    t_last = P.op("tensor", (lambda e, b=b, g=g, vn=vn: e.matmul(
                        psf[b][:, 0:128], lhsT=wsb[:, g, :], rhs=vn[:, g * 128:(g + 1) * 128], start=True, stop=True)),
                        [t_vn, t_wb, psfree.get(b)], S_pe)
                    tok = P.op("vector", (lambda e, b=b, g=g, ut=ut, gt_=gt_: e.scalar_tensor_tensor(
                        out=gt_[:, g * 128:(g + 1) * 128], in0=psf[b][:, 0:128], scalar=bst[:, g:g + 1],
                        in1=ut[:, g * 128:(g + 1) * 128], op0=ALU.add, op1=ALU.mult)), [t_last, t_u] + gfr, S_dve)
                    gfr = []
                    psfree[b] = tok
                vnr.free[nk] = [t_last]
                uring.free[uk] = [tok]
                stg.store(gm_d[tt * 128:(tt + 1) * 128, :], gt_[:], [tok])
            dout("gm", gm_d, BF16)
            if end_stage("s4"):
                return finish()

        with ExitStack() as st:
            G = 3
            kpb, t_kp = bcast_load(st, "s5kp", kpos[0:1, :], S)
            qp = st.enter_context(nc.sbuf_tensor("s5qp", [128, NTO], F32))
            t_qp = P.dma("sync", qp[:], qpos[:, :], [], S_ld)
            t_c = [t_kp, t_qp]
            qring = Ring(st, "s5q", 2, [128, TO], BF16)
            kring = Ring(st, "s5k", 2, [128, S], BF16)
            vring = Ring(st, "s5v", 2, [128, NTA, 128], BF16)
            RD = 2 * G
            ering = Ring(st, "s5e", RD, [128, 512], F32, sems=False)
            lring = Ring(st, "s5l", RD, [128, 512], F32, sems=False)
            cring = Ring(st, "s5c", RD, [128, 512], F32, sems=False)
            wring = Ring(st, "s5w", RD, [128, 512], F32, sems=False)
            aring = Ring(st, "s5a", RD, [128, 512], BF16, sems=False)
            atring = Ring(st, "s5at", RD, [128, 4, 128], BF16, sems=False)
            zeros = st.enter_context(nc.sbuf_tensor("s5z", [128, 512], F32))
            carry0 = st.enter_context(nc.sbuf_tensor("s5carry0", [128, 1], F32))
            P.op("vector", lambda e: e.memset(zeros[:], 0.0), [], S_dve)
            t_z0 = P.op("vector", lambda e: e.memset(carry0[:], 0.0), [], S_dve)
            ostg = mk_stager(st, "s5o", 4, [128, 128], BF16)
            zbanks = [2, 3, 4]
            obanks = [0, 1, 5]
            ofree = {}
            tfree = [None, None]
            tcnt = [0]
            NCH = S // 512
            items = [(hh, qb) for hh in range(NH) for qb in range(NTO)]
            head = {}

            def load_head(hh):
                qk, qt_, qs_, qfree = qring.next()
                t_q = P.dma("sync", qt_[:], qT_d[hh], qfree, qs_)
                kk, kt, ks_, kfree = kring.next()
                t_k = P.dma("sync", kt[:], kT_d[hh], kfree, ks_)
                vk, vt, vs_, vfree = vring.next()
                vsrc = v_d[:, hh * 128:(hh + 1) * 128].rearrange("(b s) d -> s b d", s=128)
                nsp = max(1, NTA // 16)
                t_v = None
                for sp in range(nsp):
                    b0, b1 = sp * NTA // nsp, (sp + 1) * NTA // nsp
                    t_v = P.dma("sync", vt[:, b0:b1, :], vsrc[:, b0:b1, :], vfree, vs_)
                head[hh] = dict(qt=qt_, kt=kt, vt=vt, t_q=t_q, t_k=t_k, t_v=t_v, qk=qk, kk=kk, vk=vk)

            def front(c, ch):
                hd = head[c["hh"]]
                zb = c["zb"]
                qb = c["qb"]
                t_z = P.op("tensor", (lambda e, zb=zb, qb=qb, hd=hd, ch=ch: e.matmul(
                    psf[zb][:, :], lhsT=hd["qt"][:, qb * 128:(qb + 1) * 128], rhs=hd["kt"][:, ch * 512:(ch + 1) * 512],
                    start=True, stop=True)), [hd["t_q"], hd["t_k"], PSF.get(zb)], S_pe)
                c["t_z"] = t_z

            def front2(c, ch):
                zb = c["zb"]
                ek, et, _, efree = ering.next()
                t_e = P.op("scalar", (lambda e, et=et, zb=zb: e.activation(
                    out=et[:], in_=psf[zb][:, :], func=AF.Exp)), [c["t_z"]] + efree, S_act)
                PSF[zb] = t_e
                c["e"] = (ek, et)
                c["t_e"] = t_e

            def front3(c, ch):
                ek, et = c["e"]
                qb = c["qb"]
                c["t_em"] = P.op("vector", (lambda e, et=et, ch=ch, qb=qb: e.scalar_tensor_tensor(
                    out=et[:], in0=kpb[:, ch * 512:(ch + 1) * 512], scalar=qp[:, qb:qb + 1], in1=et[:],
                    op0=ALU.is_gt, op1=ALU.mult)), [c["t_e"]] + t_c, S_dve)

            def front4(c, ch):
                ek, et = c["e"]
                lk, lt, _, lfree = lring.next()
                c["t_l"] = P.op("scalar", (lambda e, et=et, lt=lt: e.activation(
                    out=lt[:], in_=et[:], func=AF.Ln, bias=1.0)), [c["t_em"]] + lfree, S_act)
                c["l"] = (lk, lt)
                c["pend"] = dict(e=c["e"], l=c["l"], ch=ch)

            def back1(c):
                p = c["pend"]
                lk, lt = p["l"]
                ck, ct, _, cfree = cring.next()
                init = carry0[:, 0:1] if c["prev_c"] is None else c["prev_c"][:, 511:512]
                t_sc = P.op("vector", (lambda e, lt=lt, ct=ct, init=init: e.tensor_tensor_scan(
                    out=ct[:], data0=lt[:], data1=zeros[:], initial=init, op0=ALU.add, op1=ALU.add)),
                    [c["t_l"], c["t_prev_sc"]] + cfree, S_dve)
                c["t_prev_sc"] = t_sc
                lring.free[lk] = [t_sc]
                c["prev_c"] = ct
                p["c"] = (ck, ct)
                p["t_sc"] = t_sc

            def back2(c):
                p = c["pend"]
                ck, ct = p["c"]
                wk, wt, _, wfree = wring.next()
                t_w = P.op("scalar", (lambda e, ct=ct, wt=wt: e.activation(
                    out=wt[:], in_=ct[:], func=AF.Exp, scale=-1.0)), [p["t_sc"]] + wfree, S_act)
                cring.free[ck] = [t_w]
                p["w"] = (wk, wt)
                p["t_w"] = t_w

            def back3(c):
                p = c["pend"]
                ek, et = p["e"]
                wk, wt = p["w"]
                ak, at, _, afree = aring.next()
                t_a = P.op("vector", (lambda e, et=et, wt=wt, at=at: e.tensor_tensor(
                    out=at[:], in0=et[:], in1=wt[:], op=ALU.mult)), [p["t_w"]] + afree, S_dve)
                ering.free[ek] = [t_a]
                wring.free[wk] = [t_a]
                p["a"] = (ak, at)
                p["t_a"] = t_a

            def back4(c):
                p = c["pend"]
                ak, at = p["a"]
                tb = tcnt[0] % 2
                tcnt[0] += 1
                t_tr = None
                for j in range(4):
                    t_tr = P.op("tensor", (lambda e, tb=tb, j=j, at=at: e.transpose(
                        out=psbs[tb][:, j * 128:(j + 1) * 128], in_=at[:, j * 128:(j + 1) * 128],
                        identity=ident_b[:])), [p["t_a"], tfree[tb]], S_pe if j == 3 else None)
                aring.free[ak] = [t_tr]
                p["tb"] = tb
                p["t_tr"] = t_tr

            def back5(c):
                p = c["pend"]
                tb = p["tb"]
                tk_, att, _, atfree = atring.next()
                t_at = P.op("scalar", (lambda e, tb=tb, att=att: e.activation(
                    out=att[:], in_=psbs[tb][:, 0:512].rearrange("p (j t) -> p j t", j=4),
                    func=AF.Copy)), [p["t_tr"]] + atfree, S_act)
                tfree[tb] = t_at
                p["at"] = (tk_, att)
                p["t_at"] = t_at

            def back6(c):
                p = c["pend"]
                tk_, att = p["at"]
                hd = head[c["hh"]]
                ob = c["ob"]
                ch = p["ch"]
                last = None
                for j in range(4):
                    n_av = c["n_av"]
                    last = P.op("tensor", (lambda e, ob=ob, j=j, att=att, hd=hd, ch=ch, n_av=n_av: e.matmul(
                        psf[ob][:, 0:128], lhsT=att[:, j, :], rhs=hd["vt"][:, ch * 4 + j, :],
                        start=(n_av == 0), stop=(n_av == NCH * 4 - 1))),
                        [p["t_at"], hd["t_v"], ofree.get(ob) if n_av == 0 else None],
                        S_pe if (j == 3) else None)
                    c["n_av"] += 1
                atring.free[tk_] = [last]
                c["last_pe"] = last

            loaded = set()
            last_of_head = {}
            for gi in range(0, len(items), G):
                grp = items[gi:gi + G]
                for (hh, qb) in grp:
                    if hh not in loaded:
                        load_head(hh)
                        loaded.add(hh)
                chains = [dict(hh=hh, qb=qb, zb=zbanks[i], ob=obanks[i], prev_c=None, t_prev_sc=t_z0, n_av=0)
                          for i, (hh, qb) in enumerate(grp)]
                for i in range(NCH + 1):
                    if i >= 1:
                        for c in chains:
                            back1(c)
                    if i < NCH:
                        for c in chains:
                            front(c, i)
                        for c in chains:
                            front2(c, i)
                    if i >= 1:
                        for c in chains:
                            back2(c)
                    if i < NCH:
                        for c in chains:
                            front3(c, i)
                    if i >= 1:
                        for c in chains:
                            back3(c)
                        for c in chains:
                            back4(c)
                            back5(c)
                        for c in chains:
                            back6(c)
                    if i < NCH:
                        for c in chains:
                            front4(c, i)
                for c in chains:
                    ot, ofr = ostg.get()
                    ob = c["ob"]
                    t_o = P.op("vector", (lambda e, ot=ot, ob=ob: e.tensor_copy(out=ot[:], in_=psf[ob][:, 0:128])),
                               [c["last_pe"]] + ofr, S_dve)
                    ofree[ob] = t_o
                    ostg.store(o_d[c["qb"] * 128:(c["qb"] + 1) * 128, c["hh"] * 128:(c["hh"] + 1) * 128], ot[:], [t_o])
                    last_of_head[c["hh"]] = c["last_pe"]
                done_heads = [hh for hh in list(head.keys()) if all((hh, qb) in items[:gi + G] for qb in range(NTO))]
                for hh in done_heads:
                    hd = head.pop(hh)
                    tok = last_of_head[hh]
                    qring.free[hd["qk"]] = [tok]
                    kring.free[hd["kk"]] = [tok]
                    vring.free[hd["vk"]] = [tok]
            dout("o", o_d, BF16)
            if end_stage("s5"):
                return finish()

        for which in (() if os.environ.get('MK_SKIP6') else (0, 1)):
            with ExitStack() as st:
                actT = st.enter_context(nc.sbuf_tensor(f"s6act{which}", [128, GKC, TO], BF16))
                src = gm_d if which == 0 else o_d
                with ExitStack() as st2:
                    load_T(st2, src, GW, actT, [], f"s6l{which}")
                    if end_stage(f"s6l{which}"):
                        return finish()
                sgr = Ring(st, f"s6sg{which}", 3, [128, 512], BF16)
                t1r = Ring(st, f"s6t1{which}", 3, [128, 512], F32)
                stg_f = mk_stager(st, f"s6sf{which}", 3, [128, 512], F32)
                stg_b = mk_stager(st, f"s6sb{which}", 3, [128, 512], BF16)
                sg_src = sga_d if which == 0 else sgb_d

                def ep6(tt, c0, cw, ps, ready, which=which, sgr=sgr, t1r=t1r, stg_f=stg_f, stg_b=stg_b, sg_src=sg_src):
                    ep6m = int(os.environ.get("MK_EP6", "2"))
                    if ep6m == 0:
                        t, fr = stg_f.get()
                        tok = P.op("scalar", (lambda e, t=t, ps=ps, cw=cw: e.activation(
                            out=t[:, 0:cw], in_=ps[:, 0:cw], func=AF.Copy)), [ready] + fr, S_act)
                        stg_f.store(t1_d[tt * 128:(tt + 1) * 128, c0:c0 + cw], t[:, 0:cw], [tok])
                        return tok
                    gk, gt_, gs_, gfree = sgr.next()
                    t_g = P.dma("sync", gt_[:, 0:cw], sg_src[tt * 128:(tt + 1) * 128, c0:c0 + cw], gfree, gs_)
                    if ep6m == 1:
                        t, fr = stg_f.get()
                        tok = P.op("scalar", (lambda e, t=t, ps=ps, cw=cw: e.activation(
                            out=t[:, 0:cw], in_=ps[:, 0:cw], func=AF.Copy)), [ready, t_g] + fr, S_act)
                        sgr.free[gk] = [tok]
                        stg_f.store(t1_d[tt * 128:(tt + 1) * 128, c0:c0 + cw], t[:, 0:cw], [tok])
                        return tok
                    if which == 0:
                        t, fr = stg_f.get()
                        tok = P.op("vector", (lambda e, t=t, ps=ps, gt_=gt_, cw=cw: e.tensor_tensor(
                            out=t[:, 0:cw], in0=ps[:, 0:cw], in1=gt_[:, 0:cw], op=ALU.mult)), [ready, t_g] + fr, S_dve)
                        sgr.free[gk] = [tok]
                        stg_f.store(t1_d[tt * 128:(tt + 1) * 128, c0:c0 + cw], t[:, 0:cw], [tok])
                        return tok
                    k1, t1, s1, f1 = t1r.next()
                    t_1 = P.dma("sync", t1[:, 0:cw], t1_d[tt * 128:(tt + 1) * 128, c0:c0 + cw], f1, s1)
                    t, fr = stg_b.get()
                    tok_a = P.op("vector", (lambda e, ps=ps, gt_=gt_, cw=cw: e.tensor_tensor(
                        out=gt_[:, 0:cw], in0=ps[:, 0:cw], in1=gt_[:, 0:cw], op=ALU.mult)), [ready, t_g], S_dve)
                    tok_b = P.op("vector", (lambda e, t=t, t1=t1, gt_=gt_, cw=cw: e.tensor_tensor(
                        out=t[:, 0:cw], in0=gt_[:, 0:cw], in1=t1[:, 0:cw], op=ALU.add)), [t_1, tok_a] + fr, S_dve)
                    sgr.free[gk] = [tok_b]
                    t1r.free[k1] = [tok_b]
                    stg_b.store(mix_d[tt * 128:(tt + 1) * 128, c0:c0 + cw], t[:, 0:cw], [tok_b])
                    return tok_a

                wring = Ring(st, f"s6w{which}", 2, [128, GKC, 512], BF16)
                gemm(wring, actT, GKC, wpa if which == 0 else wpb, 0, 0, D, ep6)
                if which == 1:
                    dout("mix", mix_d, BF16)
                if end_stage(f"s6{which}"):
                    return finish()

        with ExitStack() as st:
            actT = st.enter_context(nc.sbuf_tensor("s8act", [128, KC, TO], BF16))
            with ExitStack() as st2:
                load_T(st2, (sga_d if os.environ.get("MK_S8SRC") else mix_d)[:, 0:int(os.environ.get("MK_LTK", D))], int(os.environ.get("MK_LTK", D)), actT, [], "s8l")
                if end_stage("s8l"):
                    return finish()
            stg_f = mk_stager(st, "s8sf", 3, [128, 512], F32)

            def ep8(tt, c0, cw, ps, ready):
                t, fr = stg_f.get()
                tok = P.op("scalar", (lambda e, t=t, ps=ps, cw=cw: e.activation(
                    out=t[:, 0:cw], in_=ps[:, 0:cw], func=AF.Copy)), [ready] + fr, S_act)
                stg_f.store(mo_d[tt * 128:(tt + 1) * 128, c0:c0 + cw], t[:, 0:cw], [tok])
                return tok

            wring = Ring(st, "s8w", 2, [128, KC, 512], BF16)
            gemm(wring, actT, KC, wo, 0, 0, D, ep8)
            dout("mo", mo_d, F32)
            if end_stage("s8"):
                return finish()

        def post_norm_residual(st, br_d, res_d, gsrc, i_gt, dst_d, tag):
            gtb, _ = bcast_load(st, tag + "gt", modrow_ap(i_gt), D)
            gb, t3 = bcast_load(st, tag + "g", gsrc[0:1, :], D)
            t_gg = P.op("vector", lambda e: e.tensor_tensor(out=gtb[:], in0=gtb[:], in1=gb[:], op=ALU.mult), [t3], S_dve)
            bring = Ring(st, tag + "b", 2, [128, D], F32)
            rring = Ring(st, tag + "r", 2, [128, D], F32)
            junk = st.enter_context(nc.sbuf_tensor(tag + "junk", [128, D], BF16))
            stt = st.enter_context(nc.sbuf_tensor(tag + "st", [128, 4 * NTO], F32))
            stg = mk_stager(st, tag + "o", 2, [128, D], F32)
            for tt in range(NTO):
                bk, bt, bs_, bfree = bring.next()
                t_b = P.dma("sync", bt[:], br_d[tt * 128:(tt + 1) * 128, :], bfree, bs_)
                rk, rt, rs_, rfree = rring.next()
                t_r = P.dma("sync", rt[:], res_d[tt * 128:(tt + 1) * 128, :], rfree, rs_)
                sc = stt[:, tt * 4:(tt + 1) * 4]
                t_q = P.op("scalar", (lambda e, bt=bt, sc=sc: e.activation(
                    out=junk[:], in_=bt[:], func=AF.Square, accum_out=sc[:, 0:1])), [t_b], S_act)
                t_sd = P.op("scalar", (lambda e, sc=sc: e.activation(
                    out=sc[:, 1:2], in_=sc[:, 0:1], func=AF.Sqrt, scale=1.0 / D, bias=eps_t[:, 0:1])), [t_q], S_act)
                t_iv = P.op("vector", (lambda e, sc=sc: e.reciprocal(out=sc[:, 2:3], in_=sc[:, 1:2])), [t_sd], S_dve)
                t_m = P.op("vector", (lambda e, bt=bt, sc=sc: e.scalar_tensor_tensor(
                    out=bt[:], in0=bt[:], scalar=sc[:, 2:3], in1=gtb[:], op0=ALU.mult, op1=ALU.mult)),
                    [t_gg, t_iv], S_dve)
                ot, ofr = stg.get()
                tok = P.op("vector", (lambda e, bt=bt, rt=rt, ot=ot: e.tensor_tensor(
                    out=ot[:], in0=bt[:], in1=rt[:], op=ALU.add)), [t_r, t_m] + ofr, S_dve)
                bring.free[bk] = [tok]
                rring.free[rk] = [tok]
                stg.store(dst_d[tt * 128:(tt + 1) * 128, :], ot[:], [tok])

        with ExitStack() as st:
            post_norm_residual(st, mo_d, xo, gpost, 2, x1_d, "s9")
            dout("x1", x1_d, F32)
            if end_stage("s9"):
                return finish()

        with ExitStack() as st:
            h2T = st.enter_context(nc.sbuf_tensor("h2T", [128, KC, TO], BF16))
            with ExitStack() as st2:
                mb, shb, t_m = mod_tiles(st2, gpre2, 4, 3, "s10")

                def emit10(tt, kc0, nk, ps, ready):
                    return P.op("scalar", (lambda e, tt=tt, kc0=kc0, nk=nk, ps=ps: e.activation(
                        out=h2T[:, kc0:kc0 + nk, tt * 128:(tt + 1) * 128],
                        in_=ps[:, 0:nk * 128].rearrange("p (k t) -> p k t", k=nk), func=AF.Copy)), [ready], S_act)

                P.op("vector", None, [t_m])
                frontend(st2, x1_d, NTO, mb, shb, emit10, "s10")
                if end_stage("s10a"):
                    return finish()
            rr = Ring(st, "s10r", 3, [128, 512], F32)
            stg_b = mk_stager(st, "s10sb", 3, [128, 512], BF16)

            def ep10(tt, c0, cw, ps, ready):
                rk, rt, _, rfree = rr.next()
                tok = P.op("scalar", (lambda e, rt=rt, ps=ps, cw=cw: e.activation(
                    out=rt[:, 0:cw], in_=ps[:, 0:cw], func=AF.Relu)), [ready] + rfree, S_act)
                t, fr = stg_b.get()
                tok2 = P.op("vector", (lambda e, t=t, rt=rt, cw=cw: e.tensor_tensor(
                    out=t[:, 0:cw], in0=rt[:, 0:cw], in1=rt[:, 0:cw], op=ALU.mult)), [tok] + fr, S_dve)
                rr.free[rk] = [tok2]
                stg_b.store(f_d[tt * 128:(tt + 1) * 128, c0:c0 + cw], t[:, 0:cw], [tok2])
                return tok

            wring = Ring(st, "s10w", 2, [128, KC, 512], BF16)
            gemm(wring, h2T, KC, wf1, 0, 0, DFF, ep10)
            dout("f", f_d, BF16)
            if end_stage("s10"):
                return finish()

        for kb in range(DFF // D):
            with ExitStack() as st:
                actT = st.enter_context(nc.sbuf_tensor(f"s11act{kb}", [128, KC, TO], BF16))
                with ExitStack() as st2:
                    load_T(st2, f_d[:, kb * D:(kb + 1) * D], D, actT, [], f"s11l{kb}")
                    if end_stage(f"s11l{kb}"):
                        return finish()
                ar = Ring(st, f"s11a{kb}", 3, [128, 512], F32)
                stg_f = mk_stager(st, f"s11sf{kb}", 3, [128, 512], F32)

                def ep11(tt, c0, cw, ps, ready, kb=kb, ar=ar, stg_f=stg_f):
                    t, fr = stg_f.get()
                    if kb == 0:
                        tok = P.op("scalar", (lambda e, t=t, ps=ps, cw=cw: e.activation(
                            out=t[:, 0:cw], in_=ps[:, 0:cw], func=AF.Copy)), [ready] + fr, S_act)
                    else:
                        ak, at, as_, afree = ar.next()
                        t_a = P.dma("sync", at[:, 0:cw], acc_d[tt * 128:(tt + 1) * 128, c0:c0 + cw], afree, as_)
                        tok = P.op("vector", (lambda e, t=t, ps=ps, at=at, cw=cw: e.tensor_tensor(
                            out=t[:, 0:cw], in0=ps[:, 0:cw], in1=at[:, 0:cw], op=ALU.add)), [ready, t_a] + fr, S_dve)
                        ar.free[ak] = [tok]
                    stg_f.store(acc_d[tt * 128:(tt + 1) * 128, c0:c0 + cw], t[:, 0:cw], [tok])
                    return tok

                wring = Ring(st, f"s11w{kb}", 2, [128, KC, 512], BF16)
                gemm(wring, actT, KC, wf2, kb * D, 0, D, ep11)
                if end_stage(f"s11{kb}"):
                    return finish()

        with ExitStack() as st:
            post_norm_residual(st, acc_d, x1_d, gpost2, 5, y, "s12")
            end_stage("s12")
        return finish()


def prep_inputs(cfg, x, c, w_ada, b_ada, g_pre_mix, w_in, g_v, w_s, b_s, w_proj_a, w_proj_b, w_o,
                g_post_mix, g_pre_mlp, w_ff1, w_ff2, g_post_mlp):
    f = lambda a: np.ascontiguousarray(np.asarray(a, dtype=np.float32))
    xr = f(np.asarray(x)[0, ::-1, :])
    common = {
        "xr": xr,
        "cv": f(np.asarray(c)[0].reshape(cfg.KC, 128).T),
        "wada": f(np.asarray(w_ada)[0]),
        "bada": f(np.asarray(b_ada)[0][None, :]),
        "win": f(np.asarray(w_in)[0]),
        "gpre": f(np.asarray(g_pre_mix)[0][None, :]),
        "gv": f(np.asarray(g_v)[0][None, :]),
        "wsT": f(np.asarray(w_s)[0][:, ::-1, ::-1].transpose(2, 0, 1)),
        "bsr": f(np.asarray(b_s)[0][:, ::-1].T),
        "wpa": f(np.asarray(w_proj_a)[0]),
        "wpb": f(np.asarray(w_proj_b)[0]),
        "wo": f(np.asarray(w_o)[0]),
        "gpost": f(np.asarray(g_post_mix)[0][None, :]),
        "gpre2": f(np.asarray(g_pre_mlp)[0][None, :]),
        "wf1": f(np.asarray(w_ff1)[0]),
        "wf2": f(np.asarray(w_ff2)[0]),
        "gpost2": f(np.asarray(g_post_mlp)[0][None, :]),
        "identf": np.eye(128, dtype=np.float32),
        "kpos": np.arange(cfg.S, dtype=np.float32)[None, :],
    }
    in_maps = []
    for cid in range(cfg.NC):
        m = dict(common)
        m["xo"] = f(xr[cid * cfg.TO:(cid + 1) * cfg.TO])
        m["qpos"] = f((cid * cfg.TO + np.arange(cfg.TO)).reshape(cfg.NTO, 128).T)
        in_maps.append(m)
    return in_maps


_CACHE = {}


def run(cfg, inputs, stop_after=None, debug=False, trace=False):
    key = (cfg.D, cfg.S, cfg.NC, stop_after, debug)
    if key not in _CACHE:
        _CACHE[key] = build_nc(cfg, stop_after, debug)
    nc = _CACHE[key]
    in_maps = prep_inputs(cfg, **inputs)
    return run_bass_kernel_spmd(nc, in_maps, core_ids=list(range(cfg.NC)), trace=trace)


def kernel(**inputs):
    cfg = Cfg()
    res = run(cfg, inputs)
    yr = np.concatenate([np.asarray(r["y"]) for r in res.results], axis=0)
    return np.ascontiguousarray(yr[::-1])[None].astype(np.float32)
```

```python
import os
from contextlib import ExitStack
import numpy as np
import concourse.bass as bass
import concourse.mybir as mybir
from concourse.bass_utils import run_bass_kernel_spmd

F32 = mybir.dt.float32
BF16 = mybir.dt.bfloat16
AF = mybir.ActivationFunctionType
ALU = mybir.AluOpType
EPS = 1e-6
ENGS = ("sync", "scalar", "gpsimd", "vector", "tensor")


class Cfg:
    def __init__(self, D=4096, S=8192, NC=8):
        self.D, self.S, self.NC = D, S, NC
        self.KC = D // 128
        self.TO = S // NC
        self.NTO = self.TO // 128
        self.NTA = S // 128
        self.GW = D // 2
        self.NH = self.GW // 128
        self.DFF = 4 * D
        self.NMOD = 6 * D
        self.INC = 2 * self.GW + 3 * self.GW + 2 * D


class CS:
    def __init__(self, h):
        self.h = h
        self.n = 0


class Prog:
    def __init__(self, nc):
        self.nc = nc
        self.q = {e: [] for e in ENGS}
        self.waited = {e: {} for e in ENGS}
        self.nblk = 0

    def op(self, eng, fn, waits=(), sig=None, amt=1):
        tok = None
        if sig is not None:
            sig.n += amt
            tok = (sig, sig.n)
        ws = []
        seen = self.waited[eng]
        stack = list(waits)
        while stack:
            w = stack.pop()
            if w is None:
                continue
            if isinstance(w, list):
                stack.extend(w)
                continue
            s, v = w
            if seen.get(id(s), 0) >= v:
                continue
            seen[id(s)] = v
            ws.append((s, v))
        self.q[eng].append((fn, ws, sig, amt))
        return tok

    def dma(self, eng, out, in_, waits=(), sig=None, **kw):
        return self.op(eng, lambda e: e.dma_start(out=out, in_=in_, **kw), waits, sig, 16)

    def flush(self, name=None):
        self.nblk += 1
        name = name or f"blk{self.nblk}"
        with self.nc.Block(name) as block:
            for eng in ENGS:
                items = self.q[eng]
                if not items:
                    continue

                def body(e, items=items):
                    for fn, ws, sig, amt in items:
                        for (s, v) in ws:
                            e.wait_ge(s.h, v)
                        if fn is None:
                            continue
                        ins = fn(e)
                        if sig is not None:
                            ins.then_inc(sig.h, amt)

                getattr(block, eng)(body)
        self.q = {e: [] for e in ENGS}


def build_nc(cfg, stop_after=None, debug=False):
    D, S, KC, TO, NTO, NTA, GW, NH, DFF = cfg.D, cfg.S, cfg.KC, cfg.TO, cfg.NTO, cfg.NTA, cfg.GW, cfg.NH, cfg.DFF
    GKC = GW // 128
    nc = bass.Bass("TRN2", target_bir_lowering=False)
    P = Prog(nc)

    def din(name, shape, dt=F32):
        return nc.dram_tensor(name, list(shape), dt, kind="ExternalInput").ap()

    def dscr(name, shape, dt):
        return nc.dram_tensor(name, list(shape), dt).ap()

    xr = din("xr", [S, D])
    xo = din("xo", [TO, D])
    cv = din("cv", [128, KC])
    wada = din("wada", [D, cfg.NMOD])
    bada = din("bada", [1, cfg.NMOD])
    win = din("win", [D, cfg.INC])
    gpre = din("gpre", [1, D])
    gv = din("gv", [1, GW])
    wsT = din("wsT", [128, NH, 128])
    bsr = din("bsr", [128, NH])
    wpa = din("wpa", [GW, D])
    wpb = din("wpb", [GW, D])
    wo = din("wo", [D, D])
    gpost = din("gpost", [1, D])
    gpre2 = din("gpre2", [1, D])
    wf1 = din("wf1", [D, DFF])
    wf2 = din("wf2", [DFF, D])
    gpost2 = din("gpost2", [1, D])
    identf = din("identf", [128, 128])
    kpos = din("kpos", [1, S])
    qpos = din("qpos", [128, NTO])
    y = nc.dram_tensor("y", [TO, D], F32, kind="ExternalOutput").ap()

    dbg = {}

    def dout(name, src, dt=F32):
        if not debug:
            return
        a = nc.dram_tensor("dbg_" + name, list(src.shape), dt, kind="ExternalOutput").ap()
        dbg[name] = (a, src)

    mod_d = dscr("mod_d", [1, cfg.NMOD], F32)
    hT_d = dscr("hT_d", [NTA // 4, 128, KC, 512], BF16)
    kT_d = dscr("kT_d", [NH, 128, S], BF16)
    v_d = dscr("v_d", [S, GW], BF16)
    qT_d = dscr("qT_d", [NH, 128, TO], BF16)
    u_d = dscr("u_d", [TO, GW], BF16)
    vr_d = dscr("vr_d", [TO, GW], F32)
    sga_d = dscr("sga_d", [TO, D], BF16)
    sgb_d = dscr("sgb_d", [TO, D], BF16)
    gm_d = dscr("gm_d", [TO, GW], BF16)
    o_d = dscr("o_d", [TO, GW], BF16)
    t1_d = dscr("t1_d", [TO, D], F32)
    mix_d = dscr("mix_d", [TO, D], BF16)
    mo_d = dscr("mo_d", [TO, D], F32)
    x1_d = dscr("x1_d", [TO, D], F32)
    f_d = dscr("f_d", [TO, DFF], BF16)
    acc_d = dscr("acc_d", [TO, D], F32)

    with ExitStack() as es:
        def sem(name):
            return CS(es.enter_context(nc.semaphore(name)))

        S_pe, S_act, S_dve = sem("s_pe"), sem("s_act"), sem("s_dve")
        S_ld = sem("s_ld")
        S_st = sem("s_st")
        ring_sems = [sem(f"rs{i}") for i in range(24)]
        psf = [es.enter_context(nc.psum_tensor(f"psf{i}", [128, 512], F32)) for i in range(6)]
        psbs = [es.enter_context(nc.psum_tensor(f"psb{i}", [128, 1024], BF16)) for i in range(2)]
        ident_f = es.enter_context(nc.sbuf_tensor("ident_f", [128, 128], F32))
        ident_b = es.enter_context(nc.sbuf_tensor("ident_b", [128, 128], BF16))
        eps_t = es.enter_context(nc.sbuf_tensor("eps_t", [128, 1], F32))

        state = {"rs": 0}
        PSF = {}
        PI = [0]

        def next_bank():
            b = 2 + PI[0] % 4
            PI[0] += 1
            return b

        def new_ring_sem():
            assert state["rs"] < len(ring_sems), "out of ring semaphores in this stage"
            s_ = ring_sems[state["rs"]]
            state["rs"] += 1
            return s_

        stagers = []

        def wait_stagers():
            toks = []
            for s_ in stagers:
                for fr in s_.ring.free:
                    toks.extend(fr)
            P.op("sync", None, toks)
            stagers.clear()

        def end_stage(name):
            wait_stagers()
            P.op("sync", None, [(S_st, S_st.n), (S_ld, S_ld.n)])
            P.flush(name)
            state["rs"] = 0
            if os.environ.get("MK_VERBOSE"):
                print("stage", name, "sem counts pe/act/dve/ld/st", S_pe.n, S_act.n, S_dve.n, S_ld.n, S_st.n,
                      "ring max", max(r.n for r in ring_sems), flush=True)
            return stop_after == name

        def finish():
            for name, (a, src) in dbg.items():
                P.dma("sync", a, src, [], S_st)
            P.op("sync", None, [(S_st, S_st.n)])
            P.flush("fin")
            return nc

        class Ring:
            def __init__(self, st, name, n, shape, dt, sems=True):
                self.bufs = [st.enter_context(nc.sbuf_tensor(f"{name}{i}", list(shape), dt)) for i in range(n)]
                self.sems = [new_ring_sem() if sems else None for _ in range(n)]
                self.free = [[] for _ in range(n)]
                self.i = 0

            def next(self):
                k = self.i % len(self.bufs)
                self.i += 1
                fr = self.free[k]
                self.free[k] = []
                return k, self.bufs[k], self.sems[k], fr

        class Stager:
            def __init__(self, st, name, n, shape, dt):
                self.ring = Ring(st, name, n, shape, dt)

            def get(self):
                k, t, s_, fr = self.ring.next()
                self.k, self.s = k, s_
                return t, fr

            def store(self, dst, src, waits):
                tok = P.dma("sync", dst, src, waits, self.s)
                self.ring.free[self.k] = [tok]
                return tok

        def mk_stager(st, name, n, shape, dt):
            s_ = Stager(st, name, n, shape, dt)
            stagers.append(s_)
            return s_

        def bcast_load(st, name, src_row, n, dt=F32, waits=()):
            t = st.enter_context(nc.sbuf_tensor(name, [128, n], dt))
            tok = P.dma("sync", t[:], src_row.partition_broadcast(128), list(waits), S_ld)
            return t, tok

        def frontend(st, src_d, ntiles, mb, shb, emit_hT, tag):
            xring = Ring(st, tag + "x", 2, [128, D], F32)
            hring = Ring(st, tag + "h", 2, [128, D], F32)
            junk = st.enter_context(nc.sbuf_tensor(tag + "junk", [128, D], BF16))
            ssq = st.enter_context(nc.sbuf_tensor(tag + "ssq", [128, ntiles], F32))
            std = st.enter_context(nc.sbuf_tensor(tag + "std", [128, ntiles], F32))
            inv = st.enter_context(nc.sbuf_tensor(tag + "inv", [128, ntiles], F32))
            psfree = [None, None]
            pi = 0
            for tt in range(ntiles):
                k, xt, xs, xfree = xring.next()
                t_ld = P.dma("sync", xt[:], src_d[tt * 128:(tt + 1) * 128, :], xfree, xs)
                t_sq = P.op("scalar", (lambda e, xt=xt, tt=tt: e.activation(
                    out=junk[:], in_=xt[:], func=AF.Square, accum_out=ssq[:, tt:tt + 1])), [t_ld], S_act)
                t_sd = P.op("scalar", (lambda e, tt=tt: e.activation(
                    out=std[:, tt:tt + 1], in_=ssq[:, tt:tt + 1], func=AF.Sqrt, scale=1.0 / D, bias=eps_t[:, 0:1])),
                    [t_sq], S_act)
                t_iv = P.op("vector", (lambda e, tt=tt: e.reciprocal(out=inv[:, tt:tt + 1], in_=std[:, tt:tt + 1])),
                            [t_sd], S_dve)
                hk, ht, _, hfree = hring.next()
                t_h0 = P.op("vector", (lambda e, xt=xt, ht=ht, tt=tt: e.scalar_tensor_tensor(
                    out=ht[:], in0=xt[:], scalar=inv[:, tt:tt + 1], in1=mb[:], op0=ALU.mult, op1=ALU.mult)),
                    [t_ld, t_iv] + hfree, S_dve)
                t_h = P.op("vector", (lambda e, ht=ht: e.tensor_tensor(out=ht[:], in0=ht[:], in1=shb[:], op=ALU.add)),
                           [t_h0], S_dve)
                xring.free[k] = [t_h, t_sq]
                t_last = None
                for kc0 in range(0, KC, 4):
                    nk = min(4, KC - kc0)
                    b = pi % 2
                    pi += 1
                    for j in range(nk):
                        t_last = P.op("tensor", (lambda e, b=b, j=j, ht=ht, kc=kc0 + j: e.transpose(
                            out=psf[b][:, j * 128:(j + 1) * 128], in_=ht[:, kc * 128:(kc + 1) * 128], identity=ident_f[:])),
                            [t_h, psfree[b]], S_pe if j == nk - 1 else None)
                    psfree[b] = emit_hT(tt, kc0, nk, psf[b], t_last)
                hring.free[hk] = [t_last]

        def load_T(st, src_d, K, actT, waits, tag):
            ring = Ring(st, tag + "ld", 2, [128, K], BF16)
            half_free = [None, None]
            hi = 0
            for tt in range(NTO):
                k, t, s_, fr = ring.next()
                t_ld = P.dma("sync", t[:], src_d[tt * 128:(tt + 1) * 128, :], fr + list(waits), s_)
                t_last = None
                for kc0 in range(0, K // 128, 4):
                    nk = min(4, K // 128 - kc0)
                    b = hi % 2
                    hi += 1
                    for j in range(nk):
                        t_last = P.op("tensor", (lambda e, b=b, j=j, t=t, kc=kc0 + j: e.transpose(
                            out=psbs[b][:, j * 128:(j + 1) * 128], in_=t[:, kc * 128:(kc + 1) * 128],
                            identity=ident_b[:])), [t_ld, half_free[b]], S_pe if j == nk - 1 else None)
                    half_free[b] = P.op("scalar", (lambda e, b=b, nk=nk, kc0=kc0, tt=tt: e.activation(
                        out=actT[:, kc0:kc0 + nk, tt * 128:(tt + 1) * 128],
                        in_=psbs[b][:, 0:nk * 128].rearrange("p (k t) -> p k t", k=nk), func=AF.Copy)),
                        [t_last], S_act)
                ring.free[k] = [t_last]

        def gemm(wring, actT, kcn, w_d, row0, col0, ncols, epilogue, cgw=512):
            wv = w_d[row0:row0 + kcn * 128, :].rearrange("(kc p) n -> p kc n", p=128)
            psfree = PSF
            for c0 in range(0, ncols, cgw):
                cw = min(cgw, ncols - c0)
                k, wt, ws_, wfree = wring.next()
                t_w = P.dma("gpsimd", wt[:, :, 0:cw], wv[:, :, col0 + c0:col0 + c0 + cw], wfree, ws_)
                t_last = None
                for tt in range(NTO):
                    b = next_bank()
                    for kc in range(kcn):
                        t_last = P.op("tensor", (lambda e, b=b, kc=kc, wt=wt, cw=cw, tt=tt: e.matmul(
                            psf[b][:, 0:cw], lhsT=actT[:, kc, tt * 128:(tt + 1) * 128], rhs=wt[:, kc, 0:cw],
                            start=(kc == 0), stop=(kc == kcn - 1))),
                            [t_w, psfree.get(b)], S_pe if kc == kcn - 1 else None)
                    psfree[b] = epilogue(tt, c0, cw, psf[b], t_last)
                wring.free[k] = [t_last]

        with ExitStack() as st:
            cvt = st.enter_context(nc.sbuf_tensor("cvt", [128, KC], F32))
            s_bf = st.enter_context(nc.sbuf_tensor("s_bf", [128, KC], BF16))
            P.dma("sync", ident_f[:], identf[:, :], [], S_ld)
            t_ld0 = P.dma("sync", cvt[:], cv[:, :], [], S_ld)
            P.op("vector", lambda e: e.tensor_copy(out=ident_b[:], in_=ident_f[:]), [t_ld0], S_dve)
            P.op("vector", lambda e: e.memset(eps_t[:], EPS), [], S_dve)
            t_s = P.op("scalar", lambda e: e.activation(out=s_bf[:], in_=cvt[:], func=AF.Silu), [t_ld0], S_act)
            wring = Ring(st, "wada", 2, [128, KC, 512], BF16)
            bring = Ring(st, "bada", 2, [1, 512], F32)
            mstg = mk_stager(st, "modst", 2, [1, 512], F32)
            wv = wada.rearrange("(kc p) n -> p kc n", p=128)
            psfree = [None, None]
            for cg in range(cfg.NMOD // 512):
                k, wt, ws_, wfree = wring.next()
                t_w = P.dma("gpsimd", wt[:], wv[:, :, cg * 512:(cg + 1) * 512], wfree, ws_)
                bk, bt, bs_, bfree = bring.next()
                t_b = P.dma("sync", bt[:], bada[0:1, cg * 512:(cg + 1) * 512], bfree, bs_)
                b = cg % 2
                tk = None
                for kc in range(KC):
                    tk = P.op("tensor", (lambda e, b=b, kc=kc, wt=wt: e.matmul(
                        psf[b][0:1, :], lhsT=s_bf[:, kc:kc + 1], rhs=wt[:, kc, :],
                        start=(kc == 0), stop=(kc == KC - 1))), [t_w, t_s, psfree[b]], S_pe if kc == KC - 1 else None)
                wring.free[k] = [tk]
                mt, mfr = mstg.get()
                t_ev = P.op("vector", (lambda e, b=b, mt=mt, bt=bt: e.tensor_tensor(
                    out=mt[0:1, :], in0=psf[b][0:1, :], in1=bt[0:1, :], op=ALU.add)), [tk, t_b] + mfr, S_dve)
                psfree[b] = t_ev
                bring.free[bk] = [t_ev]
                mstg.store(mod_d[0:1, cg * 512:(cg + 1) * 512], mt[0:1, :], [t_ev])
            dout("mod", mod_d)
            if end_stage("s0"):
                return finish()

        def modrow_ap(i):
            return mod_d[0:1, i * D:(i + 1) * D]

        def mod_tiles(st, gsrc, i_sc, i_sh, tag):
            mb, _ = bcast_load(st, tag + "mb", modrow_ap(i_sc), D)
            shb, _ = bcast_load(st, tag + "shb", modrow_ap(i_sh), D)
            gb, t3 = bcast_load(st, tag + "gb", gsrc[0:1, :], D)
            t_m = P.op("vector", lambda e: e.scalar_tensor_tensor(
                out=mb[:], in0=mb[:], scalar=1.0, in1=gb[:], op0=ALU.add, op1=ALU.mult), [t3], S_dve)
            return mb, shb, t_m

        with ExitStack() as st:
            mb, shb, t_m = mod_tiles(st, gpre, 1, 0, "s1")
            stg = mk_stager(st, "s1stg", 2, [128, KC, 512], BF16)
            cur = {}

            def emit1(tt, kc0, nk, ps, ready):
                if tt % 4 == 0 and kc0 == 0:
                    cur["t"], cur["fr"] = stg.get()
                t = cur["t"]
                tok = P.op("scalar", (lambda e, t=t, tt=tt, kc0=kc0, nk=nk, ps=ps: e.activation(
                    out=t[:, kc0:kc0 + nk, (tt % 4) * 128:(tt % 4 + 1) * 128],
                    in_=ps[:, 0:nk * 128].rearrange("p (k t) -> p k t", k=nk), func=AF.Copy)),
                    [ready] + cur["fr"], S_act)
                cur["fr"] = []
                if tt % 4 == 3 and kc0 + nk == KC:
                    stg.store(hT_d[tt // 4], t[:], [tok])
                return tok

            P.op("vector", None, [t_m])
            frontend(st, xr, NTA, mb, shb, emit1, "s1")
            dout("hT", hT_d, BF16)
            if end_stage("s1"):
                return finish()

        with ExitStack() as st:
            hring = Ring(st, "s2h", 3, [128, KC, 512], BF16)
            wring = Ring(st, "s2w", 2, [128, KC, 512], BF16)
            stg = mk_stager(st, "s2stg", 3, [128, 512], BF16)
            wv = win.rearrange("(kc p) n -> p kc n", p=128)
            kcol0 = 2 * GW + GW
            vcol0 = 2 * GW + 2 * GW
            psfree = PSF
            for isv in (0, 1):
                for c0 in range(0, GW, 512):
                    cw = min(512, GW - c0)
                    wk, wt, ws_, wfree = wring.next()
                    cbase = (vcol0 if isv else kcol0) + c0
                    t_w = P.dma("gpsimd", wt[:, :, 0:cw], wv[:, :, cbase:cbase + cw], wfree, ws_)
                    t_last = None
                    for tg in range(NTA // 4):
                        hk, ht, hs_, hfree = hring.next()
                        t_h = P.dma("sync", ht[:], hT_d[tg], hfree, hs_)
                        if not isv:
                            for hh in range(cw // 128):
                                b = next_bank()
                                for kc in range(KC):
                                    t_last = P.op("tensor", (lambda e, b=b, kc=kc, wt=wt, ht=ht, hh=hh: e.matmul(
                                        psf[b][:, :], lhsT=wt[:, kc, hh * 128:(hh + 1) * 128], rhs=ht[:, kc, :],
                                        start=(kc == 0), stop=(kc == KC - 1))), [t_w, t_h, psfree.get(b)],
                                        S_pe if kc == KC - 1 else None)
                                sg_t, fr = stg.get()
                                tok = P.op("scalar", (lambda e, sg_t=sg_t, b=b: e.activation(
                                    out=sg_t[:], in_=psf[b][:, :], func=AF.Copy)), [t_last] + fr, S_act)
                                psfree[b] = tok
                                head = (c0 // 128) + hh
                                stg.store(kT_d[head][:, tg * 512:(tg + 1) * 512], sg_t[:], [tok])
                        else:
                            for t4 in range(4):
                                b = next_bank()
                                for kc in range(KC):
                                    t_last = P.op("tensor", (lambda e, b=b, kc=kc, wt=wt, ht=ht, t4=t4, cw=cw: e.matmul(
                                        psf[b][:, 0:cw], lhsT=ht[:, kc, t4 * 128:(t4 + 1) * 128], rhs=wt[:, kc, 0:cw],
                                        start=(kc == 0), stop=(kc == KC - 1))), [t_w, t_h, psfree.get(b)],
                                        S_pe if kc == KC - 1 else None)
                                sg_t, fr = stg.get()
                                tok = P.op("vector", (lambda e, sg_t=sg_t, b=b, cw=cw: e.tensor_copy(
                                    out=sg_t[:, 0:cw], in_=psf[b][:, 0:cw])), [t_last] + fr, S_dve)
                                psfree[b] = tok
                                r0 = tg * 512 + t4 * 128
                                stg.store(v_d[r0:r0 + 128, c0:c0 + cw], sg_t[:, 0:cw], [tok])
                        hring.free[hk] = [t_last]
                    wring.free[wk] = [t_last]
            dout("kT", kT_d, BF16)
            dout("v", v_d, BF16)
            if end_stage("s2"):
                return finish()

        with ExitStack() as st:
            hT_own = st.enter_context(nc.sbuf_tensor("hT_own", [128, KC, TO], BF16))
            with ExitStack() as st2:
                mb, shb, t_m = mod_tiles(st2, gpre, 1, 0, "s3")

                def emit3(tt, kc0, nk, ps, ready):
                    return P.op("scalar", (lambda e, tt=tt, kc0=kc0, nk=nk, ps=ps: e.activation(
                        out=hT_own[:, kc0:kc0 + nk, tt * 128:(tt + 1) * 128],
                        in_=ps[:, 0:nk * 128].rearrange("p (k t) -> p k t", k=nk), func=AF.Copy)), [ready], S_act)

                P.op("vector", None, [t_m])
                frontend(st2, xo, NTO, mb, shb, emit3, "s3")
                if end_stage("s3a"):
                    return finish()
            stg_b = mk_stager(st, "s3sb", 3, [128, 512], BF16)
            stg_f = mk_stager(st, "s3sf", 3, [128, 512], F32)

            def ep_act(func, dst_d, stg):
                def ep(tt, c0, cw, ps, ready):
                    t, fr = stg.get()
                    tok = P.op("scalar", (lambda e, t=t, ps=ps, cw=cw: e.activation(
                        out=t[:, 0:cw], in_=ps[:, 0:cw], func=func)), [ready] + fr, S_act)
                    stg.store(dst_d[tt * 128:(tt + 1) * 128, c0:c0 + cw], t[:, 0:cw], [tok])
                    return tok
                return ep

            wring = Ring(st, "s3w", 2, [128, KC, 512], BF16)
            s3n = int(os.environ.get("MK_S3N", "5"))
            if s3n >= 1 and not os.environ.get("MK_SKIPU"):
                gemm(wring, hT_own, KC, win, 0, 0, GW, ep_act(AF.Gelu, u_d, stg_b))
            if s3n >= 2:
                gemm(wring, hT_own, KC, win, 0, GW, GW, ep_act(AF.Gelu, vr_d, stg_f))
            if s3n >= 3:
                gemm(wring, hT_own, KC, win, 0, 2 * GW + 3 * GW, D, ep_act(AF.Sigmoid, sga_d, stg_b))
            if s3n >= 4:
                gemm(wring, hT_own, KC, win, 0, 2 * GW + 3 * GW + D, D, ep_act(AF.Sigmoid, sgb_d, stg_b))
            wv = win.rearrange("(kc p) n -> p kc n", p=128)
            psfree = PSF
            for hh in range(NH if s3n >= 5 else 0):
                wk, wt, ws_, wfree = wring.next()
                cb = 2 * GW + hh * 128
                t_w = P.dma("gpsimd", wt[:, :, 0:128], wv[:, :, cb:cb + 128], wfree, ws_)
                t_last = None
                for t0 in range(0, TO, 512):
                    tw = min(512, TO - t0)
                    b = next_bank()
                    for kc in range(KC):
                        t_last = P.op("tensor", (lambda e, b=b, kc=kc, wt=wt, t0=t0, tw=tw: e.matmul(
                            psf[b][:, 0:tw], lhsT=wt[:, kc, 0:128], rhs=hT_own[:, kc, t0:t0 + tw],
                            start=(kc == 0), stop=(kc == KC - 1))), [t_w, psfree.get(b)],
                            S_pe if kc == KC - 1 else None)
                    t, fr = stg_b.get()
                    tok = P.op("scalar", (lambda e, t=t, b=b, tw=tw: e.activation(
                        out=t[:, 0:tw], in_=psf[b][:, 0:tw], func=AF.Copy, scale=float(1.0 / np.sqrt(128.0)))),
                        [t_last] + fr, S_act)
                    psfree[b] = tok
                    stg_b.store(qT_d[hh][:, t0:t0 + tw], t[:, 0:tw], [tok])
                wring.free[wk] = [t_last]
            dout("u", u_d, BF16)
            dout("vr", vr_d, F32)
            dout("qT", qT_d, BF16)
            dout("sga", sga_d, BF16)
            if end_stage("s3"):
                return finish()

        with ExitStack() as st:
            gvb, t_g = bcast_load(st, "s4gv", gv[0:1, :], GW)
            wst = st.enter_context(nc.sbuf_tensor("s4ws", [128, NH, 128], F32))
            wsb = st.enter_context(nc.sbuf_tensor("s4wsb", [128, NH, 128], BF16))
            bst = st.enter_context(nc.sbuf_tensor("s4bs", [128, NH], F32))
            P.dma("sync", wst[:], wsT[:, :, :], [], S_ld)
            t_b = P.dma("sync", bst[:], bsr[:, :], [], S_ld)
            t_g = t_b
            t_ms = P.op("vector", lambda e: e.memset(wst[0:64, :, 64:128], 0.0), [t_b], S_dve)
            t_wb = P.op("vector", lambda e: e.tensor_copy(out=wsb[:], in_=wst[:]), [t_ms], S_dve)
            vring = Ring(st, "s4v", 2, [128, GW], F32)
            uring = Ring(st, "s4u", 2, [128, GW], BF16)
            vnr = Ring(st, "s4vn", 2, [128, GW], BF16)
            stg = mk_stager(st, "s4stg", 2, [128, GW], BF16)
            stats = st.enter_context(nc.sbuf_tensor("s4stats", [128, 8 * NTO], F32))
            junk = st.enter_context(nc.sbuf_tensor("s4junk", [128, GW], BF16))
            psfree = PSF
            for tt in range(NTO):
                vk, vt, vs_, vfree = vring.next()
                t_v = P.dma("sync", vt[:], vr_d[tt * 128:(tt + 1) * 128, :], vfree, vs_)
                uk, ut, us_, ufree = uring.next()
                t_u = P.dma("sync", ut[:], u_d[tt * 128:(tt + 1) * 128, :], ufree, us_)
                sc = stats[:, tt * 8:(tt + 1) * 8]
                t_a1 = P.op("scalar", (lambda e, vt=vt, sc=sc: e.activation(
                    out=junk[:], in_=vt[:], func=AF.Copy, scale=1.0 / GW, accum_out=sc[:, 0:1])), [t_v], S_act)
                t_a2 = P.op("scalar", (lambda e, vt=vt, sc=sc: e.activation(
                    out=junk[:], in_=vt[:], func=AF.Square, scale=float(1.0 / np.sqrt(GW)), accum_out=sc[:, 1:2])),
                    [t_a1], S_act)
                t_x = P.op("vector", (lambda e, sc=sc: e.tensor_tensor(
                    out=sc[:, 2:3], in0=sc[:, 0:1], in1=sc[:, 0:1], op=ALU.mult)), [t_a2], S_dve)
                t_var = P.op("vector", (lambda e, sc=sc: e.tensor_tensor(
                    out=sc[:, 3:4], in0=sc[:, 1:2], in1=sc[:, 2:3], op=ALU.subtract)), [t_x], S_dve)
                t_sd = P.op("scalar", (lambda e, sc=sc: e.activation(
                    out=sc[:, 4:5], in_=sc[:, 3:4], func=AF.Sqrt, bias=eps_t[:, 0:1])), [t_var], S_act)
                t_iv = P.op("vector", (lambda e, sc=sc: e.reciprocal(out=sc[:, 5:6], in_=sc[:, 4:5])), [t_sd], S_dve)
                t_n0 = P.op("vector", (lambda e, vt=vt, sc=sc: e.tensor_scalar(
                    out=vt[:], in0=vt[:], scalar1=sc[:, 0:1], scalar2=sc[:, 5:6], op0=ALU.subtract, op1=ALU.mult)),
                    [t_iv], S_dve)
                nk, vn, _, nfree = vnr.next()
                t_vn = P.op("vector", (lambda e, vt=vt, vn=vn: e.tensor_tensor(
                    out=vn[:], in0=vt[:], in1=gvb[:], op=ALU.mult)), [t_g, t_n0] + nfree, S_dve)
                vring.free[vk] = [t_vn]
                gt_, gfr = stg.get()
                t_last = None
                tok = None
                for g in range(NH):
                    b = next_bank()
                    t_last = P.op("tensor", (lambda e, b=b, g=g, vn=vn: e.matmul(
                        psf[b][:, 0:128], lhsT=wsb[:, g, :], rhs=vn[:, g * 128:(g + 1) * 128], start=True, stop=True)),
                        [t_vn, t_wb, psfree.get(b)], S_pe)
                    tok = P.op("vector", (lambda e, b=b, g=g, ut=ut, gt_=gt_: e.scalar_tensor_tensor(
                        out=gt_[:, g * 128:(g + 1) * 128], in0=psf[b][:, 0:128], scalar=bst[:, g:g + 1],
                        in1=ut[:, g * 128:(g + 1) * 128], op0=ALU.add, op1=ALU.mult)), [t_last, t_u] + gfr, S_dve)
                    gfr = []
                    psfree[b] = tok
                vnr.free[nk] = [t_last]
                uring.free[uk] = [tok]
                stg.store(gm_d[tt * 128:(tt + 1) * 128, :], gt_[:], [tok])
            dout("gm", gm_d, BF16)
            if end_stage("s4"):
                return finish()

        with ExitStack() as st:
            G = 3
            kpb, t_kp = bcast_load(st, "s5kp", kpos[0:1, :], S)
            qp = st.enter_context(nc.sbuf_tensor("s5qp", [128, NTO], F32))
            t_qp = P.dma("sync", qp[:], qpos[:, :], [], S_ld)
            t_c = [t_kp, t_qp]
            qring = Ring(st, "s5q", 2, [128, TO], BF16)
            kring = Ring(st, "s5k", 2, [128, S], BF16)
            vring = Ring(st, "s5v", 2, [128, NTA, 128], BF16)
            RD = 2 * G
            ering = Ring(st, "s5e", RD, [128, 512], F32, sems=False)
            lring = Ring(st, "s5l", RD, [128, 512], F32, sems=False)
            cring = Ring(st, "s5c", RD, [128, 512], F32, sems=False)
            wring = Ring(st, "s5w", RD, [128, 512], F32, sems=False)
            aring = Ring(st, "s5a", RD, [128, 512], BF16, sems=False)
            atring = Ring(st, "s5at", RD, [128, 4, 128], BF16, sems=False)
            zeros = st.enter_context(nc.sbuf_tensor("s5z", [128, 512], F32))
            carry0 = st.enter_context(nc.sbuf_tensor("s5carry0", [128, 1], F32))
            P.op("vector", lambda e: e.memset(zeros[:], 0.0), [], S_dve)
            t_z0 = P.op("vector", lambda e: e.memset(carry0[:], 0.0), [], S_dve)
            ostg = mk_stager(st, "s5o", 4, [128, 128], BF16)
            zbanks = [2, 3, 4]
            obanks = [0, 1, 5]
            ofree = {}
            tfree = [None, None]
            tcnt = [0]
            NCH = S // 512
            items = [(hh, qb) for hh in range(NH) for qb in range(NTO)]
            head = {}

            def load_head(hh):
                qk, qt_, qs_, qfree = qring.next()
                t_q = P.dma("sync", qt_[:], qT_d[hh], qfree, qs_)
                kk, kt, ks_, kfree = kring.next()
                t_k = P.dma("sync", kt[:], kT_d[hh], kfree, ks_)
                vk, vt, vs_, vfree = vring.next()
                vsrc = v_d[:, hh * 128:(hh + 1) * 128].rearrange("(b s) d -> s b d", s=128)
                nsp = max(1, NTA // 16)
                t_v = None
                for sp in range(nsp):
                    b0, b1 = sp * NTA // nsp, (sp + 1) * NTA // nsp
                    t_v = P.dma("sync", vt[:, b0:b1, :], vsrc[:, b0:b1, :], vfree, vs_)
                head[hh] = dict(qt=qt_, kt=kt, vt=vt, t_q=t_q, t_k=t_k, t_v=t_v, qk=qk, kk=kk, vk=vk)

            def front(c, ch):
                hd = head[c["hh"]]
                zb = c["zb"]
                qb = c["qb"]
                t_z = P.op("tensor", (lambda e, zb=zb, qb=qb, hd=hd, ch=ch: e.matmul(
                    psf[zb][:, :], lhsT=hd["qt"][:, qb * 128:(qb + 1) * 128], rhs=hd["kt"][:, ch * 512:(ch + 1) * 512],
                    start=True, stop=True)), [hd["t_q"], hd["t_k"], PSF.get(zb)], S_pe)
                c["t_z"] = t_z

            def front2(c, ch):
                zb = c["zb"]
                ek, et, _, efree = ering.next()
                t_e = P.op("scalar", (lambda e, et=et, zb=zb: e.activation(
                    out=et[:], in_=psf[zb][:, :], func=AF.Exp)), [c["t_z"]] + efree, S_act)
                PSF[zb] = t_e
                c["e"] = (ek, et)
                c["t_e"] = t_e

            def front3(c, ch):
                ek, et = c["e"]
                qb = c["qb"]
                c["t_em"] = P.op("vector", (lambda e, et=et, ch=ch, qb=qb: e.scalar_tensor_tensor(
                    out=et[:], in0=kpb[:, ch * 512:(ch + 1) * 512], scalar=qp[:, qb:qb + 1], in1=et[:],
                    op0=ALU.is_gt, op1=ALU.mult)), [c["t_e"]] + t_c, S_dve)

            def front4(c, ch):
                ek, et = c["e"]
                lk, lt, _, lfree = lring.next()
                c["t_l"] = P.op("scalar", (lambda e, et=et, lt=lt: e.activation(
                    out=lt[:], in_=et[:], func=AF.Ln, bias=1.0)), [c["t_em"]] + lfree, S_act)
                c["l"] = (lk, lt)
                c["pend"] = dict(e=c["e"], l=c["l"], ch=ch)

            def back1(c):
                p = c["pend"]
                lk, lt = p["l"]
                ck, ct, _, cfree = cring.next()
                init = carry0[:, 0:1] if c["prev_c"] is None else c["prev_c"][:, 511:512]
                t_sc = P.op("vector", (lambda e, lt=lt, ct=ct, init=init: e.tensor_tensor_scan(
                    out=ct[:], data0=lt[:], data1=zeros[:], initial=init, op0=ALU.add, op1=ALU.add)),
                    [c["t_l"], c["t_prev_sc"]] + cfree, S_dve)
                c["t_prev_sc"] = t_sc
                lring.free[lk] = [t_sc]
                c["prev_c"] = ct
                p["c"] = (ck, ct)
                p["t_sc"] = t_sc

            def back2(c):
                p = c["pend"]
                ck, ct = p["c"]
                wk, wt, _, wfree = wring.next()
                t_w = P.op("scalar", (lambda e, ct=ct, wt=wt: e.activation(
                    out=wt[:], in_=ct[:], func=AF.Exp, scale=-1.0)), [p["t_sc"]] + wfree, S_act)
                cring.free[ck] = [t_w]
                p["w"] = (wk, wt)
                p["t_w"] = t_w

            def back3(c):
                p = c["pend"]
                ek, et = p["e"]
                wk, wt = p["w"]
                ak, at, _, afree = aring.next()
                t_a = P.op("vector", (lambda e, et=et, wt=wt, at=at: e.tensor_tensor(
                    out=at[:], in0=et[:], in1=wt[:], op=ALU.mult)), [p["t_w"]] + afree, S_dve)
                ering.free[ek] = [t_a]
                wring.free[wk] = [t_a]
                p["a"] = (ak, at)
                p["t_a"] = t_a

            def back4(c):
                p = c["pend"]
                ak, at = p["a"]
                tb = tcnt[0] % 2
                tcnt[0] += 1
                t_tr = None
                for j in range(4):
                    t_tr = P.op("tensor", (lambda e, tb=tb, j=j, at=at: e.transpose(
                        out=psbs[tb][:, j * 128:(j + 1) * 128], in_=at[:, j * 128:(j + 1) * 128],
                        identity=ident_b[:])), [p["t_a"], tfree[tb]], S_pe if j == 3 else None)
                aring.free[ak] = [t_tr]
                p["tb"] = tb
                p["t_tr"] = t_tr

            def back5(c):
                p = c["pend"]
                tb = p["tb"]
                tk_, att, _, atfree = atring.next()
                t_at = P.op("scalar", (lambda e, tb=tb, att=att: e.activation(
                    out=att[:], in_=psbs[tb][:, 0:512].rearrange("p (j t) -> p j t", j=4),
                    func=AF.Copy)), [p["t_tr"]] + atfree, S_act)
                tfree[tb] = t_at
                p["at"] = (tk_, att)
                p["t_at"] = t_at

            def back6(c):
                p = c["pend"]
                tk_, att = p["at"]
                hd = head[c["hh"]]
                ob = c["ob"]
                ch = p["ch"]
                last = None
                for j in range(4):
                    n_av = c["n_av"]
                    last = P.op("tensor", (lambda e, ob=ob, j=j, att=att, hd=hd, ch=ch, n_av=n_av: e.matmul(
                        psf[ob][:, 0:128], lhsT=att[:, j, :], rhs=hd["vt"][:, ch * 4 + j, :],
                        start=(n_av == 0), stop=(n_av == c["nav_tot"] - 1))),
                        [p["t_at"], hd["t_v"], ofree.get(ob) if n_av == 0 else None],
                        S_pe if (j == 3) else None)
                    c["n_av"] += 1
                atring.free[tk_] = [last]
                c["last_pe"] = last

            loaded = set()
            last_of_head = {}
            for gi in range(0, len(items), G):
                grp = items[gi:gi + G]
                for (hh, qb) in grp:
                    if hh not in loaded:
                        load_head(hh)
                        loaded.add(hh)
                chains = [dict(hh=hh, qb=qb, zb=zbanks[i], ob=obanks[i], prev_c=None, t_prev_sc=t_z0, n_av=0,
                               ch0=(cfg.NC * qb * 128) // 512)
                          for i, (hh, qb) in enumerate(grp)]
                for c in chains:
                    c["nav_tot"] = (NCH - c["ch0"]) * 4
                for i in range(min(c["ch0"] for c in chains), NCH + 1):
                    fr_ = [c for c in chains if c["ch0"] <= i < NCH]
                    bk_ = [c for c in chains if c["ch0"] <= i - 1]
                    for c in bk_:
                        back1(c)
                    for c in fr_:
                        front(c, i)
                    for c in fr_:
                        front2(c, i)
                    for c in bk_:
                        back2(c)
                    for c in fr_:
                        front3(c, i)
                    for c in bk_:
                        back3(c)
                    for c in bk_:
                        back4(c)
                        back5(c)
                    for c in bk_:
                        back6(c)
                    for c in fr_:
                        front4(c, i)
                for c in chains:
                    ot, ofr = ostg.get()
                    ob = c["ob"]
                    t_o = P.op("vector", (lambda e, ot=ot, ob=ob: e.tensor_copy(out=ot[:], in_=psf[ob][:, 0:128])),
                               [c["last_pe"]] + ofr, S_dve)
                    ofree[ob] = t_o
                    ostg.store(o_d[c["qb"] * 128:(c["qb"] + 1) * 128, c["hh"] * 128:(c["hh"] + 1) * 128], ot[:], [t_o])
                    last_of_head[c["hh"]] = c["last_pe"]
                done_heads = [hh for hh in list(head.keys()) if all((hh, qb) in items[:gi + G] for qb in range(NTO))]
                for hh in done_heads:
                    hd = head.pop(hh)
                    tok = last_of_head[hh]
                    qring.free[hd["qk"]] = [tok]
                    kring.free[hd["kk"]] = [tok]
                    vring.free[hd["vk"]] = [tok]
            dout("o", o_d, BF16)
            if end_stage("s5"):
                return finish()

        for which in (() if os.environ.get('MK_SKIP6') else (0, 1)):
            with ExitStack() as st:
                actT = st.enter_context(nc.sbuf_tensor(f"s6act{which}", [128, GKC, TO], BF16))
                src = gm_d if which == 0 else o_d
                with ExitStack() as st2:
                    load_T(st2, src, GW, actT, [], f"s6l{which}")
                    if end_stage(f"s6l{which}"):
                        return finish()
                sgr = Ring(st, f"s6sg{which}", 3, [128, 512], BF16)
                t1r = Ring(st, f"s6t1{which}", 3, [128, 512], F32)
                stg_f = mk_stager(st, f"s6sf{which}", 3, [128, 512], F32)
                stg_b = mk_stager(st, f"s6sb{which}", 3, [128, 512], BF16)
                sg_src = sga_d if which == 0 else sgb_d

                def ep6(tt, c0, cw, ps, ready, which=which, sgr=sgr, t1r=t1r, stg_f=stg_f, stg_b=stg_b, sg_src=sg_src):
                    ep6m = int(os.environ.get("MK_EP6", "2"))
                    if ep6m == 0:
                        t, fr = stg_f.get()
                        tok = P.op("scalar", (lambda e, t=t, ps=ps, cw=cw: e.activation(
                            out=t[:, 0:cw], in_=ps[:, 0:cw], func=AF.Copy)), [ready] + fr, S_act)
                        stg_f.store(t1_d[tt * 128:(tt + 1) * 128, c0:c0 + cw], t[:, 0:cw], [tok])
                        return tok
                    gk, gt_, gs_, gfree = sgr.next()
                    t_g = P.dma("sync", gt_[:, 0:cw], sg_src[tt * 128:(tt + 1) * 128, c0:c0 + cw], gfree, gs_)
                    if ep6m == 1:
                        t, fr = stg_f.get()
                        tok = P.op("scalar", (lambda e, t=t, ps=ps, cw=cw: e.activation(
                            out=t[:, 0:cw], in_=ps[:, 0:cw], func=AF.Copy)), [ready, t_g] + fr, S_act)
                        sgr.free[gk] = [tok]
                        stg_f.store(t1_d[tt * 128:(tt + 1) * 128, c0:c0 + cw], t[:, 0:cw], [tok])
                        return tok
                    if which == 0:
                        t, fr = stg_f.get()
                        tok = P.op("vector", (lambda e, t=t, ps=ps, gt_=gt_, cw=cw: e.tensor_tensor(
                            out=t[:, 0:cw], in0=ps[:, 0:cw], in1=gt_[:, 0:cw], op=ALU.mult)), [ready, t_g] + fr, S_dve)
                        sgr.free[gk] = [tok]
                        stg_f.store(t1_d[tt * 128:(tt + 1) * 128, c0:c0 + cw], t[:, 0:cw], [tok])
                        return tok
                    k1, t1, s1, f1 = t1r.next()
                    t_1 = P.dma("sync", t1[:, 0:cw], t1_d[tt * 128:(tt + 1) * 128, c0:c0 + cw], f1, s1)
                    t, fr = stg_b.get()
                    tok_a = P.op("vector", (lambda e, ps=ps, gt_=gt_, cw=cw: e.tensor_tensor(
                        out=gt_[:, 0:cw], in0=ps[:, 0:cw], in1=gt_[:, 0:cw], op=ALU.mult)), [ready, t_g], S_dve)
                    tok_b = P.op("vector", (lambda e, t=t, t1=t1, gt_=gt_, cw=cw: e.tensor_tensor(
                        out=t[:, 0:cw], in0=gt_[:, 0:cw], in1=t1[:, 0:cw], op=ALU.add)), [t_1, tok_a] + fr, S_dve)
                    sgr.free[gk] = [tok_b]
                    t1r.free[k1] = [tok_b]
                    stg_b.store(mix_d[tt * 128:(tt + 1) * 128, c0:c0 + cw], t[:, 0:cw], [tok_b])
                    return tok_a

                wring = Ring(st, f"s6w{which}", 2, [128, GKC, 512], BF16)
                gemm(wring, actT, GKC, wpa if which == 0 else wpb, 0, 0, D, ep6)
                if which == 1:
                    dout("mix", mix_d, BF16)
                if end_stage(f"s6{which}"):
                    return finish()

        with ExitStack() as st:
            actT = st.enter_context(nc.sbuf_tensor("s8act", [128, KC, TO], BF16))
            with ExitStack() as st2:
                load_T(st2, (sga_d if os.environ.get("MK_S8SRC") else mix_d)[:, 0:int(os.environ.get("MK_LTK", D))], int(os.environ.get("MK_LTK", D)), actT, [], "s8l")
                if end_stage("s8l"):
                    return finish()
            stg_f = mk_stager(st, "s8sf", 3, [128, 512], F32)

            def ep8(tt, c0, cw, ps, ready):
                t, fr = stg_f.get()
                tok = P.op("scalar", (lambda e, t=t, ps=ps, cw=cw: e.activation(
                    out=t[:, 0:cw], in_=ps[:, 0:cw], func=AF.Copy)), [ready] + fr, S_act)
                stg_f.store(mo_d[tt * 128:(tt + 1) * 128, c0:c0 + cw], t[:, 0:cw], [tok])
                return tok

            wring = Ring(st, "s8w", 2, [128, KC, 512], BF16)
            gemm(wring, actT, KC, wo, 0, 0, D, ep8)
            dout("mo", mo_d, F32)
            if end_stage("s8"):
                return finish()

        def post_norm_residual(st, br_d, res_d, gsrc, i_gt, dst_d, tag):
            gtb, _ = bcast_load(st, tag + "gt", modrow_ap(i_gt), D)
            gb, t3 = bcast_load(st, tag + "g", gsrc[0:1, :], D)
            t_gg = P.op("vector", lambda e: e.tensor_tensor(out=gtb[:], in0=gtb[:], in1=gb[:], op=ALU.mult), [t3], S_dve)
            bring = Ring(st, tag + "b", 2, [128, D], F32)
            rring = Ring(st, tag + "r", 2, [128, D], F32)
            junk = st.enter_context(nc.sbuf_tensor(tag + "junk", [128, D], BF16))
            stt = st.enter_context(nc.sbuf_tensor(tag + "st", [128, 4 * NTO], F32))
            stg = mk_stager(st, tag + "o", 2, [128, D], F32)
            for tt in range(NTO):
                bk, bt, bs_, bfree = bring.next()
                t_b = P.dma("sync", bt[:], br_d[tt * 128:(tt + 1) * 128, :], bfree, bs_)
                rk, rt, rs_, rfree = rring.next()
                t_r = P.dma("sync", rt[:], res_d[tt * 128:(tt + 1) * 128, :], rfree, rs_)
                sc = stt[:, tt * 4:(tt + 1) * 4]
                t_q = P.op("scalar", (lambda e, bt=bt, sc=sc: e.activation(
                    out=junk[:], in_=bt[:], func=AF.Square, accum_out=sc[:, 0:1])), [t_b], S_act)
                t_sd = P.op("scalar", (lambda e, sc=sc: e.activation(
                    out=sc[:, 1:2], in_=sc[:, 0:1], func=AF.Sqrt, scale=1.0 / D, bias=eps_t[:, 0:1])), [t_q], S_act)
                t_iv = P.op("vector", (lambda e, sc=sc: e.reciprocal(out=sc[:, 2:3], in_=sc[:, 1:2])), [t_sd], S_dve)
                t_m = P.op("vector", (lambda e, bt=bt, sc=sc: e.scalar_tensor_tensor(
                    out=bt[:], in0=bt[:], scalar=sc[:, 2:3], in1=gtb[:], op0=ALU.mult, op1=ALU.mult)),
                    [t_gg, t_iv], S_dve)
                ot, ofr = stg.get()
                tok = P.op("vector", (lambda e, bt=bt, rt=rt, ot=ot: e.tensor_tensor(
                    out=ot[:], in0=bt[:], in1=rt[:], op=ALU.add)), [t_r, t_m] + ofr, S_dve)
                bring.free[bk] = [tok]
                rring.free[rk] = [tok]
                stg.store(dst_d[tt * 128:(tt + 1) * 128, :], ot[:], [tok])

        with ExitStack() as st:
            post_norm_residual(st, mo_d, xo, gpost, 2, x1_d, "s9")
            dout("x1", x1_d, F32)
            if end_stage("s9"):
                return finish()

        with ExitStack() as st:
            h2T = st.enter_context(nc.sbuf_tensor("h2T", [128, KC, TO], BF16))
            with ExitStack() as st2:
                mb, shb, t_m = mod_tiles(st2, gpre2, 4, 3, "s10")

                def emit10(tt, kc0, nk, ps, ready):
                    return P.op("scalar", (lambda e, tt=tt, kc0=kc0, nk=nk, ps=ps: e.activation(
                        out=h2T[:, kc0:kc0 + nk, tt * 128:(tt + 1) * 128],
                        in_=ps[:, 0:nk * 128].rearrange("p (k t) -> p k t", k=nk), func=AF.Copy)), [ready], S_act)

                P.op("vector", None, [t_m])
                frontend(st2, x1_d, NTO, mb, shb, emit10, "s10")
                if end_stage("s10a"):
                    return finish()
            rr = Ring(st, "s10r", 3, [128, 512], F32)
            stg_b = mk_stager(st, "s10sb", 3, [128, 512], BF16)

            def ep10(tt, c0, cw, ps, ready):
                rk, rt, _, rfree = rr.next()
                tok = P.op("scalar", (lambda e, rt=rt, ps=ps, cw=cw: e.activation(
                    out=rt[:, 0:cw], in_=ps[:, 0:cw], func=AF.Relu)), [ready] + rfree, S_act)
                t, fr = stg_b.get()
                tok2 = P.op("vector", (lambda e, t=t, rt=rt, cw=cw: e.tensor_tensor(
                    out=t[:, 0:cw], in0=rt[:, 0:cw], in1=rt[:, 0:cw], op=ALU.mult)), [tok] + fr, S_dve)
                rr.free[rk] = [tok2]
                stg_b.store(f_d[tt * 128:(tt + 1) * 128, c0:c0 + cw], t[:, 0:cw], [tok2])
                return tok

            wring = Ring(st, "s10w", 2, [128, KC, 512], BF16)
            gemm(wring, h2T, KC, wf1, 0, 0, DFF, ep10)
            dout("f", f_d, BF16)
            if end_stage("s10"):
                return finish()

        for kb in range(DFF // D):
            with ExitStack() as st:
                actT = st.enter_context(nc.sbuf_tensor(f"s11act{kb}", [128, KC, TO], BF16))
                with ExitStack() as st2:
                    load_T(st2, f_d[:, kb * D:(kb + 1) * D], D, actT, [], f"s11l{kb}")
                    if end_stage(f"s11l{kb}"):
                        return finish()
                ar = Ring(st, f"s11a{kb}", 3, [128, 512], F32)
                stg_f = mk_stager(st, f"s11sf{kb}", 3, [128, 512], F32)

                def ep11(tt, c0, cw, ps, ready, kb=kb, ar=ar, stg_f=stg_f):
                    t, fr = stg_f.get()
                    if kb == 0:
                        tok = P.op("scalar", (lambda e, t=t, ps=ps, cw=cw: e.activation(
                            out=t[:, 0:cw], in_=ps[:, 0:cw], func=AF.Copy)), [ready] + fr, S_act)
                    else:
                        ak, at, as_, afree = ar.next()
                        t_a = P.dma("sync", at[:, 0:cw], acc_d[tt * 128:(tt + 1) * 128, c0:c0 + cw], afree, as_)
                        tok = P.op("vector", (lambda e, t=t, ps=ps, at=at, cw=cw: e.tensor_tensor(
                            out=t[:, 0:cw], in0=ps[:, 0:cw], in1=at[:, 0:cw], op=ALU.add)), [ready, t_a] + fr, S_dve)
                        ar.free[ak] = [tok]
                    stg_f.store(acc_d[tt * 128:(tt + 1) * 128, c0:c0 + cw], t[:, 0:cw], [tok])
                    return tok

                wring = Ring(st, f"s11w{kb}", 2, [128, KC, 512], BF16)
                gemm(wring, actT, KC, wf2, kb * D, 0, D, ep11)
                if end_stage(f"s11{kb}"):
                    return finish()

        with ExitStack() as st:
            post_norm_residual(st, acc_d, x1_d, gpost2, 5, y, "s12")
            end_stage("s12")
        return finish()


def own_rows(cfg, cid):
    blk = cfg.NC * np.arange(cfg.NTO) + cid
    return (blk[:, None] * 128 + np.arange(128)[None, :]).reshape(-1)


def prep_inputs(cfg, x, c, w_ada, b_ada, g_pre_mix, w_in, g_v, w_s, b_s, w_proj_a, w_proj_b, w_o,
                g_post_mix, g_pre_mlp, w_ff1, w_ff2, g_post_mlp):
    f = lambda a: np.ascontiguousarray(np.asarray(a, dtype=np.float32))
    xr = f(np.asarray(x)[0, ::-1, :])
    common = {
        "xr": xr,
        "cv": f(np.asarray(c)[0].reshape(cfg.KC, 128).T),
        "wada": f(np.asarray(w_ada)[0]),
        "bada": f(np.asarray(b_ada)[0][None, :]),
        "win": f(np.asarray(w_in)[0]),
        "gpre": f(np.asarray(g_pre_mix)[0][None, :]),
        "gv": f(np.asarray(g_v)[0][None, :]),
        "wsT": f(np.asarray(w_s)[0][:, ::-1, ::-1].transpose(2, 0, 1)),
        "bsr": f(np.asarray(b_s)[0][:, ::-1].T),
        "wpa": f(np.asarray(w_proj_a)[0]),
        "wpb": f(np.asarray(w_proj_b)[0]),
        "wo": f(np.asarray(w_o)[0]),
        "gpost": f(np.asarray(g_post_mix)[0][None, :]),
        "gpre2": f(np.asarray(g_pre_mlp)[0][None, :]),
        "wf1": f(np.asarray(w_ff1)[0]),
        "wf2": f(np.asarray(w_ff2)[0]),
        "gpost2": f(np.asarray(g_post_mlp)[0][None, :]),
        "identf": np.eye(128, dtype=np.float32),
        "kpos": np.arange(cfg.S, dtype=np.float32)[None, :],
    }
    in_maps = []
    for cid in range(cfg.NC):
        m = dict(common)
        rows = own_rows(cfg, cid)
        m["xo"] = f(xr[rows])
        m["qpos"] = f(rows.astype(np.float32).reshape(cfg.NTO, 128).T)
        in_maps.append(m)
    return in_maps


_CACHE = {}


def run(cfg, inputs, stop_after=None, debug=False, trace=False):
    key = (cfg.D, cfg.S, cfg.NC, stop_after, debug)
    if key not in _CACHE:
        _CACHE[key] = build_nc(cfg, stop_after, debug)
    nc = _CACHE[key]
    in_maps = prep_inputs(cfg, **inputs)
    return run_bass_kernel_spmd(nc, in_maps, core_ids=list(range(cfg.NC)), trace=trace)


def kernel(**inputs):
    cfg = Cfg()
    res = run(cfg, inputs)
    return assemble(cfg, [np.asarray(r["y"]) for r in res.results])


def assemble(cfg, ys):
    yr = np.empty((cfg.S, cfg.D), np.float32)
    for cid, yc in enumerate(ys):
        yr[own_rows(cfg, cid)] = yc
    return np.ascontiguousarray(yr[::-1])[None].astype(np.float32)
```

```python
import os
from contextlib import ExitStack
import numpy as np
import concourse.bass as bass
import concourse.mybir as mybir
from concourse.bass_utils import run_bass_kernel_spmd

F32 = mybir.dt.float32
BF16 = mybir.dt.bfloat16
AF = mybir.ActivationFunctionType
ALU = mybir.AluOpType
EPS = 1e-6
ENGS = ("sync", "scalar", "gpsimd", "vector", "tensor")


class Cfg:
    def __init__(self, D=4096, S=8192, NC=8):
        self.D, self.S, self.NC = D, S, NC
        self.KC = D // 128
        self.TO = S // NC
        self.NTO = self.TO // 128
        self.NTA = S // 128
        self.GW = D // 2
        self.NH = self.GW // 128
        self.DFF = 4 * D
        self.NMOD = 6 * D
        self.INC = 2 * self.GW + 3 * self.GW + 2 * D


class CS:
    def __init__(self, h):
        self.h = h
        self.n = 0


class Prog:
    def __init__(self, nc):
        self.nc = nc
        self.q = {e: [] for e in ENGS}
        self.waited = {e: {} for e in ENGS}
        self.nblk = 0

    def op(self, eng, fn, waits=(), sig=None, amt=1):
        tok = None
        if sig is not None:
            sig.n += amt
            tok = (sig, sig.n)
        ws = []
        seen = self.waited[eng]
        stack = list(waits)
        while stack:
            w = stack.pop()
            if w is None:
                continue
            if isinstance(w, list):
                stack.extend(w)
                continue
            s, v = w
            if seen.get(id(s), 0) >= v:
                continue
            seen[id(s)] = v
            ws.append((s, v))
        self.q[eng].append((fn, ws, sig, amt))
        return tok

    def dma(self, eng, out, in_, waits=(), sig=None, **kw):
        return self.op(eng, lambda e: e.dma_start(out=out, in_=in_, **kw), waits, sig, 16)

    def flush(self, name=None):
        self.nblk += 1
        name = name or f"blk{self.nblk}"
        with self.nc.Block(name) as block:
            for eng in ENGS:
                items = self.q[eng]
                if not items:
                    continue

                def body(e, items=items):
                    for fn, ws, sig, amt in items:
                        for (s, v) in ws:
                            e.wait_ge(s.h, v)
                        if fn is None:
                            continue
                        ins = fn(e)
                        if sig is not None:
                            ins.then_inc(sig.h, amt)

                getattr(block, eng)(body)
        self.q = {e: [] for e in ENGS}


def build_nc(cfg, stop_after=None, debug=False):
    D, S, KC, TO, NTO, NTA, GW, NH, DFF = cfg.D, cfg.S, cfg.KC, cfg.TO, cfg.NTO, cfg.NTA, cfg.GW, cfg.NH, cfg.DFF
    GKC = GW // 128
    nc = bass.Bass("TRN2", target_bir_lowering=False)
    P = Prog(nc)

    def din(name, shape, dt=F32):
        return nc.dram_tensor(name, list(shape), dt, kind="ExternalInput").ap()

    def dscr(name, shape, dt):
        return nc.dram_tensor(name, list(shape), dt).ap()

    xr = din("xr", [S, D])
    xo = din("xo", [TO, D])
    cv = din("cv", [128, KC])
    wada = din("wada", [D, cfg.NMOD])
    bada = din("bada", [1, cfg.NMOD])
    win = din("win", [D, cfg.INC])
    gpre = din("gpre", [1, D])
    gv = din("gv", [1, GW])
    wsT = din("wsT", [128, NH, 128])
    bsr = din("bsr", [128, NH])
    wpa = din("wpa", [GW, D])
    wpb = din("wpb", [GW, D])
    wo = din("wo", [D, D])
    gpost = din("gpost", [1, D])
    gpre2 = din("gpre2", [1, D])
    wf1 = din("wf1", [D, DFF])
    wf2 = din("wf2", [DFF, D])
    gpost2 = din("gpost2", [1, D])
    identf = din("identf", [128, 128])
    kpos = din("kpos", [1, S])
    qpos = din("qpos", [128, NTO])
    y = nc.dram_tensor("y", [TO, D], F32, kind="ExternalOutput").ap()

    dbg = {}

    def dout(name, src, dt=F32):
        if not debug:
            return
        a = nc.dram_tensor("dbg_" + name, list(src.shape), dt, kind="ExternalOutput").ap()
        dbg[name] = (a, src)

    mod_d = dscr("mod_d", [1, cfg.NMOD], F32)
    hT_d = dscr("hT_d", [NTA // 4, 128, KC, 512], BF16)
    kT_d = dscr("kT_d", [NH, 128, S], BF16)
    v_d = dscr("v_d", [S, GW], BF16)
    qT_d = dscr("qT_d", [NH, 128, TO], BF16)
    u_d = dscr("u_d", [TO, GW], BF16)
    vr_d = dscr("vr_d", [TO, GW], F32)
    sga_d = dscr("sga_d", [TO, D], BF16)
    sgb_d = dscr("sgb_d", [TO, D], BF16)
    gm_d = dscr("gm_d", [TO, GW], BF16)
    o_d = dscr("o_d", [TO, GW], BF16)
    t1_d = dscr("t1_d", [TO, D], F32)
    mix_d = dscr("mix_d", [TO, D], BF16)
    mo_d = dscr("mo_d", [TO, D], F32)
    x1_d = dscr("x1_d", [TO, D], F32)
    f_d = dscr("f_d", [TO, DFF], BF16)
    acc_d = dscr("acc_d", [TO, D], F32)

    with ExitStack() as es:
        def sem(name):
            return CS(es.enter_context(nc.semaphore(name)))

        S_pe, S_act, S_dve = sem("s_pe"), sem("s_act"), sem("s_dve")
        S_pool = sem("s_pool")
        S_ld = sem("s_ld")
        S_st = sem("s_st")
        ring_sems = [sem(f"rs{i}") for i in range(24)]
        psf = [es.enter_context(nc.psum_tensor(f"psf{i}", [128, 512], F32)) for i in range(6)]
        psbs = [es.enter_context(nc.psum_tensor(f"psb{i}", [128, 1024], BF16)) for i in range(2)]
        ident_f = es.enter_context(nc.sbuf_tensor("ident_f", [128, 128], F32))
        ident_b = es.enter_context(nc.sbuf_tensor("ident_b", [128, 128], BF16))
        eps_t = es.enter_context(nc.sbuf_tensor("eps_t", [128, 1], F32))

        state = {"rs": 0}
        PSF = {}
        PI = [0]

        def next_bank():
            b = 2 + PI[0] % 4
            PI[0] += 1
            return b

        def new_ring_sem():
            assert state["rs"] < len(ring_sems), "out of ring semaphores in this stage"
            s_ = ring_sems[state["rs"]]
            state["rs"] += 1
            return s_

        stagers = []

        def wait_stagers():
            toks = []
            for s_ in stagers:
                for fr in s_.ring.free:
                    toks.extend(fr)
            P.op("sync", None, toks)
            stagers.clear()

        def end_stage(name):
            wait_stagers()
            P.op("sync", None, [(S_st, S_st.n), (S_ld, S_ld.n)])
            P.flush(name)
            state["rs"] = 0
            if os.environ.get("MK_VERBOSE"):
                print("stage", name, "sem counts pe/act/dve/ld/st", S_pe.n, S_act.n, S_dve.n, S_ld.n, S_st.n,
                      "ring max", max(r.n for r in ring_sems), flush=True)
            return stop_after == name

        def finish():
            for name, (a, src) in dbg.items():
                P.dma("sync", a, src, [], S_st)
            P.op("sync", None, [(S_st, S_st.n)])
            P.flush("fin")
            return nc

        class Ring:
            def __init__(self, st, name, n, shape, dt, sems=True):
                self.bufs = [st.enter_context(nc.sbuf_tensor(f"{name}{i}", list(shape), dt)) for i in range(n)]
                self.sems = [new_ring_sem() if sems else None for _ in range(n)]
                self.free = [[] for _ in range(n)]
                self.i = 0

            def next(self):
                k = self.i % len(self.bufs)
                self.i += 1
                fr = self.free[k]
                self.free[k] = []
                return k, self.bufs[k], self.sems[k], fr

        class Stager:
            def __init__(self, st, name, n, shape, dt):
                self.ring = Ring(st, name, n, shape, dt)

            def get(self):
                k, t, s_, fr = self.ring.next()
                self.k, self.s = k, s_
                return t, fr

            def store(self, dst, src, waits, eng="sync"):
                tok = P.dma(eng, dst, src, waits, self.s)
                self.ring.free[self.k] = [tok]
                return tok

        def mk_stager(st, name, n, shape, dt):
            s_ = Stager(st, name, n, shape, dt)
            stagers.append(s_)
            return s_

        def bcast_load(st, name, src_row, n, dt=F32, waits=()):
            t = st.enter_context(nc.sbuf_tensor(name, [128, n], dt))
            tok = P.dma("sync", t[:], src_row.partition_broadcast(128), list(waits), S_ld)
            return t, tok

        def frontend(st, src_d, ntiles, mb, shb, emit_hT, tag):
            xring = Ring(st, tag + "x", 2, [128, D], F32)
            hring = Ring(st, tag + "h", 2, [128, D], F32)
            junk = st.enter_context(nc.sbuf_tensor(tag + "junk", [128, D], BF16))
            ssq = st.enter_context(nc.sbuf_tensor(tag + "ssq", [128, ntiles], F32))
            std = st.enter_context(nc.sbuf_tensor(tag + "std", [128, ntiles], F32))
            inv = st.enter_context(nc.sbuf_tensor(tag + "inv", [128, ntiles], F32))
            psfree = [None, None]
            pi = 0
            for tt in range(ntiles):
                k, xt, xs, xfree = xring.next()
                t_ld = P.dma("sync", xt[:], src_d[tt * 128:(tt + 1) * 128, :], xfree, xs)
                t_sq = P.op("scalar", (lambda e, xt=xt, tt=tt: e.activation(
                    out=junk[:], in_=xt[:], func=AF.Square, accum_out=ssq[:, tt:tt + 1])), [t_ld], S_act)
                t_sd = P.op("scalar", (lambda e, tt=tt: e.activation(
                    out=std[:, tt:tt + 1], in_=ssq[:, tt:tt + 1], func=AF.Sqrt, scale=1.0 / D, bias=eps_t[:, 0:1])),
                    [t_sq], S_act)
                t_iv = P.op("vector", (lambda e, tt=tt: e.reciprocal(out=inv[:, tt:tt + 1], in_=std[:, tt:tt + 1])),
                            [t_sd], S_dve)
                hk, ht, _, hfree = hring.next()
                t_h0 = P.op("vector", (lambda e, xt=xt, ht=ht, tt=tt: e.scalar_tensor_tensor(
                    out=ht[:], in0=xt[:], scalar=inv[:, tt:tt + 1], in1=mb[:], op0=ALU.mult, op1=ALU.mult)),
                    [t_ld, t_iv] + hfree, S_dve)
                t_h = P.op("vector", (lambda e, ht=ht: e.tensor_tensor(out=ht[:], in0=ht[:], in1=shb[:], op=ALU.add)),
                           [t_h0], S_dve)
                xring.free[k] = [t_h, t_sq]
                t_last = None
                for kc0 in range(0, KC, 4):
                    nk = min(4, KC - kc0)
                    b = pi % 2
                    pi += 1
                    for j in range(nk):
                        t_last = P.op("tensor", (lambda e, b=b, j=j, ht=ht, kc=kc0 + j: e.transpose(
                            out=psf[b][:, j * 128:(j + 1) * 128], in_=ht[:, kc * 128:(kc + 1) * 128], identity=ident_f[:])),
                            [t_h, psfree[b]], S_pe if j == nk - 1 else None)
                    psfree[b] = emit_hT(tt, kc0, nk, psf[b], t_last)
                hring.free[hk] = [t_last]

        def load_T(st, src_d, K, actT, waits, tag):
            ring = Ring(st, tag + "ld", 2, [128, K], BF16)
            half_free = [None, None]
            hi = 0
            for tt in range(NTO):
                k, t, s_, fr = ring.next()
                t_ld = P.dma("sync", t[:], src_d[tt * 128:(tt + 1) * 128, :], fr + list(waits), s_)
                t_last = None
                for kc0 in range(0, K // 128, 4):
                    nk = min(4, K // 128 - kc0)
                    b = hi % 2
                    hi += 1
                    for j in range(nk):
                        t_last = P.op("tensor", (lambda e, b=b, j=j, t=t, kc=kc0 + j: e.transpose(
                            out=psbs[b][:, j * 128:(j + 1) * 128], in_=t[:, kc * 128:(kc + 1) * 128],
                            identity=ident_b[:])), [t_ld, half_free[b]], S_pe if j == nk - 1 else None)
                    half_free[b] = P.op("scalar", (lambda e, b=b, nk=nk, kc0=kc0, tt=tt: e.activation(
                        out=actT[:, kc0:kc0 + nk, tt * 128:(tt + 1) * 128],
                        in_=psbs[b][:, 0:nk * 128].rearrange("p (k t) -> p k t", k=nk), func=AF.Copy)),
                        [t_last], S_act)
                ring.free[k] = [t_last]

        def gemm(wring, actT, kcn, w_d, row0, col0, ncols, epilogue, cgw=512):
            wv = w_d[row0:row0 + kcn * 128, :].rearrange("(kc p) n -> p kc n", p=128)
            psfree = PSF
            for c0 in range(0, ncols, cgw):
                cw = min(cgw, ncols - c0)
                k, wt, ws_, wfree = wring.next()
                t_w = P.dma("gpsimd", wt[:, :, 0:cw], wv[:, :, col0 + c0:col0 + c0 + cw], wfree, ws_)
                t_last = None
                for tt in range(NTO):
                    b = next_bank()
                    for kc in range(kcn):
                        t_last = P.op("tensor", (lambda e, b=b, kc=kc, wt=wt, cw=cw, tt=tt: e.matmul(
                            psf[b][:, 0:cw], lhsT=actT[:, kc, tt * 128:(tt + 1) * 128], rhs=wt[:, kc, 0:cw],
                            start=(kc == 0), stop=(kc == kcn - 1))),
                            [t_w, psfree.get(b)], S_pe if kc == kcn - 1 else None)
                    psfree[b] = epilogue(tt, c0, cw, psf[b], t_last)
                wring.free[k] = [t_last]

        with ExitStack() as st:
            cvt = st.enter_context(nc.sbuf_tensor("cvt", [128, KC], F32))
            s_bf = st.enter_context(nc.sbuf_tensor("s_bf", [128, KC], BF16))
            P.dma("sync", ident_f[:], identf[:, :], [], S_ld)
            t_ld0 = P.dma("sync", cvt[:], cv[:, :], [], S_ld)
            P.op("vector", lambda e: e.tensor_copy(out=ident_b[:], in_=ident_f[:]), [t_ld0], S_dve)
            P.op("vector", lambda e: e.memset(eps_t[:], EPS), [], S_dve)
            t_s = P.op("scalar", lambda e: e.activation(out=s_bf[:], in_=cvt[:], func=AF.Silu), [t_ld0], S_act)
            wring = Ring(st, "wada", 2, [128, KC, 512], BF16)
            bring = Ring(st, "bada", 2, [1, 512], F32)
            mstg = mk_stager(st, "modst", 2, [1, 512], F32)
            wv = wada.rearrange("(kc p) n -> p kc n", p=128)
            psfree = [None, None]
            for cg in range(cfg.NMOD // 512):
                k, wt, ws_, wfree = wring.next()
                t_w = P.dma("gpsimd", wt[:], wv[:, :, cg * 512:(cg + 1) * 512], wfree, ws_)
                bk, bt, bs_, bfree = bring.next()
                t_b = P.dma("sync", bt[:], bada[0:1, cg * 512:(cg + 1) * 512], bfree, bs_)
                b = cg % 2
                tk = None
                for kc in range(KC):
                    tk = P.op("tensor", (lambda e, b=b, kc=kc, wt=wt: e.matmul(
                        psf[b][0:1, :], lhsT=s_bf[:, kc:kc + 1], rhs=wt[:, kc, :],
                        start=(kc == 0), stop=(kc == KC - 1))), [t_w, t_s, psfree[b]], S_pe if kc == KC - 1 else None)
                wring.free[k] = [tk]
                mt, mfr = mstg.get()
                t_ev = P.op("vector", (lambda e, b=b, mt=mt, bt=bt: e.tensor_tensor(
                    out=mt[0:1, :], in0=psf[b][0:1, :], in1=bt[0:1, :], op=ALU.add)), [tk, t_b] + mfr, S_dve)
                psfree[b] = t_ev
                bring.free[bk] = [t_ev]
                mstg.store(mod_d[0:1, cg * 512:(cg + 1) * 512], mt[0:1, :], [t_ev])
            dout("mod", mod_d)
            if end_stage("s0"):
                return finish()

        def modrow_ap(i):
            return mod_d[0:1, i * D:(i + 1) * D]

        def mod_tiles(st, gsrc, i_sc, i_sh, tag):
            mb, _ = bcast_load(st, tag + "mb", modrow_ap(i_sc), D)
            shb, _ = bcast_load(st, tag + "shb", modrow_ap(i_sh), D)
            gb, t3 = bcast_load(st, tag + "gb", gsrc[0:1, :], D)
            t_m = P.op("vector", lambda e: e.scalar_tensor_tensor(
                out=mb[:], in0=mb[:], scalar=1.0, in1=gb[:], op0=ALU.add, op1=ALU.mult), [t3], S_dve)
            return mb, shb, t_m

        with ExitStack() as st:
            mb, shb, t_m = mod_tiles(st, gpre, 1, 0, "s1")
            stg = mk_stager(st, "s1stg", 2, [128, KC, 512], BF16)
            cur = {}

            def emit1(tt, kc0, nk, ps, ready):
                if tt % 4 == 0 and kc0 == 0:
                    cur["t"], cur["fr"] = stg.get()
                t = cur["t"]
                tok = P.op("scalar", (lambda e, t=t, tt=tt, kc0=kc0, nk=nk, ps=ps: e.activation(
                    out=t[:, kc0:kc0 + nk, (tt % 4) * 128:(tt % 4 + 1) * 128],
                    in_=ps[:, 0:nk * 128].rearrange("p (k t) -> p k t", k=nk), func=AF.Copy)),
                    [ready] + cur["fr"], S_act)
                cur["fr"] = []
                if tt % 4 == 3 and kc0 + nk == KC:
                    stg.store(hT_d[tt // 4], t[:], [tok])
                return tok

            P.op("vector", None, [t_m])
            frontend(st, xr, NTA, mb, shb, emit1, "s1")
            dout("hT", hT_d, BF16)
            if end_stage("s1"):
                return finish()

        with ExitStack() as st:
            hring = Ring(st, "s2h", 3, [128, KC, 512], BF16)
            wring = Ring(st, "s2w", 2, [128, KC, 512], BF16)
            stg = mk_stager(st, "s2stg", 3, [128, 512], BF16)
            wv = win.rearrange("(kc p) n -> p kc n", p=128)
            kcol0 = 2 * GW + GW
            vcol0 = 2 * GW + 2 * GW
            psfree = PSF
            for isv in (0, 1):
                for c0 in range(0, GW, 512):
                    cw = min(512, GW - c0)
                    wk, wt, ws_, wfree = wring.next()
                    cbase = (vcol0 if isv else kcol0) + c0
                    t_w = P.dma("gpsimd", wt[:, :, 0:cw], wv[:, :, cbase:cbase + cw], wfree, ws_)
                    t_last = None
                    for tg in range(NTA // 4):
                        hk, ht, hs_, hfree = hring.next()
                        t_h = P.dma("sync", ht[:], hT_d[tg], hfree, hs_)
                        if not isv:
                            for hh in range(cw // 128):
                                b = next_bank()
                                for kc in range(KC):
                                    t_last = P.op("tensor", (lambda e, b=b, kc=kc, wt=wt, ht=ht, hh=hh: e.matmul(
                                        psf[b][:, :], lhsT=wt[:, kc, hh * 128:(hh + 1) * 128], rhs=ht[:, kc, :],
                                        start=(kc == 0), stop=(kc == KC - 1))), [t_w, t_h, psfree.get(b)],
                                        S_pe if kc == KC - 1 else None)
                                sg_t, fr = stg.get()
                                tok = P.op("scalar", (lambda e, sg_t=sg_t, b=b: e.activation(
                                    out=sg_t[:], in_=psf[b][:, :], func=AF.Copy)), [t_last] + fr, S_act)
                                psfree[b] = tok
                                head = (c0 // 128) + hh
                                stg.store(kT_d[head][:, tg * 512:(tg + 1) * 512], sg_t[:], [tok], eng="scalar")
                        else:
                            for t4 in range(4):
                                b = next_bank()
                                for kc in range(KC):
                                    t_last = P.op("tensor", (lambda e, b=b, kc=kc, wt=wt, ht=ht, t4=t4, cw=cw: e.matmul(
                                        psf[b][:, 0:cw], lhsT=ht[:, kc, t4 * 128:(t4 + 1) * 128], rhs=wt[:, kc, 0:cw],
                                        start=(kc == 0), stop=(kc == KC - 1))), [t_w, t_h, psfree.get(b)],
                                        S_pe if kc == KC - 1 else None)
                                sg_t, fr = stg.get()
                                tok = P.op("vector", (lambda e, sg_t=sg_t, b=b, cw=cw: e.tensor_copy(
                                    out=sg_t[:, 0:cw], in_=psf[b][:, 0:cw])), [t_last] + fr, S_dve)
                                psfree[b] = tok
                                r0 = tg * 512 + t4 * 128
                                stg.store(v_d[r0:r0 + 128, c0:c0 + cw], sg_t[:, 0:cw], [tok], eng="scalar")
                        hring.free[hk] = [t_last]
                    wring.free[wk] = [t_last]
            dout("kT", kT_d, BF16)
            dout("v", v_d, BF16)
            if end_stage("s2"):
                return finish()

        with ExitStack() as st:
            hT_own = st.enter_context(nc.sbuf_tensor("hT_own", [128, KC, TO], BF16))
            with ExitStack() as st2:
                mb, shb, t_m = mod_tiles(st2, gpre, 1, 0, "s3")

                def emit3(tt, kc0, nk, ps, ready):
                    return P.op("scalar", (lambda e, tt=tt, kc0=kc0, nk=nk, ps=ps: e.activation(
                        out=hT_own[:, kc0:kc0 + nk, tt * 128:(tt + 1) * 128],
                        in_=ps[:, 0:nk * 128].rearrange("p (k t) -> p k t", k=nk), func=AF.Copy)), [ready], S_act)

                P.op("vector", None, [t_m])
                frontend(st2, xo, NTO, mb, shb, emit3, "s3")
                if end_stage("s3a"):
                    return finish()
            stg_b = mk_stager(st, "s3sb", 3, [128, 512], BF16)
            stg_f = mk_stager(st, "s3sf", 3, [128, 512], F32)

            def ep_act(func, dst_d, stg):
                def ep(tt, c0, cw, ps, ready):
                    t, fr = stg.get()
                    tok = P.op("scalar", (lambda e, t=t, ps=ps, cw=cw: e.activation(
                        out=t[:, 0:cw], in_=ps[:, 0:cw], func=func)), [ready] + fr, S_act)
                    stg.store(dst_d[tt * 128:(tt + 1) * 128, c0:c0 + cw], t[:, 0:cw], [tok])
                    return tok
                return ep

            wring = Ring(st, "s3w", 2, [128, KC, 512], BF16)
            s3n = int(os.environ.get("MK_S3N", "5"))
            if s3n >= 1 and not os.environ.get("MK_SKIPU"):
                gemm(wring, hT_own, KC, win, 0, 0, GW, ep_act(AF.Gelu, u_d, stg_b))
            if s3n >= 2:
                gemm(wring, hT_own, KC, win, 0, GW, GW, ep_act(AF.Gelu, vr_d, stg_f))
            if s3n >= 3:
                gemm(wring, hT_own, KC, win, 0, 2 * GW + 3 * GW, D, ep_act(AF.Sigmoid, sga_d, stg_b))
            if s3n >= 4:
                gemm(wring, hT_own, KC, win, 0, 2 * GW + 3 * GW + D, D, ep_act(AF.Sigmoid, sgb_d, stg_b))
            wv = win.rearrange("(kc p) n -> p kc n", p=128)
            psfree = PSF
            for hh in range(NH if s3n >= 5 else 0):
                wk, wt, ws_, wfree = wring.next()
                cb = 2 * GW + hh * 128
                t_w = P.dma("gpsimd", wt[:, :, 0:128], wv[:, :, cb:cb + 128], wfree, ws_)
                t_last = None
                for t0 in range(0, TO, 512):
                    tw = min(512, TO - t0)
                    b = next_bank()
                    for kc in range(KC):
                        t_last = P.op("tensor", (lambda e, b=b, kc=kc, wt=wt, t0=t0, tw=tw: e.matmul(
                            psf[b][:, 0:tw], lhsT=wt[:, kc, 0:128], rhs=hT_own[:, kc, t0:t0 + tw],
                            start=(kc == 0), stop=(kc == KC - 1))), [t_w, psfree.get(b)],
                            S_pe if kc == KC - 1 else None)
                    t, fr = stg_b.get()
                    tok = P.op("scalar", (lambda e, t=t, b=b, tw=tw: e.activation(
                        out=t[:, 0:tw], in_=psf[b][:, 0:tw], func=AF.Copy, scale=float(1.0 / np.sqrt(128.0)))),
                        [t_last] + fr, S_act)
                    psfree[b] = tok
                    stg_b.store(qT_d[hh][:, t0:t0 + tw], t[:, 0:tw], [tok])
                wring.free[wk] = [t_last]
            dout("u", u_d, BF16)
            dout("vr", vr_d, F32)
            dout("qT", qT_d, BF16)
            dout("sga", sga_d, BF16)
            if end_stage("s3"):
                return finish()

        with ExitStack() as st:
            gvb, t_g = bcast_load(st, "s4gv", gv[0:1, :], GW)
            wst = st.enter_context(nc.sbuf_tensor("s4ws", [128, NH, 128], F32))
            wsb = st.enter_context(nc.sbuf_tensor("s4wsb", [128, NH, 128], BF16))
            bst = st.enter_context(nc.sbuf_tensor("s4bs", [128, NH], F32))
            P.dma("sync", wst[:], wsT[:, :, :], [], S_ld)
            t_b = P.dma("sync", bst[:], bsr[:, :], [], S_ld)
            t_g = t_b
            t_ms = P.op("vector", lambda e: e.memset(wst[0:64, :, 64:128], 0.0), [t_b], S_dve)
            t_wb = P.op("vector", lambda e: e.tensor_copy(out=wsb[:], in_=wst[:]), [t_ms], S_dve)
            vring = Ring(st, "s4v", 2, [128, GW], F32)
            uring = Ring(st, "s4u", 2, [128, GW], BF16)
            vnr = Ring(st, "s4vn", 2, [128, GW], BF16)
            stg = mk_stager(st, "s4stg", 2, [128, GW], BF16)
            stats = st.enter_context(nc.sbuf_tensor("s4stats", [128, 8 * NTO], F32))
            junk = st.enter_context(nc.sbuf_tensor("s4junk", [128, GW], BF16))
            psfree = PSF
            for tt in range(NTO):
                vk, vt, vs_, vfree = vring.next()
                t_v = P.dma("sync", vt[:], vr_d[tt * 128:(tt + 1) * 128, :], vfree, vs_)
                uk, ut, us_, ufree = uring.next()
                t_u = P.dma("sync", ut[:], u_d[tt * 128:(tt + 1) * 128, :], ufree, us_)
                sc = stats[:, tt * 8:(tt + 1) * 8]
                t_a1 = P.op("scalar", (lambda e, vt=vt, sc=sc: e.activation(
                    out=junk[:], in_=vt[:], func=AF.Copy, scale=1.0 / GW, accum_out=sc[:, 0:1])), [t_v], S_act)
                t_a2 = P.op("scalar", (lambda e, vt=vt, sc=sc: e.activation(
                    out=junk[:], in_=vt[:], func=AF.Square, scale=float(1.0 / np.sqrt(GW)), accum_out=sc[:, 1:2])),
                    [t_a1], S_act)
                t_x = P.op("vector", (lambda e, sc=sc: e.tensor_tensor(
                    out=sc[:, 2:3], in0=sc[:, 0:1], in1=sc[:, 0:1], op=ALU.mult)), [t_a2], S_dve)
                t_var = P.op("vector", (lambda e, sc=sc: e.tensor_tensor(
                    out=sc[:, 3:4], in0=sc[:, 1:2], in1=sc[:, 2:3], op=ALU.subtract)), [t_x], S_dve)
                t_sd = P.op("scalar", (lambda e, sc=sc: e.activation(
                    out=sc[:, 4:5], in_=sc[:, 3:4], func=AF.Sqrt, bias=eps_t[:, 0:1])), [t_var], S_act)
                t_iv = P.op("vector", (lambda e, sc=sc: e.reciprocal(out=sc[:, 5:6], in_=sc[:, 4:5])), [t_sd], S_dve)
                t_n0 = P.op("vector", (lambda e, vt=vt, sc=sc: e.tensor_scalar(
                    out=vt[:], in0=vt[:], scalar1=sc[:, 0:1], scalar2=sc[:, 5:6], op0=ALU.subtract, op1=ALU.mult)),
                    [t_iv], S_dve)
                nk, vn, _, nfree = vnr.next()
                t_vn = P.op("vector", (lambda e, vt=vt, vn=vn: e.tensor_tensor(
                    out=vn[:], in0=vt[:], in1=gvb[:], op=ALU.mult)), [t_g, t_n0] + nfree, S_dve)
                vring.free[vk] = [t_vn]
                gt_, gfr = stg.get()
                t_last = None
                tok = None
                for g in range(NH):
                    b = next_bank()
                    t_last = P.op("tensor", (lambda e, b=b, g=g, vn=vn: e.matmul(
                        psf[b][:, 0:128], lhsT=wsb[:, g, :], rhs=vn[:, g * 128:(g + 1) * 128], start=True, stop=True)),
                        [t_vn, t_wb, psfree.get(b)], S_pe)
                    tok = P.op("vector", (lambda e, b=b, g=g, ut=ut, gt_=gt_: e.scalar_tensor_tensor(
                        out=gt_[:, g * 128:(g + 1) * 128], in0=psf[b][:, 0:128], scalar=bst[:, g:g + 1],
                        in1=ut[:, g * 128:(g + 1) * 128], op0=ALU.add, op1=ALU.mult)), [t_last, t_u] + gfr, S_dve)
                    gfr = []
                    psfree[b] = tok
                vnr.free[nk] = [t_last]
                uring.free[uk] = [tok]
                stg.store(gm_d[tt * 128:(tt + 1) * 128, :], gt_[:], [tok])
            dout("gm", gm_d, BF16)
            if end_stage("s4"):
                return finish()

        with ExitStack() as st:
            G = 3
            kpb, t_kp = bcast_load(st, "s5kp", kpos[0:1, :], S)
            qp = st.enter_context(nc.sbuf_tensor("s5qp", [128, NTO], F32))
            t_qp = P.dma("sync", qp[:], qpos[:, :], [], S_ld)
            t_c = [t_kp, t_qp]
            qring = Ring(st, "s5q", 2, [128, TO], BF16)
            kring = Ring(st, "s5k", 2, [128, S], BF16)
            vring = Ring(st, "s5v", 2, [128, NTA, 128], BF16)
            RD = 2 * G
            ering = Ring(st, "s5e", RD, [128, 512], F32, sems=False)
            lring = Ring(st, "s5l", RD, [128, 512], F32, sems=False)
            cring = Ring(st, "s5c", RD, [128, 512], F32, sems=False)
            wring = Ring(st, "s5w", RD, [128, 512], F32, sems=False)
            aring = Ring(st, "s5a", RD, [128, 512], BF16, sems=False)
            atring = Ring(st, "s5at", RD, [128, 4, 128], BF16, sems=False)
            zeros = st.enter_context(nc.sbuf_tensor("s5z", [128, 512], F32))
            carry0 = st.enter_context(nc.sbuf_tensor("s5carry0", [128, 1], F32))
            P.op("vector", lambda e: e.memset(zeros[:], 0.0), [], S_dve)
            t_z0 = P.op("vector", lambda e: e.memset(carry0[:], 0.0), [], S_dve)
            ostg = mk_stager(st, "s5o", 4, [128, 128], BF16)
            zbanks = [2, 3, 4]
            obanks = [0, 1, 5]
            ofree = {}
            tfree = [None, None]
            tcnt = [0]
            NCH = S // 512
            items = [(hh, qb) for hh in range(NH) for qb in range(NTO)]
            head = {}

            def load_head(hh):
                qk, qt_, qs_, qfree = qring.next()
                t_q = P.dma("sync", qt_[:], qT_d[hh], qfree, qs_)
                kk, kt, ks_, kfree = kring.next()
                t_k = P.dma("sync", kt[:], kT_d[hh], kfree, ks_)
                vk, vt, vs_, vfree = vring.next()
                vsrc = v_d[:, hh * 128:(hh + 1) * 128].rearrange("(b s) d -> s b d", s=128)
                nsp = max(1, NTA // 16)
                t_v = None
                for sp in range(nsp):
                    b0, b1 = sp * NTA // nsp, (sp + 1) * NTA // nsp
                    t_v = P.dma("sync", vt[:, b0:b1, :], vsrc[:, b0:b1, :], vfree, vs_)
                head[hh] = dict(qt=qt_, kt=kt, vt=vt, t_q=t_q, t_k=t_k, t_v=t_v, qk=qk, kk=kk, vk=vk)

            def front(c, ch):
                hd = head[c["hh"]]
                zb = c["zb"]
                qb = c["qb"]
                t_z = P.op("tensor", (lambda e, zb=zb, qb=qb, hd=hd, ch=ch: e.matmul(
                    psf[zb][:, :], lhsT=hd["qt"][:, qb * 128:(qb + 1) * 128], rhs=hd["kt"][:, ch * 512:(ch + 1) * 512],
                    start=True, stop=True)), [hd["t_q"], hd["t_k"], PSF.get(zb)], S_pe)
                c["t_z"] = t_z

            def front2(c, ch):
                zb = c["zb"]
                ek, et, _, efree = ering.next()
                t_e = P.op("scalar", (lambda e, et=et, zb=zb: e.activation(
                    out=et[:], in_=psf[zb][:, :], func=AF.Exp)), [c["t_z"]] + efree, S_act)
                PSF[zb] = t_e
                c["e"] = (ek, et)
                c["t_e"] = t_e

            def front3(c, ch):
                ek, et = c["e"]
                qb = c["qb"]
                c["t_em"] = P.op("vector", (lambda e, et=et, ch=ch, qb=qb: e.scalar_tensor_tensor(
                    out=et[:], in0=kpb[:, ch * 512:(ch + 1) * 512], scalar=qp[:, qb:qb + 1], in1=et[:],
                    op0=ALU.is_gt, op1=ALU.mult)), [c["t_e"]] + t_c, S_dve)

            def front4(c, ch):
                ek, et = c["e"]
                lk, lt, _, lfree = lring.next()
                c["t_l"] = P.op("scalar", (lambda e, et=et, lt=lt: e.activation(
                    out=lt[:], in_=et[:], func=AF.Ln, bias=1.0)), [c["t_em"]] + lfree, S_act)
                c["l"] = (lk, lt)
                c["pend"] = dict(e=c["e"], l=c["l"], ch=ch, t_em_=c["t_em"])

            def back1(c):
                p = c["pend"]
                lk, lt = p["l"]
                ck, ct, _, cfree = cring.next()
                init = carry0[:, 0:1] if c["prev_c"] is None else c["prev_c"][:, 511:512]
                t_sc = P.op("vector", (lambda e, lt=lt, ct=ct, init=init: e.tensor_tensor_scan(
                    out=ct[:], data0=lt[:], data1=zeros[:], initial=init, op0=ALU.add, op1=ALU.add)),
                    [c["t_l"], c["t_prev_sc"]] + cfree, S_dve)
                c["t_prev_sc"] = t_sc
                lring.free[lk] = [t_sc]
                c["prev_c"] = ct
                p["c"] = (ck, ct)
                p["t_sc"] = t_sc

            def back2(c):
                p = c["pend"]
                ck, ct = p["c"]
                wk, wt, _, wfree = wring.next()
                t_w = P.op("scalar", (lambda e, ct=ct, wt=wt: e.activation(
                    out=wt[:], in_=ct[:], func=AF.Exp, scale=-1.0)), [p["t_sc"]] + wfree, S_act)
                cring.free[ck] = [t_w]
                p["w"] = (wk, wt)
                p["t_w"] = t_w

            def back3(c):
                p = c["pend"]
                ek, et = p["e"]
                wk, wt = p["w"]
                ak, at, _, afree = aring.next()
                if os.environ.get("MK_NOPOOL"):
                    t_a = P.op("vector", (lambda e, et=et, wt=wt, at=at: e.tensor_tensor(
                        out=at[:], in0=et[:], in1=wt[:], op=ALU.mult)), [p["t_w"]] + afree, S_dve)
                else:
                    t_a = P.op("gpsimd", (lambda e, et=et, wt=wt, at=at: e.tensor_tensor(
                        out=at[:], in0=et[:], in1=wt[:], op=ALU.mult)), [p["t_w"], p["t_em_"]] + afree, S_pool)
                ering.free[ek] = [t_a]
                wring.free[wk] = [t_a]
                p["a"] = (ak, at)
                p["t_a"] = t_a

            def back4(c):
                p = c["pend"]
                ak, at = p["a"]
                tb = tcnt[0] % 2
                tcnt[0] += 1
                t_tr = None
                for j in range(4):
                    t_tr = P.op("tensor", (lambda e, tb=tb, j=j, at=at: e.transpose(
                        out=psbs[tb][:, j * 128:(j + 1) * 128], in_=at[:, j * 128:(j + 1) * 128],
                        identity=ident_b[:])), [p["t_a"], tfree[tb]], S_pe if j == 3 else None)
                aring.free[ak] = [t_tr]
                p["tb"] = tb
                p["t_tr"] = t_tr

            def back5(c):
                p = c["pend"]
                tb = p["tb"]
                tk_, att, _, atfree = atring.next()
                t_at = P.op("scalar", (lambda e, tb=tb, att=att: e.activation(
                    out=att[:], in_=psbs[tb][:, 0:512].rearrange("p (j t) -> p j t", j=4),
                    func=AF.Copy)), [p["t_tr"]] + atfree, S_act)
                tfree[tb] = t_at
                p["at"] = (tk_, att)
                p["t_at"] = t_at

            def back6(c):
                p = c["pend"]
                tk_, att = p["at"]
                hd = head[c["hh"]]
                ob = c["ob"]
                ch = p["ch"]
                last = None
                for j in range(4):
                    n_av = c["n_av"]
                    last = P.op("tensor", (lambda e, ob=ob, j=j, att=att, hd=hd, ch=ch, n_av=n_av: e.matmul(
                        psf[ob][:, 0:128], lhsT=att[:, j, :], rhs=hd["vt"][:, ch * 4 + j, :],
                        start=(n_av == 0), stop=(n_av == c["nav_tot"] - 1))),
                        [p["t_at"], hd["t_v"], ofree.get(ob) if n_av == 0 else None],
                        S_pe if (j == 3) else None)
                    c["n_av"] += 1
                atring.free[tk_] = [last]
                c["last_pe"] = last

            loaded = set()
            last_of_head = {}
            for gi in range(0, len(items), G):
                grp = items[gi:gi + G]
                for (hh, qb) in grp:
                    if hh not in loaded:
                        load_head(hh)
                        loaded.add(hh)
                chains = [dict(hh=hh, qb=qb, zb=zbanks[i], ob=obanks[i], prev_c=None, t_prev_sc=t_z0, n_av=0,
                               ch0=(cfg.NC * qb * 128) // 512)
                          for i, (hh, qb) in enumerate(grp)]
                for c in chains:
                    c["nav_tot"] = (NCH - c["ch0"]) * 4
                for i in range(min(c["ch0"] for c in chains), NCH + 1):
                    fr_ = [c for c in chains if c["ch0"] <= i < NCH]
                    bk_ = [c for c in chains if c["ch0"] <= i - 1]
                    for c in bk_:
                        back1(c)
                    for c in fr_:
                        front(c, i)
                    for c in fr_:
                        front2(c, i)
                    for c in bk_:
                        back2(c)
                    for c in fr_:
                        front3(c, i)
                    for c in bk_:
                        back3(c)
                    for c in bk_:
                        back4(c)
                        back5(c)
                    for c in bk_:
                        back6(c)
                    for c in fr_:
                        front4(c, i)
                for c in chains:
                    ot, ofr = ostg.get()
                    ob = c["ob"]
                    t_o = P.op("vector", (lambda e, ot=ot, ob=ob: e.tensor_copy(out=ot[:], in_=psf[ob][:, 0:128])),
                               [c["last_pe"]] + ofr, S_dve)
                    ofree[ob] = t_o
                    ostg.store(o_d[c["qb"] * 128:(c["qb"] + 1) * 128, c["hh"] * 128:(c["hh"] + 1) * 128], ot[:], [t_o])
                    last_of_head[c["hh"]] = c["last_pe"]
                done_heads = [hh for hh in list(head.keys()) if all((hh, qb) in items[:gi + G] for qb in range(NTO))]
                for hh in done_heads:
                    hd = head.pop(hh)
                    tok = last_of_head[hh]
                    qring.free[hd["qk"]] = [tok]
                    kring.free[hd["kk"]] = [tok]
                    vring.free[hd["vk"]] = [tok]
            dout("o", o_d, BF16)
            if end_stage("s5"):
                return finish()

        for which in (() if os.environ.get('MK_SKIP6') else (0, 1)):
            with ExitStack() as st:
                actT = st.enter_context(nc.sbuf_tensor(f"s6act{which}", [128, GKC, TO], BF16))
                src = gm_d if which == 0 else o_d
                with ExitStack() as st2:
                    load_T(st2, src, GW, actT, [], f"s6l{which}")
                    if end_stage(f"s6l{which}"):
                        return finish()
                sgr = Ring(st, f"s6sg{which}", 3, [128, 512], BF16)
                t1r = Ring(st, f"s6t1{which}", 3, [128, 512], F32)
                stg_f = mk_stager(st, f"s6sf{which}", 3, [128, 512], F32)
                stg_b = mk_stager(st, f"s6sb{which}", 3, [128, 512], BF16)
                sg_src = sga_d if which == 0 else sgb_d

                def ep6(tt, c0, cw, ps, ready, which=which, sgr=sgr, t1r=t1r, stg_f=stg_f, stg_b=stg_b, sg_src=sg_src):
                    ep6m = int(os.environ.get("MK_EP6", "2"))
                    if ep6m == 0:
                        t, fr = stg_f.get()
                        tok = P.op("scalar", (lambda e, t=t, ps=ps, cw=cw: e.activation(
                            out=t[:, 0:cw], in_=ps[:, 0:cw], func=AF.Copy)), [ready] + fr, S_act)
                        stg_f.store(t1_d[tt * 128:(tt + 1) * 128, c0:c0 + cw], t[:, 0:cw], [tok])
                        return tok
                    gk, gt_, gs_, gfree = sgr.next()
                    t_g = P.dma("sync", gt_[:, 0:cw], sg_src[tt * 128:(tt + 1) * 128, c0:c0 + cw], gfree, gs_)
                    if ep6m == 1:
                        t, fr = stg_f.get()
                        tok = P.op("scalar", (lambda e, t=t, ps=ps, cw=cw: e.activation(
                            out=t[:, 0:cw], in_=ps[:, 0:cw], func=AF.Copy)), [ready, t_g] + fr, S_act)
                        sgr.free[gk] = [tok]
                        stg_f.store(t1_d[tt * 128:(tt + 1) * 128, c0:c0 + cw], t[:, 0:cw], [tok])
                        return tok
                    if which == 0:
                        t, fr = stg_f.get()
                        tok = P.op("vector", (lambda e, t=t, ps=ps, gt_=gt_, cw=cw: e.tensor_tensor(
                            out=t[:, 0:cw], in0=ps[:, 0:cw], in1=gt_[:, 0:cw], op=ALU.mult)), [ready, t_g] + fr, S_dve)
                        sgr.free[gk] = [tok]
                        stg_f.store(t1_d[tt * 128:(tt + 1) * 128, c0:c0 + cw], t[:, 0:cw], [tok])
                        return tok
                    k1, t1, s1, f1 = t1r.next()
                    t_1 = P.dma("sync", t1[:, 0:cw], t1_d[tt * 128:(tt + 1) * 128, c0:c0 + cw], f1, s1)
                    t, fr = stg_b.get()
                    tok_a = P.op("vector", (lambda e, ps=ps, gt_=gt_, cw=cw: e.tensor_tensor(
                        out=gt_[:, 0:cw], in0=ps[:, 0:cw], in1=gt_[:, 0:cw], op=ALU.mult)), [ready, t_g], S_dve)
                    tok_b = P.op("vector", (lambda e, t=t, t1=t1, gt_=gt_, cw=cw: e.tensor_tensor(
                        out=t[:, 0:cw], in0=gt_[:, 0:cw], in1=t1[:, 0:cw], op=ALU.add)), [t_1, tok_a] + fr, S_dve)
                    sgr.free[gk] = [tok_b]
                    t1r.free[k1] = [tok_b]
                    stg_b.store(mix_d[tt * 128:(tt + 1) * 128, c0:c0 + cw], t[:, 0:cw], [tok_b])
                    return tok_a

                wring = Ring(st, f"s6w{which}", 2, [128, GKC, 512], BF16)
                gemm(wring, actT, GKC, wpa if which == 0 else wpb, 0, 0, D, ep6)
                if which == 1:
                    dout("mix", mix_d, BF16)
                if end_stage(f"s6{which}"):
                    return finish()

        with ExitStack() as st:
            actT = st.enter_context(nc.sbuf_tensor("s8act", [128, KC, TO], BF16))
            with ExitStack() as st2:
                load_T(st2, (sga_d if os.environ.get("MK_S8SRC") else mix_d)[:, 0:int(os.environ.get("MK_LTK", D))], int(os.environ.get("MK_LTK", D)), actT, [], "s8l")
                if end_stage("s8l"):
                    return finish()
            stg_f = mk_stager(st, "s8sf", 3, [128, 512], F32)

            def ep8(tt, c0, cw, ps, ready):
                t, fr = stg_f.get()
                tok = P.op("scalar", (lambda e, t=t, ps=ps, cw=cw: e.activation(
                    out=t[:, 0:cw], in_=ps[:, 0:cw], func=AF.Copy)), [ready] + fr, S_act)
                stg_f.store(mo_d[tt * 128:(tt + 1) * 128, c0:c0 + cw], t[:, 0:cw], [tok])
                return tok

            wring = Ring(st, "s8w", 2, [128, KC, 512], BF16)
            gemm(wring, actT, KC, wo, 0, 0, D, ep8)
            dout("mo", mo_d, F32)
            if end_stage("s8"):
                return finish()

        def post_norm_residual(st, br_d, res_d, gsrc, i_gt, dst_d, tag):
            gtb, _ = bcast_load(st, tag + "gt", modrow_ap(i_gt), D)
            gb, t3 = bcast_load(st, tag + "g", gsrc[0:1, :], D)
            t_gg = P.op("vector", lambda e: e.tensor_tensor(out=gtb[:], in0=gtb[:], in1=gb[:], op=ALU.mult), [t3], S_dve)
            bring = Ring(st, tag + "b", 2, [128, D], F32)
            rring = Ring(st, tag + "r", 2, [128, D], F32)
            junk = st.enter_context(nc.sbuf_tensor(tag + "junk", [128, D], BF16))
            stt = st.enter_context(nc.sbuf_tensor(tag + "st", [128, 4 * NTO], F32))
            stg = mk_stager(st, tag + "o", 2, [128, D], F32)
            for tt in range(NTO):
                bk, bt, bs_, bfree = bring.next()
                t_b = P.dma("sync", bt[:], br_d[tt * 128:(tt + 1) * 128, :], bfree, bs_)
                rk, rt, rs_, rfree = rring.next()
                t_r = P.dma("sync", rt[:], res_d[tt * 128:(tt + 1) * 128, :], rfree, rs_)
                sc = stt[:, tt * 4:(tt + 1) * 4]
                t_q = P.op("scalar", (lambda e, bt=bt, sc=sc: e.activation(
                    out=junk[:], in_=bt[:], func=AF.Square, accum_out=sc[:, 0:1])), [t_b], S_act)
                t_sd = P.op("scalar", (lambda e, sc=sc: e.activation(
                    out=sc[:, 1:2], in_=sc[:, 0:1], func=AF.Sqrt, scale=1.0 / D, bias=eps_t[:, 0:1])), [t_q], S_act)
                t_iv = P.op("vector", (lambda e, sc=sc: e.reciprocal(out=sc[:, 2:3], in_=sc[:, 1:2])), [t_sd], S_dve)
                t_m = P.op("vector", (lambda e, bt=bt, sc=sc: e.scalar_tensor_tensor(
                    out=bt[:], in0=bt[:], scalar=sc[:, 2:3], in1=gtb[:], op0=ALU.mult, op1=ALU.mult)),
                    [t_gg, t_iv], S_dve)
                ot, ofr = stg.get()
                tok = P.op("vector", (lambda e, bt=bt, rt=rt, ot=ot: e.tensor_tensor(
                    out=ot[:], in0=bt[:], in1=rt[:], op=ALU.add)), [t_r, t_m] + ofr, S_dve)
                bring.free[bk] = [tok]
                rring.free[rk] = [tok]
                stg.store(dst_d[tt * 128:(tt + 1) * 128, :], ot[:], [tok])

        with ExitStack() as st:
            post_norm_residual(st, mo_d, xo, gpost, 2, x1_d, "s9")
            dout("x1", x1_d, F32)
            if end_stage("s9"):
                return finish()

        with ExitStack() as st:
            h2T = st.enter_context(nc.sbuf_tensor("h2T", [128, KC, TO], BF16))
            with ExitStack() as st2:
                mb, shb, t_m = mod_tiles(st2, gpre2, 4, 3, "s10")

                def emit10(tt, kc0, nk, ps, ready):
                    return P.op("scalar", (lambda e, tt=tt, kc0=kc0, nk=nk, ps=ps: e.activation(
                        out=h2T[:, kc0:kc0 + nk, tt * 128:(tt + 1) * 128],
                        in_=ps[:, 0:nk * 128].rearrange("p (k t) -> p k t", k=nk), func=AF.Copy)), [ready], S_act)

                P.op("vector", None, [t_m])
                frontend(st2, x1_d, NTO, mb, shb, emit10, "s10")
                if end_stage("s10a"):
                    return finish()
            rr = Ring(st, "s10r", 3, [128, 512], F32)
            stg_b = mk_stager(st, "s10sb", 3, [128, 512], BF16)

            def ep10(tt, c0, cw, ps, ready):
                rk, rt, _, rfree = rr.next()
                tok = P.op("scalar", (lambda e, rt=rt, ps=ps, cw=cw: e.activation(
                    out=rt[:, 0:cw], in_=ps[:, 0:cw], func=AF.Relu)), [ready] + rfree, S_act)
                t, fr = stg_b.get()
                tok2 = P.op("vector", (lambda e, t=t, rt=rt, cw=cw: e.tensor_tensor(
                    out=t[:, 0:cw], in0=rt[:, 0:cw], in1=rt[:, 0:cw], op=ALU.mult)), [tok] + fr, S_dve)
                rr.free[rk] = [tok2]
                stg_b.store(f_d[tt * 128:(tt + 1) * 128, c0:c0 + cw], t[:, 0:cw], [tok2])
                return tok

            wring = Ring(st, "s10w", 2, [128, KC, 512], BF16)
            gemm(wring, h2T, KC, wf1, 0, 0, DFF, ep10)
            dout("f", f_d, BF16)
            if end_stage("s10"):
                return finish()

        for kb in range(DFF // D):
            with ExitStack() as st:
                actT = st.enter_context(nc.sbuf_tensor(f"s11act{kb}", [128, KC, TO], BF16))
                with ExitStack() as st2:
                    load_T(st2, f_d[:, kb * D:(kb + 1) * D], D, actT, [], f"s11l{kb}")
                    if end_stage(f"s11l{kb}"):
                        return finish()
                ar = Ring(st, f"s11a{kb}", 3, [128, 512], F32)
                stg_f = mk_stager(st, f"s11sf{kb}", 3, [128, 512], F32)

                def ep11(tt, c0, cw, ps, ready, kb=kb, ar=ar, stg_f=stg_f):
                    t, fr = stg_f.get()
                    if kb == 0:
                        tok = P.op("scalar", (lambda e, t=t, ps=ps, cw=cw: e.activation(
                            out=t[:, 0:cw], in_=ps[:, 0:cw], func=AF.Copy)), [ready] + fr, S_act)
                    else:
                        ak, at, as_, afree = ar.next()
                        t_a = P.dma("sync", at[:, 0:cw], acc_d[tt * 128:(tt + 1) * 128, c0:c0 + cw], afree, as_)
                        tok = P.op("vector", (lambda e, t=t, ps=ps, at=at, cw=cw: e.tensor_tensor(
                            out=t[:, 0:cw], in0=ps[:, 0:cw], in1=at[:, 0:cw], op=ALU.add)), [ready, t_a] + fr, S_dve)
                        ar.free[ak] = [tok]
                    stg_f.store(acc_d[tt * 128:(tt + 1) * 128, c0:c0 + cw], t[:, 0:cw], [tok])
                    return tok

                wring = Ring(st, f"s11w{kb}", 2, [128, KC, 512], BF16)
                gemm(wring, actT, KC, wf2, kb * D, 0, D, ep11)
                if end_stage(f"s11{kb}"):
                    return finish()

        with ExitStack() as st:
            post_norm_residual(st, acc_d, x1_d, gpost2, 5, y, "s12")
            end_stage("s12")
        return finish()


def own_rows(cfg, cid):
    blk = cfg.NC * np.arange(cfg.NTO) + cid
    return (blk[:, None] * 128 + np.arange(128)[None, :]).reshape(-1)


def prep_inputs(cfg, x, c, w_ada, b_ada, g_pre_mix, w_in, g_v, w_s, b_s, w_proj_a, w_proj_b, w_o,
                g_post_mix, g_pre_mlp, w_ff1, w_ff2, g_post_mlp):
    f = lambda a: np.ascontiguousarray(np.asarray(a, dtype=np.float32))
    xr = f(np.asarray(x)[0, ::-1, :])
    common = {
        "xr": xr,
        "cv": f(np.asarray(c)[0].reshape(cfg.KC, 128).T),
        "wada": f(np.asarray(w_ada)[0]),
        "bada": f(np.asarray(b_ada)[0][None, :]),
        "win": f(np.asarray(w_in)[0]),
        "gpre": f(np.asarray(g_pre_mix)[0][None, :]),
        "gv": f(np.asarray(g_v)[0][None, :]),
        "wsT": f(np.asarray(w_s)[0][:, ::-1, ::-1].transpose(2, 0, 1)),
        "bsr": f(np.asarray(b_s)[0][:, ::-1].T),
        "wpa": f(np.asarray(w_proj_a)[0]),
        "wpb": f(np.asarray(w_proj_b)[0]),
        "wo": f(np.asarray(w_o)[0]),
        "gpost": f(np.asarray(g_post_mix)[0][None, :]),
        "gpre2": f(np.asarray(g_pre_mlp)[0][None, :]),
        "wf1": f(np.asarray(w_ff1)[0]),
        "wf2": f(np.asarray(w_ff2)[0]),
        "gpost2": f(np.asarray(g_post_mlp)[0][None, :]),
        "identf": np.eye(128, dtype=np.float32),
        "kpos": np.arange(cfg.S, dtype=np.float32)[None, :],
    }
    in_maps = []
    for cid in range(cfg.NC):
        m = dict(common)
        rows = own_rows(cfg, cid)
        m["xo"] = f(xr[rows])
        m["qpos"] = f(rows.astype(np.float32).reshape(cfg.NTO, 128).T)
        in_maps.append(m)
    return in_maps


_CACHE = {}


def run(cfg, inputs, stop_after=None, debug=False, trace=False):
    key = (cfg.D, cfg.S, cfg.NC, stop_after, debug)
    if key not in _CACHE:
        _CACHE[key] = build_nc(cfg, stop_after, debug)
    nc = _CACHE[key]
    in_maps = prep_inputs(cfg, **inputs)
    return run_bass_kernel_spmd(nc, in_maps, core_ids=list(range(cfg.NC)), trace=trace)


def kernel(**inputs):
    cfg = Cfg()
    res = run(cfg, inputs)
    return assemble(cfg, [np.asarray(r["y"]) for r in res.results])


def assemble(cfg, ys):
    yr = np.empty((cfg.S, cfg.D), np.float32)
    for cid, yc in enumerate(ys):
        yr[own_rows(cfg, cid)] = yc
    return np.ascontiguousarray(yr[::-1])[None].astype(np.float32)
```

```python
import os
from contextlib import ExitStack
import numpy as np
import concourse.bass as bass
import concourse.mybir as mybir
from concourse.bass_utils import run_bass_kernel_spmd

F32 = mybir.dt.float32
BF16 = mybir.dt.bfloat16
AF = mybir.ActivationFunctionType
ALU = mybir.AluOpType
EPS = 1e-6
ENGS = ("sync", "scalar", "gpsimd", "vector", "tensor")


class Cfg:
    def __init__(self, D=4096, S=8192, NC=8):
        self.D, self.S, self.NC = D, S, NC
        self.KC = D // 128
        self.TO = S // NC
        self.NTO = self.TO // 128
        self.NTA = S // 128
        self.GW = D // 2
        self.NH = self.GW // 128
        self.DFF = 4 * D
        self.NMOD = 6 * D
        self.INC = 2 * self.GW + 3 * self.GW + 2 * D


class CS:
    def __init__(self, h):
        self.h = h
        self.n = 0


class Prog:
    def __init__(self, nc):
        self.nc = nc
        self.q = {e: [] for e in ENGS}
        self.waited = {e: {} for e in ENGS}
        self.nblk = 0

    def op(self, eng, fn, waits=(), sig=None, amt=1):
        tok = None
        if sig is not None:
            sig.n += amt
            tok = (sig, sig.n)
        ws = []
        seen = self.waited[eng]
        stack = list(waits)
        while stack:
            w = stack.pop()
            if w is None:
                continue
            if isinstance(w, list):
                stack.extend(w)
                continue
            s, v = w
            if seen.get(id(s), 0) >= v:
                continue
            seen[id(s)] = v
            ws.append((s, v))
        self.q[eng].append((fn, ws, sig, amt))
        return tok

    def dma(self, eng, out, in_, waits=(), sig=None, **kw):
        return self.op(eng, lambda e: e.dma_start(out=out, in_=in_, **kw), waits, sig, 16)

    def flush(self, name=None):
        self.nblk += 1
        name = name or f"blk{self.nblk}"
        with self.nc.Block(name) as block:
            for eng in ENGS:
                items = self.q[eng]
                if not items:
                    continue

                def body(e, items=items):
                    for fn, ws, sig, amt in items:
                        for (s, v) in ws:
                            e.wait_ge(s.h, v)
                        if fn is None:
                            continue
                        ins = fn(e)
                        if sig is not None:
                            ins.then_inc(sig.h, amt)

                getattr(block, eng)(body)
        self.q = {e: [] for e in ENGS}


def build_nc(cfg, stop_after=None, debug=False):
    D, S, KC, TO, NTO, NTA, GW, NH, DFF = cfg.D, cfg.S, cfg.KC, cfg.TO, cfg.NTO, cfg.NTA, cfg.GW, cfg.NH, cfg.DFF
    GKC = GW // 128
    nc = bass.Bass("TRN2", target_bir_lowering=False)
    P = Prog(nc)

    def din(name, shape, dt=F32):
        return nc.dram_tensor(name, list(shape), dt, kind="ExternalInput").ap()

    def dscr(name, shape, dt):
        return nc.dram_tensor(name, list(shape), dt).ap()

    xr = din("xr", [S, D])
    xo = din("xo", [TO, D])
    cv = din("cv", [128, KC])
    wada = din("wada", [D, cfg.NMOD])
    bada = din("bada", [1, cfg.NMOD])
    win = din("win", [D, cfg.INC])
    gpre = din("gpre", [1, D])
    gv = din("gv", [1, GW])
    wsT = din("wsT", [128, NH, 128])
    bsr = din("bsr", [128, NH])
    wpa = din("wpa", [GW, D])
    wpb = din("wpb", [GW, D])
    wo = din("wo", [D, D])
    gpost = din("gpost", [1, D])
    gpre2 = din("gpre2", [1, D])
    wf1 = din("wf1", [D, DFF])
    wf2 = din("wf2", [DFF, D])
    gpost2 = din("gpost2", [1, D])
    identf = din("identf", [128, 128])
    kpos = din("kpos", [1, S])
    qpos = din("qpos", [128, NTO])
    y = nc.dram_tensor("y", [TO, D], F32, kind="ExternalOutput").ap()

    dbg = {}

    def dout(name, src, dt=F32):
        if not debug:
            return
        a = nc.dram_tensor("dbg_" + name, list(src.shape), dt, kind="ExternalOutput").ap()
        dbg[name] = (a, src)

    mod_d = dscr("mod_d", [1, cfg.NMOD], F32)
    hT_d = dscr("hT_d", [NTA // 4, 128, KC, 512], BF16)
    kT_d = dscr("kT_d", [NH, 128, S], BF16)
    v_d = dscr("v_d", [S, GW], BF16)
    qT_d = dscr("qT_d", [NH, 128, TO], BF16)
    u_d = dscr("u_d", [TO, GW], BF16)
    vr_d = dscr("vr_d", [TO, GW], F32)
    sga_d = dscr("sga_d", [TO, D], BF16)
    sgb_d = dscr("sgb_d", [TO, D], BF16)
    gm_d = dscr("gm_d", [TO, GW], BF16)
    o_d = dscr("o_d", [TO, GW], BF16)
    t1_d = dscr("t1_d", [TO, D], F32)
    mix_d = dscr("mix_d", [TO, D], BF16)
    mo_d = dscr("mo_d", [TO, D], F32)
    x1_d = dscr("x1_d", [TO, D], F32)
    f_d = dscr("f_d", [TO, DFF], BF16)
    acc_d = dscr("acc_d", [TO, D], F32)

    with ExitStack() as es:
        def sem(name):
            return CS(es.enter_context(nc.semaphore(name)))

        S_pe, S_act, S_dve = sem("s_pe"), sem("s_act"), sem("s_dve")
        S_pool = sem("s_pool")
        S_ld = sem("s_ld")
        S_st = sem("s_st")
        ring_sems = [sem(f"rs{i}") for i in range(24)]
        psf = [es.enter_context(nc.psum_tensor(f"psf{i}", [128, 512], F32)) for i in range(6)]
        psbs = [es.enter_context(nc.psum_tensor(f"psb{i}", [128, 1024], BF16)) for i in range(2)]
        ident_f = es.enter_context(nc.sbuf_tensor("ident_f", [128, 128], F32))
        ident_b = es.enter_context(nc.sbuf_tensor("ident_b", [128, 128], BF16))
        eps_t = es.enter_context(nc.sbuf_tensor("eps_t", [128, 1], F32))

        state = {"rs": 0}
        PSF = {}
        PI = [0]

        def next_bank():
            b = 2 + PI[0] % 4
            PI[0] += 1
            return b

        def new_ring_sem():
            assert state["rs"] < len(ring_sems), "out of ring semaphores in this stage"
            s_ = ring_sems[state["rs"]]
            state["rs"] += 1
            return s_

        stagers = []

        def wait_stagers():
            toks = []
            for s_ in stagers:
                for fr in s_.ring.free:
                    toks.extend(fr)
            P.op("sync", None, toks)
            stagers.clear()

        def end_stage(name):
            wait_stagers()
            P.op("sync", None, [(S_st, S_st.n), (S_ld, S_ld.n)])
            P.flush(name)
            state["rs"] = 0
            if os.environ.get("MK_VERBOSE"):
                print("stage", name, "sem counts pe/act/dve/ld/st", S_pe.n, S_act.n, S_dve.n, S_ld.n, S_st.n,
                      "ring max", max(r.n for r in ring_sems), flush=True)
            return stop_after == name

        def finish():
            for name, (a, src) in dbg.items():
                P.dma("sync", a, src, [], S_st)
            P.op("sync", None, [(S_st, S_st.n)])
            P.flush("fin")
            return nc

        class Ring:
            def __init__(self, st, name, n, shape, dt, sems=True):
                self.bufs = [st.enter_context(nc.sbuf_tensor(f"{name}{i}", list(shape), dt)) for i in range(n)]
                self.sems = [new_ring_sem() if sems else None for _ in range(n)]
                self.free = [[] for _ in range(n)]
                self.i = 0

            def next(self):
                k = self.i % len(self.bufs)
                self.i += 1
                fr = self.free[k]
                self.free[k] = []
                return k, self.bufs[k], self.sems[k], fr

        class Stager:
            def __init__(self, st, name, n, shape, dt):
                self.ring = Ring(st, name, n, shape, dt)

            def get(self):
                k, t, s_, fr = self.ring.next()
                self.k, self.s = k, s_
                return t, fr

            def store(self, dst, src, waits, eng="sync"):
                tok = P.dma(eng, dst, src, waits, self.s)
                self.ring.free[self.k] = [tok]
                return tok

        def mk_stager(st, name, n, shape, dt):
            s_ = Stager(st, name, n, shape, dt)
            stagers.append(s_)
            return s_

        def bcast_load(st, name, src_row, n, dt=F32, waits=()):
            t = st.enter_context(nc.sbuf_tensor(name, [128, n], dt))
            tok = P.dma("sync", t[:], src_row.partition_broadcast(128), list(waits), S_ld)
            return t, tok

        def frontend(st, src_d, ntiles, mb, shb, emit_hT, tag):
            xring = Ring(st, tag + "x", 2, [128, D], F32)
            hring = Ring(st, tag + "h", 2, [128, D], F32)
            junk = st.enter_context(nc.sbuf_tensor(tag + "junk", [128, D], BF16))
            ssq = st.enter_context(nc.sbuf_tensor(tag + "ssq", [128, ntiles], F32))
            std = st.enter_context(nc.sbuf_tensor(tag + "std", [128, ntiles], F32))
            inv = st.enter_context(nc.sbuf_tensor(tag + "inv", [128, ntiles], F32))
            psfree = [None, None]
            pi = [0]

            def phaseA(tt):
                k, xt, xs, xfree = xring.next()
                t_ld = P.dma("sync", xt[:], src_d[tt * 128:(tt + 1) * 128, :], xfree, xs)
                t_sq = P.op("scalar", (lambda e, xt=xt, tt=tt: e.activation(
                    out=junk[:], in_=xt[:], func=AF.Square, accum_out=ssq[:, tt:tt + 1])), [t_ld], S_act)
                t_sd = P.op("scalar", (lambda e, tt=tt: e.activation(
                    out=std[:, tt:tt + 1], in_=ssq[:, tt:tt + 1], func=AF.Sqrt, scale=1.0 / D, bias=eps_t[:, 0:1])),
                    [t_sq], S_act)
                t_iv = P.op("vector", (lambda e, tt=tt: e.reciprocal(out=inv[:, tt:tt + 1], in_=std[:, tt:tt + 1])),
                            [t_sd], S_dve)
                hk, ht, _, hfree = hring.next()
                t_h0 = P.op("vector", (lambda e, xt=xt, ht=ht, tt=tt: e.scalar_tensor_tensor(
                    out=ht[:], in0=xt[:], scalar=inv[:, tt:tt + 1], in1=mb[:], op0=ALU.mult, op1=ALU.mult)),
                    [t_ld, t_iv] + hfree, S_dve)
                t_h = P.op("vector", (lambda e, ht=ht: e.tensor_tensor(out=ht[:], in0=ht[:], in1=shb[:], op=ALU.add)),
                           [t_h0], S_dve)
                xring.free[k] = [t_h, t_sq]
                return dict(hk=hk, ht=ht, t_h=t_h)

            def phaseB(tt, a):
                ht, t_h = a["ht"], a["t_h"]
                t_last = None
                for kc0 in range(0, KC, 4):
                    nk = min(4, KC - kc0)
                    b = pi[0] % 2
                    pi[0] += 1
                    for j in range(nk):
                        t_last = P.op("tensor", (lambda e, b=b, j=j, ht=ht, kc=kc0 + j: e.transpose(
                            out=psf[b][:, j * 128:(j + 1) * 128], in_=ht[:, kc * 128:(kc + 1) * 128], identity=ident_f[:])),
                            [t_h, psfree[b]], S_pe if j == nk - 1 else None)
                    psfree[b] = emit_hT(tt, kc0, nk, psf[b], t_last)
                hring.free[a["hk"]] = [t_last]

            cur_a = phaseA(0)
            for tt in range(ntiles):
                nxt_a = phaseA(tt + 1) if tt + 1 < ntiles else None
                phaseB(tt, cur_a)
                cur_a = nxt_a

        def load_T(st, src_d, K, actT, waits, tag):
            ring = Ring(st, tag + "ld", 2, [128, K], BF16)
            half_free = [None, None]
            hi = 0
            for tt in range(NTO):
                k, t, s_, fr = ring.next()
                t_ld = P.dma("sync", t[:], src_d[tt * 128:(tt + 1) * 128, :], fr + list(waits), s_)
                t_last = None
                for kc0 in range(0, K // 128, 4):
                    nk = min(4, K // 128 - kc0)
                    b = hi % 2
                    hi += 1
                    for j in range(nk):
                        t_last = P.op("tensor", (lambda e, b=b, j=j, t=t, kc=kc0 + j: e.transpose(
                            out=psbs[b][:, j * 128:(j + 1) * 128], in_=t[:, kc * 128:(kc + 1) * 128],
                            identity=ident_b[:])), [t_ld, half_free[b]], S_pe if j == nk - 1 else None)
                    half_free[b] = P.op("scalar", (lambda e, b=b, nk=nk, kc0=kc0, tt=tt: e.activation(
                        out=actT[:, kc0:kc0 + nk, tt * 128:(tt + 1) * 128],
                        in_=psbs[b][:, 0:nk * 128].rearrange("p (k t) -> p k t", k=nk), func=AF.Copy)),
                        [t_last], S_act)
                ring.free[k] = [t_last]

        def gemm(wring, actT, kcn, w_d, row0, col0, ncols, epilogue, cgw=512):
            wv = w_d[row0:row0 + kcn * 128, :].rearrange("(kc p) n -> p kc n", p=128)
            psfree = PSF
            for c0 in range(0, ncols, cgw):
                cw = min(cgw, ncols - c0)
                k, wt, ws_, wfree = wring.next()
                t_w = P.dma("gpsimd", wt[:, :, 0:cw], wv[:, :, col0 + c0:col0 + c0 + cw], wfree, ws_)
                t_last = None
                for tt in range(NTO):
                    b = next_bank()
                    for kc in range(kcn):
                        t_last = P.op("tensor", (lambda e, b=b, kc=kc, wt=wt, cw=cw, tt=tt: e.matmul(
                            psf[b][:, 0:cw], lhsT=actT[:, kc, tt * 128:(tt + 1) * 128], rhs=wt[:, kc, 0:cw],
                            start=(kc == 0), stop=(kc == kcn - 1))),
                            [t_w, psfree.get(b)], S_pe if kc == kcn - 1 else None)
                    psfree[b] = epilogue(tt, c0, cw, psf[b], t_last)
                wring.free[k] = [t_last]

        with ExitStack() as st:
            cvt = st.enter_context(nc.sbuf_tensor("cvt", [128, KC], F32))
            s_bf = st.enter_context(nc.sbuf_tensor("s_bf", [128, KC], BF16))
            P.dma("sync", ident_f[:], identf[:, :], [], S_ld)
            t_ld0 = P.dma("sync", cvt[:], cv[:, :], [], S_ld)
            P.op("vector", lambda e: e.tensor_copy(out=ident_b[:], in_=ident_f[:]), [t_ld0], S_dve)
            P.op("vector", lambda e: e.memset(eps_t[:], EPS), [], S_dve)
            t_s = P.op("scalar", lambda e: e.activation(out=s_bf[:], in_=cvt[:], func=AF.Silu), [t_ld0], S_act)
            wring = Ring(st, "wada", 2, [128, KC, 512], BF16)
            bring = Ring(st, "bada", 2, [1, 512], F32)
            mstg = mk_stager(st, "modst", 2, [1, 512], F32)
            wv = wada.rearrange("(kc p) n -> p kc n", p=128)
            psfree = [None, None]
            for cg in range(cfg.NMOD // 512):
                k, wt, ws_, wfree = wring.next()
                t_w = P.dma("gpsimd", wt[:], wv[:, :, cg * 512:(cg + 1) * 512], wfree, ws_)
                bk, bt, bs_, bfree = bring.next()
                t_b = P.dma("sync", bt[:], bada[0:1, cg * 512:(cg + 1) * 512], bfree, bs_)
                b = cg % 2
                tk = None
                for kc in range(KC):
                    tk = P.op("tensor", (lambda e, b=b, kc=kc, wt=wt: e.matmul(
                        psf[b][0:1, :], lhsT=s_bf[:, kc:kc + 1], rhs=wt[:, kc, :],
                        start=(kc == 0), stop=(kc == KC - 1))), [t_w, t_s, psfree[b]], S_pe if kc == KC - 1 else None)
                wring.free[k] = [tk]
                mt, mfr = mstg.get()
                t_ev = P.op("vector", (lambda e, b=b, mt=mt, bt=bt: e.tensor_tensor(
                    out=mt[0:1, :], in0=psf[b][0:1, :], in1=bt[0:1, :], op=ALU.add)), [tk, t_b] + mfr, S_dve)
                psfree[b] = t_ev
                bring.free[bk] = [t_ev]
                mstg.store(mod_d[0:1, cg * 512:(cg + 1) * 512], mt[0:1, :], [t_ev])
            dout("mod", mod_d)
            if end_stage("s0"):
                return finish()

        def modrow_ap(i):
            return mod_d[0:1, i * D:(i + 1) * D]

        def mod_tiles(st, gsrc, i_sc, i_sh, tag):
            mb, _ = bcast_load(st, tag + "mb", modrow_ap(i_sc), D)
            shb, _ = bcast_load(st, tag + "shb", modrow_ap(i_sh), D)
            gb, t3 = bcast_load(st, tag + "gb", gsrc[0:1, :], D)
            t_m = P.op("vector", lambda e: e.scalar_tensor_tensor(
                out=mb[:], in0=mb[:], scalar=1.0, in1=gb[:], op0=ALU.add, op1=ALU.mult), [t3], S_dve)
            return mb, shb, t_m

        with ExitStack() as st:
            mb, shb, t_m = mod_tiles(st, gpre, 1, 0, "s1")
            stg = mk_stager(st, "s1stg", 2, [128, KC, 512], BF16)
            cur = {}

            def emit1(tt, kc0, nk, ps, ready):
                if tt % 4 == 0 and kc0 == 0:
                    cur["t"], cur["fr"] = stg.get()
                t = cur["t"]
                tok = P.op("scalar", (lambda e, t=t, tt=tt, kc0=kc0, nk=nk, ps=ps: e.activation(
                    out=t[:, kc0:kc0 + nk, (tt % 4) * 128:(tt % 4 + 1) * 128],
                    in_=ps[:, 0:nk * 128].rearrange("p (k t) -> p k t", k=nk), func=AF.Copy)),
                    [ready] + cur["fr"], S_act)
                cur["fr"] = []
                if tt % 4 == 3 and kc0 + nk == KC:
                    stg.store(hT_d[tt // 4], t[:], [tok], eng="scalar")
                return tok

            P.op("vector", None, [t_m])
            frontend(st, xr, NTA, mb, shb, emit1, "s1")
            dout("hT", hT_d, BF16)
            if end_stage("s1"):
                return finish()

        with ExitStack() as st:
            hring = Ring(st, "s2h", 3, [128, KC, 512], BF16)
            wring = Ring(st, "s2w", 2, [128, KC, 512], BF16)
            stg = mk_stager(st, "s2stg", 3, [128, 512], BF16)
            wv = win.rearrange("(kc p) n -> p kc n", p=128)
            kcol0 = 2 * GW + GW
            vcol0 = 2 * GW + 2 * GW
            psfree = PSF
            for isv in (0, 1):
                for c0 in range(0, GW, 512):
                    cw = min(512, GW - c0)
                    wk, wt, ws_, wfree = wring.next()
                    cbase = (vcol0 if isv else kcol0) + c0
                    t_w = P.dma("gpsimd", wt[:, :, 0:cw], wv[:, :, cbase:cbase + cw], wfree, ws_)
                    t_last = None
                    for tg in range(NTA // 4):
                        hk, ht, hs_, hfree = hring.next()
                        t_h = P.dma("sync", ht[:], hT_d[tg], hfree, hs_)
                        if not isv:
                            for hh in range(cw // 128):
                                b = next_bank()
                                for kc in range(KC):
                                    t_last = P.op("tensor", (lambda e, b=b, kc=kc, wt=wt, ht=ht, hh=hh: e.matmul(
                                        psf[b][:, :], lhsT=wt[:, kc, hh * 128:(hh + 1) * 128], rhs=ht[:, kc, :],
                                        start=(kc == 0), stop=(kc == KC - 1))), [t_w, t_h, psfree.get(b)],
                                        S_pe if kc == KC - 1 else None)
                                sg_t, fr = stg.get()
                                tok = P.op("scalar", (lambda e, sg_t=sg_t, b=b: e.activation(
                                    out=sg_t[:], in_=psf[b][:, :], func=AF.Copy)), [t_last] + fr, S_act)
                                psfree[b] = tok
                                head = (c0 // 128) + hh
                                stg.store(kT_d[head][:, tg * 512:(tg + 1) * 512], sg_t[:], [tok], eng="scalar")
                        else:
                            for t4 in range(4):
                                b = next_bank()
                                for kc in range(KC):
                                    t_last = P.op("tensor", (lambda e, b=b, kc=kc, wt=wt, ht=ht, t4=t4, cw=cw: e.matmul(
                                        psf[b][:, 0:cw], lhsT=ht[:, kc, t4 * 128:(t4 + 1) * 128], rhs=wt[:, kc, 0:cw],
                                        start=(kc == 0), stop=(kc == KC - 1))), [t_w, t_h, psfree.get(b)],
                                        S_pe if kc == KC - 1 else None)
                                sg_t, fr = stg.get()
                                tok = P.op("vector", (lambda e, sg_t=sg_t, b=b, cw=cw: e.tensor_copy(
                                    out=sg_t[:, 0:cw], in_=psf[b][:, 0:cw])), [t_last] + fr, S_dve)
                                psfree[b] = tok
                                r0 = tg * 512 + t4 * 128
                                stg.store(v_d[r0:r0 + 128, c0:c0 + cw], sg_t[:, 0:cw], [tok], eng="scalar")
                        hring.free[hk] = [t_last]
                    wring.free[wk] = [t_last]
            dout("kT", kT_d, BF16)
            dout("v", v_d, BF16)
            if end_stage("s2"):
                return finish()

        with ExitStack() as st:
            hT_own = st.enter_context(nc.sbuf_tensor("hT_own", [128, KC, TO], BF16))
            with ExitStack() as st2:
                mb, shb, t_m = mod_tiles(st2, gpre, 1, 0, "s3")

                def emit3(tt, kc0, nk, ps, ready):
                    return P.op("scalar", (lambda e, tt=tt, kc0=kc0, nk=nk, ps=ps: e.activation(
                        out=hT_own[:, kc0:kc0 + nk, tt * 128:(tt + 1) * 128],
                        in_=ps[:, 0:nk * 128].rearrange("p (k t) -> p k t", k=nk), func=AF.Copy)), [ready], S_act)

                P.op("vector", None, [t_m])
                frontend(st2, xo, NTO, mb, shb, emit3, "s3")
                if end_stage("s3a"):
                    return finish()
            stg_b = mk_stager(st, "s3sb", 3, [128, 512], BF16)
            stg_f = mk_stager(st, "s3sf", 3, [128, 512], F32)

            def ep_act(func, dst_d, stg):
                def ep(tt, c0, cw, ps, ready):
                    t, fr = stg.get()
                    tok = P.op("scalar", (lambda e, t=t, ps=ps, cw=cw: e.activation(
                        out=t[:, 0:cw], in_=ps[:, 0:cw], func=func)), [ready] + fr, S_act)
                    stg.store(dst_d[tt * 128:(tt + 1) * 128, c0:c0 + cw], t[:, 0:cw], [tok], eng="scalar")
                    return tok
                return ep

            wring = Ring(st, "s3w", 2, [128, KC, 512], BF16)
            s3n = int(os.environ.get("MK_S3N", "5"))
            if s3n >= 1 and not os.environ.get("MK_SKIPU"):
                gemm(wring, hT_own, KC, win, 0, 0, GW, ep_act(AF.Gelu, u_d, stg_b))
            if s3n >= 2:
                gemm(wring, hT_own, KC, win, 0, GW, GW, ep_act(AF.Gelu, vr_d, stg_f))
            if s3n >= 3:
                gemm(wring, hT_own, KC, win, 0, 2 * GW + 3 * GW, D, ep_act(AF.Sigmoid, sga_d, stg_b))
            if s3n >= 4:
                gemm(wring, hT_own, KC, win, 0, 2 * GW + 3 * GW + D, D, ep_act(AF.Sigmoid, sgb_d, stg_b))
            wv = win.rearrange("(kc p) n -> p kc n", p=128)
            psfree = PSF
            for hh in range(NH if s3n >= 5 else 0):
                wk, wt, ws_, wfree = wring.next()
                cb = 2 * GW + hh * 128
                t_w = P.dma("gpsimd", wt[:, :, 0:128], wv[:, :, cb:cb + 128], wfree, ws_)
                t_last = None
                for t0 in range(0, TO, 512):
                    tw = min(512, TO - t0)
                    b = next_bank()
                    for kc in range(KC):
                        t_last = P.op("tensor", (lambda e, b=b, kc=kc, wt=wt, t0=t0, tw=tw: e.matmul(
                            psf[b][:, 0:tw], lhsT=wt[:, kc, 0:128], rhs=hT_own[:, kc, t0:t0 + tw],
                            start=(kc == 0), stop=(kc == KC - 1))), [t_w, psfree.get(b)],
                            S_pe if kc == KC - 1 else None)
                    t, fr = stg_b.get()
                    tok = P.op("scalar", (lambda e, t=t, b=b, tw=tw: e.activation(
                        out=t[:, 0:tw], in_=psf[b][:, 0:tw], func=AF.Copy, scale=float(1.0 / np.sqrt(128.0)))),
                        [t_last] + fr, S_act)
                    psfree[b] = tok
                    stg_b.store(qT_d[hh][:, t0:t0 + tw], t[:, 0:tw], [tok], eng="scalar")
                wring.free[wk] = [t_last]
            dout("u", u_d, BF16)
            dout("vr", vr_d, F32)
            dout("qT", qT_d, BF16)
            dout("sga", sga_d, BF16)
            if end_stage("s3"):
                return finish()

        with ExitStack() as st:
            gvb, t_g = bcast_load(st, "s4gv", gv[0:1, :], GW)
            wst = st.enter_context(nc.sbuf_tensor("s4ws", [128, NH, 128], F32))
            wsb = st.enter_context(nc.sbuf_tensor("s4wsb", [128, NH, 128], BF16))
            bst = st.enter_context(nc.sbuf_tensor("s4bs", [128, NH], F32))
            P.dma("sync", wst[:], wsT[:, :, :], [], S_ld)
            t_b = P.dma("sync", bst[:], bsr[:, :], [], S_ld)
            t_g = t_b
            t_ms = P.op("vector", lambda e: e.memset(wst[0:64, :, 64:128], 0.0), [t_b], S_dve)
            t_wb = P.op("vector", lambda e: e.tensor_copy(out=wsb[:], in_=wst[:]), [t_ms], S_dve)
            vring = Ring(st, "s4v", 2, [128, GW], F32)
            uring = Ring(st, "s4u", 2, [128, GW], BF16)
            vnr = Ring(st, "s4vn", 2, [128, GW], BF16)
            stg = mk_stager(st, "s4stg", 2, [128, GW], BF16)
            stats = st.enter_context(nc.sbuf_tensor("s4stats", [128, 8 * NTO], F32))
            junk = st.enter_context(nc.sbuf_tensor("s4junk", [128, GW], BF16))
            psfree = PSF
            for tt in range(NTO):
                vk, vt, vs_, vfree = vring.next()
                t_v = P.dma("sync", vt[:], vr_d[tt * 128:(tt + 1) * 128, :], vfree, vs_)
                uk, ut, us_, ufree = uring.next()
                t_u = P.dma("sync", ut[:], u_d[tt * 128:(tt + 1) * 128, :], ufree, us_)
                sc = stats[:, tt * 8:(tt + 1) * 8]
                t_a1 = P.op("scalar", (lambda e, vt=vt, sc=sc: e.activation(
                    out=junk[:], in_=vt[:], func=AF.Copy, scale=1.0 / GW, accum_out=sc[:, 0:1])), [t_v], S_act)
                t_a2 = P.op("scalar", (lambda e, vt=vt, sc=sc: e.activation(
                    out=junk[:], in_=vt[:], func=AF.Square, scale=float(1.0 / np.sqrt(GW)), accum_out=sc[:, 1:2])),
                    [t_a1], S_act)
                t_x = P.op("vector", (lambda e, sc=sc: e.tensor_tensor(
                    out=sc[:, 2:3], in0=sc[:, 0:1], in1=sc[:, 0:1], op=ALU.mult)), [t_a2], S_dve)
                t_var = P.op("vector", (lambda e, sc=sc: e.tensor_tensor(
                    out=sc[:, 3:4], in0=sc[:, 1:2], in1=sc[:, 2:3], op=ALU.subtract)), [t_x], S_dve)
                t_sd = P.op("scalar", (lambda e, sc=sc: e.activation(
                    out=sc[:, 4:5], in_=sc[:, 3:4], func=AF.Sqrt, bias=eps_t[:, 0:1])), [t_var], S_act)
                t_iv = P.op("vector", (lambda e, sc=sc: e.reciprocal(out=sc[:, 5:6], in_=sc[:, 4:5])), [t_sd], S_dve)
                t_n0 = P.op("vector", (lambda e, vt=vt, sc=sc: e.tensor_scalar(
                    out=vt[:], in0=vt[:], scalar1=sc[:, 0:1], scalar2=sc[:, 5:6], op0=ALU.subtract, op1=ALU.mult)),
                    [t_iv], S_dve)
                nk, vn, _, nfree = vnr.next()
                t_vn = P.op("vector", (lambda e, vt=vt, vn=vn: e.tensor_tensor(
                    out=vn[:], in0=vt[:], in1=gvb[:], op=ALU.mult)), [t_g, t_n0] + nfree, S_dve)
                vring.free[vk] = [t_vn]
                gt_, gfr = stg.get()
                t_last = None
                tok = None
                for g in range(NH):
                    b = next_bank()
                    t_last = P.op("tensor", (lambda e, b=b, g=g, vn=vn: e.matmul(
                        psf[b][:, 0:128], lhsT=wsb[:, g, :], rhs=vn[:, g * 128:(g + 1) * 128], start=True, stop=True)),
                        [t_vn, t_wb, psfree.get(b)], S_pe)
                    tok = P.op("vector", (lambda e, b=b, g=g, ut=ut, gt_=gt_: e.scalar_tensor_tensor(
                        out=gt_[:, g * 128:(g + 1) * 128], in0=psf[b][:, 0:128], scalar=bst[:, g:g + 1],
                        in1=ut[:, g * 128:(g + 1) * 128], op0=ALU.add, op1=ALU.mult)), [t_last, t_u] + gfr, S_dve)
                    gfr = []
                    psfree[b] = tok
                vnr.free[nk] = [t_last]
                uring.free[uk] = [tok]
                stg.store(gm_d[tt * 128:(tt + 1) * 128, :], gt_[:], [tok])
            dout("gm", gm_d, BF16)
            if end_stage("s4"):
                return finish()

        with ExitStack() as st:
            G = 3
            kpb, t_kp = bcast_load(st, "s5kp", kpos[0:1, :], S)
            qp = st.enter_context(nc.sbuf_tensor("s5qp", [128, NTO], F32))
            t_qp = P.dma("sync", qp[:], qpos[:, :], [], S_ld)
            t_c = [t_kp, t_qp]
            qring = Ring(st, "s5q", 2, [128, TO], BF16)
            kring = Ring(st, "s5k", 2, [128, S], BF16)
            vring = Ring(st, "s5v", 2, [128, NTA, 128], BF16)
            RD = 2 * G
            ering = Ring(st, "s5e", RD, [128, 512], F32, sems=False)
            lring = Ring(st, "s5l", RD, [128, 512], F32, sems=False)
            cring = Ring(st, "s5c", RD, [128, 512], F32, sems=False)
            wring = Ring(st, "s5w", RD, [128, 512], F32, sems=False)
            aring = Ring(st, "s5a", RD, [128, 512], BF16, sems=False)
            atring = Ring(st, "s5at", RD, [128, 4, 128], BF16, sems=False)
            zeros = st.enter_context(nc.sbuf_tensor("s5z", [128, 512], F32))
            carry0 = st.enter_context(nc.sbuf_tensor("s5carry0", [128, 1], F32))
            P.op("vector", lambda e: e.memset(zeros[:], 0.0), [], S_dve)
            t_z0 = P.op("vector", lambda e: e.memset(carry0[:], 0.0), [], S_dve)
            ostg = mk_stager(st, "s5o", 4, [128, 128], BF16)
            zbanks = [2, 3, 4]
            obanks = [0, 1, 5]
            ofree = {}
            tfree = [None, None]
            tcnt = [0]
            NCH = S // 512
            items = [(hh, qb) for hh in range(NH) for qb in range(NTO)]
            head = {}

            def load_head(hh):
                qk, qt_, qs_, qfree = qring.next()
                t_q = P.dma("sync", qt_[:], qT_d[hh], qfree, qs_)
                kk, kt, ks_, kfree = kring.next()
                t_k = P.dma("sync", kt[:], kT_d[hh], kfree, ks_)
                vk, vt, vs_, vfree = vring.next()
                vsrc = v_d[:, hh * 128:(hh + 1) * 128].rearrange("(b s) d -> s b d", s=128)
                nsp = max(1, NTA // 16)
                t_v = None
                for sp in range(nsp):
                    b0, b1 = sp * NTA // nsp, (sp + 1) * NTA // nsp
                    t_v = P.dma("sync", vt[:, b0:b1, :], vsrc[:, b0:b1, :], vfree, vs_)
                head[hh] = dict(qt=qt_, kt=kt, vt=vt, t_q=t_q, t_k=t_k, t_v=t_v, qk=qk, kk=kk, vk=vk)

            def front(c, ch):
                hd = head[c["hh"]]
                zb = c["zb"]
                qb = c["qb"]
                t_z = P.op("tensor", (lambda e, zb=zb, qb=qb, hd=hd, ch=ch: e.matmul(
                    psf[zb][:, :], lhsT=hd["qt"][:, qb * 128:(qb + 1) * 128], rhs=hd["kt"][:, ch * 512:(ch + 1) * 512],
                    start=True, stop=True)), [hd["t_q"], hd["t_k"], PSF.get(zb)], S_pe)
                c["t_z"] = t_z

            def front2(c, ch):
                zb = c["zb"]
                ek, et, _, efree = ering.next()
                t_e = P.op("scalar", (lambda e, et=et, zb=zb: e.activation(
                    out=et[:], in_=psf[zb][:, :], func=AF.Exp)), [c["t_z"]] + efree, S_act)
                PSF[zb] = t_e
                c["e"] = (ek, et)
                c["t_e"] = t_e

            def front3(c, ch):
                ek, et = c["e"]
                qb = c["qb"]
                c["t_em"] = P.op("vector", (lambda e, et=et, ch=ch, qb=qb: e.scalar_tensor_tensor(
                    out=et[:], in0=kpb[:, ch * 512:(ch + 1) * 512], scalar=qp[:, qb:qb + 1], in1=et[:],
                    op0=ALU.is_gt, op1=ALU.mult)), [c["t_e"]] + t_c, S_dve)

            def front4(c, ch):
                ek, et = c["e"]
                lk, lt, _, lfree = lring.next()
                c["t_l"] = P.op("scalar", (lambda e, et=et, lt=lt: e.activation(
                    out=lt[:], in_=et[:], func=AF.Ln, bias=1.0)), [c["t_em"]] + lfree, S_act)
                c["l"] = (lk, lt)
                c["nxt"] = dict(e=c["e"], l=c["l"], ch=ch, t_em_=c["t_em"])

            def back1(c):
                p = c["pend"]
                lk, lt = p["l"]
                ck, ct, _, cfree = cring.next()
                init = carry0[:, 0:1] if c["prev_c"] is None else c["prev_c"][:, 511:512]
                t_sc = P.op("vector", (lambda e, lt=lt, ct=ct, init=init: e.tensor_tensor_scan(
                    out=ct[:], data0=lt[:], data1=zeros[:], initial=init, op0=ALU.add, op1=ALU.add)),
                    [c["t_l"], c["t_prev_sc"]] + cfree, S_dve)
                c["t_prev_sc"] = t_sc
                lring.free[lk] = [t_sc]
                c["prev_c"] = ct
                p["c"] = (ck, ct)
                p["t_sc"] = t_sc

            def back2(c):
                p = c["pend"]
                ck, ct = p["c"]
                wk, wt, _, wfree = wring.next()
                t_w = P.op("scalar", (lambda e, ct=ct, wt=wt: e.activation(
                    out=wt[:], in_=ct[:], func=AF.Exp, scale=-1.0)), [p["t_sc"]] + wfree, S_act)
                cring.free[ck] = [t_w]
                p["w"] = (wk, wt)
                p["t_w"] = t_w

            def back3(c):
                p = c["pend"]
                ek, et = p["e"]
                wk, wt = p["w"]
                ak, at, _, afree = aring.next()
                if os.environ.get("MK_NOPOOL"):
                    t_a = P.op("vector", (lambda e, et=et, wt=wt, at=at: e.tensor_tensor(
                        out=at[:], in0=et[:], in1=wt[:], op=ALU.mult)), [p["t_w"]] + afree, S_dve)
                else:
                    t_a = P.op("gpsimd", (lambda e, et=et, wt=wt, at=at: e.tensor_tensor(
                        out=at[:], in0=et[:], in1=wt[:], op=ALU.mult)), [p["t_w"], p["t_em_"]] + afree, S_pool)
                ering.free[ek] = [t_a]
                wring.free[wk] = [t_a]
                p["a"] = (ak, at)
                p["t_a"] = t_a

            def back4(c):
                p = c["pend"]
                ak, at = p["a"]
                tb = tcnt[0] % 2
                tcnt[0] += 1
                t_tr = None
                for j in range(4):
                    t_tr = P.op("tensor", (lambda e, tb=tb, j=j, at=at: e.transpose(
                        out=psbs[tb][:, j * 128:(j + 1) * 128], in_=at[:, j * 128:(j + 1) * 128],
                        identity=ident_b[:])), [p["t_a"], tfree[tb]], S_pe if j == 3 else None)
                aring.free[ak] = [t_tr]
                p["tb"] = tb
                p["t_tr"] = t_tr

            def back5(c):
                p = c["pend"]
                tb = p["tb"]
                tk_, att, _, atfree = atring.next()
                t_at = P.op("scalar", (lambda e, tb=tb, att=att: e.activation(
                    out=att[:], in_=psbs[tb][:, 0:512].rearrange("p (j t) -> p j t", j=4),
                    func=AF.Copy)), [p["t_tr"]] + atfree, S_act)
                tfree[tb] = t_at
                p["at"] = (tk_, att)
                p["t_at"] = t_at

            def back6(c):
                p = c["pend"]
                tk_, att = p["at"]
                hd = head[c["hh"]]
                ob = c["ob"]
                ch = p["ch"]
                last = None
                for j in range(4):
                    n_av = c["n_av"]
                    last = P.op("tensor", (lambda e, ob=ob, j=j, att=att, hd=hd, ch=ch, n_av=n_av: e.matmul(
                        psf[ob][:, 0:128], lhsT=att[:, j, :], rhs=hd["vt"][:, ch * 4 + j, :],
                        start=(n_av == 0), stop=(n_av == c["nav_tot"] - 1))),
                        [p["t_at"], hd["t_v"], ofree.get(ob) if n_av == 0 else None],
                        S_pe if (j == 3) else None)
                    c["n_av"] += 1
                atring.free[tk_] = [last]
                c["last_pe"] = last

            loaded = set()
            last_of_head = {}
            for gi in range(0, len(items), G):
                grp = items[gi:gi + G]
                for (hh, qb) in grp:
                    if hh not in loaded:
                        load_head(hh)
                        loaded.add(hh)
                chains = [dict(hh=hh, qb=qb, zb=zbanks[i], ob=obanks[i], prev_c=None, t_prev_sc=t_z0, n_av=0,
                               ch0=(cfg.NC * qb * 128) // 512)
                          for i, (hh, qb) in enumerate(grp)]
                for c in chains:
                    c["nav_tot"] = (NCH - c["ch0"]) * 4
                for i in range(min(c["ch0"] for c in chains), NCH + 1):
                    fr_ = [c for c in chains if c["ch0"] <= i < NCH]
                    bk_ = [c for c in chains if c["ch0"] <= i - 1]
                    for c in bk_:
                        back1(c)
                    for c in fr_:
                        front(c, i)
                    for c in fr_:
                        front2(c, i)
                    for c in bk_:
                        back2(c)
                    for c in fr_:
                        front3(c, i)
                    for c in bk_:
                        back3(c)
                    for c in fr_:
                        front4(c, i)
                    for c in bk_:
                        back4(c)
                        back5(c)
                    for c in bk_:
                        back6(c)
                    for c in fr_:
                        c["pend"] = c["nxt"]
                for c in chains:
                    ot, ofr = ostg.get()
                    ob = c["ob"]
                    t_o = P.op("vector", (lambda e, ot=ot, ob=ob: e.tensor_copy(out=ot[:], in_=psf[ob][:, 0:128])),
                               [c["last_pe"]] + ofr, S_dve)
                    ofree[ob] = t_o
                    ostg.store(o_d[c["qb"] * 128:(c["qb"] + 1) * 128, c["hh"] * 128:(c["hh"] + 1) * 128], ot[:], [t_o])
                    last_of_head[c["hh"]] = c["last_pe"]
                done_heads = [hh for hh in list(head.keys()) if all((hh, qb) in items[:gi + G] for qb in range(NTO))]
                for hh in done_heads:
                    hd = head.pop(hh)
                    tok = last_of_head[hh]
                    qring.free[hd["qk"]] = [tok]
                    kring.free[hd["kk"]] = [tok]
                    vring.free[hd["vk"]] = [tok]
            dout("o", o_d, BF16)
            if end_stage("s5"):
                return finish()

        for which in (() if os.environ.get('MK_SKIP6') else (0, 1)):
            with ExitStack() as st:
                actT = st.enter_context(nc.sbuf_tensor(f"s6act{which}", [128, GKC, TO], BF16))
                src = gm_d if which == 0 else o_d
                with ExitStack() as st2:
                    load_T(st2, src, GW, actT, [], f"s6l{which}")
                    if end_stage(f"s6l{which}"):
                        return finish()
                sgr = Ring(st, f"s6sg{which}", 3, [128, 512], BF16)
                t1r = Ring(st, f"s6t1{which}", 3, [128, 512], F32)
                stg_f = mk_stager(st, f"s6sf{which}", 3, [128, 512], F32)
                stg_b = mk_stager(st, f"s6sb{which}", 3, [128, 512], BF16)
                sg_src = sga_d if which == 0 else sgb_d

                def ep6(tt, c0, cw, ps, ready, which=which, sgr=sgr, t1r=t1r, stg_f=stg_f, stg_b=stg_b, sg_src=sg_src):
                    ep6m = int(os.environ.get("MK_EP6", "2"))
                    if ep6m == 0:
                        t, fr = stg_f.get()
                        tok = P.op("scalar", (lambda e, t=t, ps=ps, cw=cw: e.activation(
                            out=t[:, 0:cw], in_=ps[:, 0:cw], func=AF.Copy)), [ready] + fr, S_act)
                        stg_f.store(t1_d[tt * 128:(tt + 1) * 128, c0:c0 + cw], t[:, 0:cw], [tok], eng="scalar")
                        return tok
                    gk, gt_, gs_, gfree = sgr.next()
                    t_g = P.dma("sync", gt_[:, 0:cw], sg_src[tt * 128:(tt + 1) * 128, c0:c0 + cw], gfree, gs_)
                    if ep6m == 1:
                        t, fr = stg_f.get()
                        tok = P.op("scalar", (lambda e, t=t, ps=ps, cw=cw: e.activation(
                            out=t[:, 0:cw], in_=ps[:, 0:cw], func=AF.Copy)), [ready, t_g] + fr, S_act)
                        sgr.free[gk] = [tok]
                        stg_f.store(t1_d[tt * 128:(tt + 1) * 128, c0:c0 + cw], t[:, 0:cw], [tok], eng="scalar")
                        return tok
                    if which == 0:
                        t, fr = stg_f.get()
                        tok = P.op("vector", (lambda e, t=t, ps=ps, gt_=gt_, cw=cw: e.tensor_tensor(
                            out=t[:, 0:cw], in0=ps[:, 0:cw], in1=gt_[:, 0:cw], op=ALU.mult)), [ready, t_g] + fr, S_dve)
                        sgr.free[gk] = [tok]
                        stg_f.store(t1_d[tt * 128:(tt + 1) * 128, c0:c0 + cw], t[:, 0:cw], [tok], eng="scalar")
                        return tok
                    k1, t1, s1, f1 = t1r.next()
                    t_1 = P.dma("sync", t1[:, 0:cw], t1_d[tt * 128:(tt + 1) * 128, c0:c0 + cw], f1, s1)
                    t, fr = stg_b.get()
                    tok_a = P.op("vector", (lambda e, ps=ps, gt_=gt_, cw=cw: e.tensor_tensor(
                        out=gt_[:, 0:cw], in0=ps[:, 0:cw], in1=gt_[:, 0:cw], op=ALU.mult)), [ready, t_g], S_dve)
                    tok_b = P.op("vector", (lambda e, t=t, t1=t1, gt_=gt_, cw=cw: e.tensor_tensor(
                        out=t[:, 0:cw], in0=gt_[:, 0:cw], in1=t1[:, 0:cw], op=ALU.add)), [t_1, tok_a] + fr, S_dve)
                    sgr.free[gk] = [tok_b]
                    t1r.free[k1] = [tok_b]
                    stg_b.store(mix_d[tt * 128:(tt + 1) * 128, c0:c0 + cw], t[:, 0:cw], [tok_b], eng="scalar")
                    return tok_a

                wring = Ring(st, f"s6w{which}", 2, [128, GKC, 512], BF16)
                gemm(wring, actT, GKC, wpa if which == 0 else wpb, 0, 0, D, ep6)
                if which == 1:
                    dout("mix", mix_d, BF16)
                if end_stage(f"s6{which}"):
                    return finish()

        with ExitStack() as st:
            actT = st.enter_context(nc.sbuf_tensor("s8act", [128, KC, TO], BF16))
            with ExitStack() as st2:
                load_T(st2, (sga_d if os.environ.get("MK_S8SRC") else mix_d)[:, 0:int(os.environ.get("MK_LTK", D))], int(os.environ.get("MK_LTK", D)), actT, [], "s8l")
                if end_stage("s8l"):
                    return finish()
            stg_f = mk_stager(st, "s8sf", 3, [128, 512], F32)

            def ep8(tt, c0, cw, ps, ready):
                t, fr = stg_f.get()
                tok = P.op("scalar", (lambda e, t=t, ps=ps, cw=cw: e.activation(
                    out=t[:, 0:cw], in_=ps[:, 0:cw], func=AF.Copy)), [ready] + fr, S_act)
                stg_f.store(mo_d[tt * 128:(tt + 1) * 128, c0:c0 + cw], t[:, 0:cw], [tok], eng="scalar")
                return tok

            wring = Ring(st, "s8w", 2, [128, KC, 512], BF16)
            gemm(wring, actT, KC, wo, 0, 0, D, ep8)
            dout("mo", mo_d, F32)
            if end_stage("s8"):
                return finish()

        def post_norm_residual(st, br_d, res_d, gsrc, i_gt, dst_d, tag):
            gtb, _ = bcast_load(st, tag + "gt", modrow_ap(i_gt), D)
            gb, t3 = bcast_load(st, tag + "g", gsrc[0:1, :], D)
            t_gg = P.op("vector", lambda e: e.tensor_tensor(out=gtb[:], in0=gtb[:], in1=gb[:], op=ALU.mult), [t3], S_dve)
            bring = Ring(st, tag + "b", 2, [128, D], F32)
            rring = Ring(st, tag + "r", 2, [128, D], F32)
            junk = st.enter_context(nc.sbuf_tensor(tag + "junk", [128, D], BF16))
            stt = st.enter_context(nc.sbuf_tensor(tag + "st", [128, 4 * NTO], F32))
            stg = mk_stager(st, tag + "o", 2, [128, D], F32)
            for tt in range(NTO):
                bk, bt, bs_, bfree = bring.next()
                t_b = P.dma("sync", bt[:], br_d[tt * 128:(tt + 1) * 128, :], bfree, bs_)
                rk, rt, rs_, rfree = rring.next()
                t_r = P.dma("sync", rt[:], res_d[tt * 128:(tt + 1) * 128, :], rfree, rs_)
                sc = stt[:, tt * 4:(tt + 1) * 4]
                t_q = P.op("scalar", (lambda e, bt=bt, sc=sc: e.activation(
                    out=junk[:], in_=bt[:], func=AF.Square, accum_out=sc[:, 0:1])), [t_b], S_act)
                t_sd = P.op("scalar", (lambda e, sc=sc: e.activation(
                    out=sc[:, 1:2], in_=sc[:, 0:1], func=AF.Sqrt, scale=1.0 / D, bias=eps_t[:, 0:1])), [t_q], S_act)
                t_iv = P.op("vector", (lambda e, sc=sc: e.reciprocal(out=sc[:, 2:3], in_=sc[:, 1:2])), [t_sd], S_dve)
                t_m = P.op("vector", (lambda e, bt=bt, sc=sc: e.scalar_tensor_tensor(
                    out=bt[:], in0=bt[:], scalar=sc[:, 2:3], in1=gtb[:], op0=ALU.mult, op1=ALU.mult)),
                    [t_gg, t_iv], S_dve)
                ot, ofr = stg.get()
                tok = P.op("vector", (lambda e, bt=bt, rt=rt, ot=ot: e.tensor_tensor(
                    out=ot[:], in0=bt[:], in1=rt[:], op=ALU.add)), [t_r, t_m] + ofr, S_dve)
                bring.free[bk] = [tok]
                rring.free[rk] = [tok]
                stg.store(dst_d[tt * 128:(tt + 1) * 128, :], ot[:], [tok])

        with ExitStack() as st:
            post_norm_residual(st, mo_d, xo, gpost, 2, x1_d, "s9")
            dout("x1", x1_d, F32)
            if end_stage("s9"):
                return finish()

        with ExitStack() as st:
            h2T = st.enter_context(nc.sbuf_tensor("h2T", [128, KC, TO], BF16))
            with ExitStack() as st2:
                mb, shb, t_m = mod_tiles(st2, gpre2, 4, 3, "s10")

                def emit10(tt, kc0, nk, ps, ready):
                    return P.op("scalar", (lambda e, tt=tt, kc0=kc0, nk=nk, ps=ps: e.activation(
                        out=h2T[:, kc0:kc0 + nk, tt * 128:(tt + 1) * 128],
                        in_=ps[:, 0:nk * 128].rearrange("p (k t) -> p k t", k=nk), func=AF.Copy)), [ready], S_act)

                P.op("vector", None, [t_m])
                frontend(st2, x1_d, NTO, mb, shb, emit10, "s10")
                if end_stage("s10a"):
                    return finish()
            rr = Ring(st, "s10r", 3, [128, 512], F32)
            stg_b = mk_stager(st, "s10sb", 3, [128, 512], BF16)

            def ep10(tt, c0, cw, ps, ready):
                rk, rt, _, rfree = rr.next()
                tok = P.op("scalar", (lambda e, rt=rt, ps=ps, cw=cw: e.activation(
                    out=rt[:, 0:cw], in_=ps[:, 0:cw], func=AF.Relu)), [ready] + rfree, S_act)
                t, fr = stg_b.get()
                tok2 = P.op("vector", (lambda e, t=t, rt=rt, cw=cw: e.tensor_tensor(
                    out=t[:, 0:cw], in0=rt[:, 0:cw], in1=rt[:, 0:cw], op=ALU.mult)), [tok] + fr, S_dve)
                rr.free[rk] = [tok2]
                stg_b.store(f_d[tt * 128:(tt + 1) * 128, c0:c0 + cw], t[:, 0:cw], [tok2], eng="scalar")
                return tok

            wring = Ring(st, "s10w", 2, [128, KC, 512], BF16)
            gemm(wring, h2T, KC, wf1, 0, 0, DFF, ep10)
            dout("f", f_d, BF16)
            if end_stage("s10"):
                return finish()

        for kb in range(DFF // D):
            with ExitStack() as st:
                actT = st.enter_context(nc.sbuf_tensor(f"s11act{kb}", [128, KC, TO], BF16))
                with ExitStack() as st2:
                    load_T(st2, f_d[:, kb * D:(kb + 1) * D], D, actT, [], f"s11l{kb}")
                    if end_stage(f"s11l{kb}"):
                        return finish()
                ar = Ring(st, f"s11a{kb}", 3, [128, 512], F32)
                stg_f = mk_stager(st, f"s11sf{kb}", 3, [128, 512], F32)

                def ep11(tt, c0, cw, ps, ready, kb=kb, ar=ar, stg_f=stg_f):
                    t, fr = stg_f.get()
                    if kb == 0:
                        tok = P.op("scalar", (lambda e, t=t, ps=ps, cw=cw: e.activation(
                            out=t[:, 0:cw], in_=ps[:, 0:cw], func=AF.Copy)), [ready] + fr, S_act)
                    else:
                        ak, at, as_, afree = ar.next()
                        t_a = P.dma("sync", at[:, 0:cw], acc_d[tt * 128:(tt + 1) * 128, c0:c0 + cw], afree, as_)
                        tok = P.op("vector", (lambda e, t=t, ps=ps, at=at, cw=cw: e.tensor_tensor(
                            out=t[:, 0:cw], in0=ps[:, 0:cw], in1=at[:, 0:cw], op=ALU.add)), [ready, t_a] + fr, S_dve)
                        ar.free[ak] = [tok]
                    stg_f.store(acc_d[tt * 128:(tt + 1) * 128, c0:c0 + cw], t[:, 0:cw], [tok], eng="scalar")
                    return tok

                wring = Ring(st, f"s11w{kb}", 2, [128, KC, 512], BF16)
                gemm(wring, actT, KC, wf2, kb * D, 0, D, ep11)
                if end_stage(f"s11{kb}"):
                    return finish()

        with ExitStack() as st:
            post_norm_residual(st, acc_d, x1_d, gpost2, 5, y, "s12")
            end_stage("s12")
        return finish()


def own_rows(cfg, cid):
    blk = cfg.NC * np.arange(cfg.NTO) + cid
    return (blk[:, None] * 128 + np.arange(128)[None, :]).reshape(-1)


def prep_inputs(cfg, x, c, w_ada, b_ada, g_pre_mix, w_in, g_v, w_s, b_s, w_proj_a, w_proj_b, w_o,
                g_post_mix, g_pre_mlp, w_ff1, w_ff2, g_post_mlp):
    f = lambda a: np.ascontiguousarray(np.asarray(a, dtype=np.float32))
    xr = f(np.asarray(x)[0, ::-1, :])
    common = {
        "xr": xr,
        "cv": f(np.asarray(c)[0].reshape(cfg.KC, 128).T),
        "wada": f(np.asarray(w_ada)[0]),
        "bada": f(np.asarray(b_ada)[0][None, :]),
        "win": f(np.asarray(w_in)[0]),
        "gpre": f(np.asarray(g_pre_mix)[0][None, :]),
        "gv": f(np.asarray(g_v)[0][None, :]),
        "wsT": f(np.asarray(w_s)[0][:, ::-1, ::-1].transpose(2, 0, 1)),
        "bsr": f(np.asarray(b_s)[0][:, ::-1].T),
        "wpa": f(np.asarray(w_proj_a)[0]),
        "wpb": f(np.asarray(w_proj_b)[0]),
        "wo": f(np.asarray(w_o)[0]),
        "gpost": f(np.asarray(g_post_mix)[0][None, :]),
        "gpre2": f(np.asarray(g_pre_mlp)[0][None, :]),
        "wf1": f(np.asarray(w_ff1)[0]),
        "wf2": f(np.asarray(w_ff2)[0]),
        "gpost2": f(np.asarray(g_post_mlp)[0][None, :]),
        "identf": np.eye(128, dtype=np.float32),
        "kpos": np.arange(cfg.S, dtype=np.float32)[None, :],
    }
    in_maps = []
    for cid in range(cfg.NC):
        m = dict(common)
        rows = own_rows(cfg, cid)
        m["xo"] = f(xr[rows])
        m["qpos"] = f(rows.astype(np.float32).reshape(cfg.NTO, 128).T)
        in_maps.append(m)
    return in_maps


_CACHE = {}


def run(cfg, inputs, stop_after=None, debug=False, trace=False):
    key = (cfg.D, cfg.S, cfg.NC, stop_after, debug)
    if key not in _CACHE:
        _CACHE[key] = build_nc(cfg, stop_after, debug)
    nc = _CACHE[key]
    in_maps = prep_inputs(cfg, **inputs)
    return run_bass_kernel_spmd(nc, in_maps, core_ids=list(range(cfg.NC)), trace=trace)


def kernel(**inputs):
    cfg = Cfg()
    res = run(cfg, inputs)
    return assemble(cfg, [np.asarray(r["y"]) for r in res.results])


def assemble(cfg, ys):
    yr = np.empty((cfg.S, cfg.D), np.float32)
    for cid, yc in enumerate(ys):
        yr[own_rows(cfg, cid)] = yc
    return np.ascontiguousarray(yr[::-1])[None].astype(np.float32)
```

```python
import os
from contextlib import ExitStack
import numpy as np
import concourse.bass as bass
import concourse.mybir as mybir
from concourse.bass_utils import run_bass_kernel_spmd

F32 = mybir.dt.float32
BF16 = mybir.dt.bfloat16
AF = mybir.ActivationFunctionType
ALU = mybir.AluOpType
EPS = 1e-6
ENGS = ("sync", "scalar", "gpsimd", "vector", "tensor")


class Cfg:
    def __init__(self, D=4096, S=8192, NC=8):
        self.D, self.S, self.NC = D, S, NC
        self.KC = D // 128
        self.TO = S // NC
        self.NTO = self.TO // 128
        self.NTA = S // 128
        self.GW = D // 2
        self.NH = self.GW // 128
        self.DFF = 4 * D
        self.NMOD = 6 * D
        self.INC = 2 * self.GW + 3 * self.GW + 2 * D


class CS:
    def __init__(self, h):
        self.h = h
        self.n = 0


class Prog:
    def __init__(self, nc):
        self.nc = nc
        self.q = {e: [] for e in ENGS}
        self.waited = {e: {} for e in ENGS}
        self.nblk = 0

    def op(self, eng, fn, waits=(), sig=None, amt=1):
        tok = None
        if sig is not None:
            sig.n += amt
            tok = (sig, sig.n)
        ws = []
        seen = self.waited[eng]
        stack = list(waits)
        while stack:
            w = stack.pop()
            if w is None:
                continue
            if isinstance(w, list):
                stack.extend(w)
                continue
            s, v = w
            if seen.get(id(s), 0) >= v:
                continue
            seen[id(s)] = v
            ws.append((s, v))
        self.q[eng].append((fn, ws, sig, amt))
        return tok

    def dma(self, eng, out, in_, waits=(), sig=None, **kw):
        return self.op(eng, lambda e: e.dma_start(out=out, in_=in_, **kw), waits, sig, 16)

    def flush(self, name=None):
        self.nblk += 1
        name = name or f"blk{self.nblk}"
        with self.nc.Block(name) as block:
            for eng in ENGS:
                items = self.q[eng]
                if not items:
                    continue

                def body(e, items=items):
                    for fn, ws, sig, amt in items:
                        for (s, v) in ws:
                            e.wait_ge(s.h, v)
                        if fn is None:
                            continue
                        ins = fn(e)
                        if sig is not None:
                            ins.then_inc(sig.h, amt)

                getattr(block, eng)(body)
        self.q = {e: [] for e in ENGS}


def build_nc(cfg, stop_after=None, debug=False):
    D, S, KC, TO, NTO, NTA, GW, NH, DFF = cfg.D, cfg.S, cfg.KC, cfg.TO, cfg.NTO, cfg.NTA, cfg.GW, cfg.NH, cfg.DFF
    GKC = GW // 128
    nc = bass.Bass("TRN2", target_bir_lowering=False)
    P = Prog(nc)

    def din(name, shape, dt=F32):
        return nc.dram_tensor(name, list(shape), dt, kind="ExternalInput").ap()

    def dscr(name, shape, dt):
        return nc.dram_tensor(name, list(shape), dt).ap()

    xr = din("xr", [S, D])
    xo = din("xo", [TO, D])
    cv = din("cv", [128, KC])
    wada = din("wada", [D, cfg.NMOD])
    bada = din("bada", [1, cfg.NMOD])
    win = din("win", [D, cfg.INC])
    gpre = din("gpre", [1, D])
    gv = din("gv", [1, GW])
    wsT = din("wsT", [128, NH, 128])
    bsr = din("bsr", [128, NH])
    wpa = din("wpa", [GW, D])
    wpb = din("wpb", [GW, D])
    wo = din("wo", [D, D])
    gpost = din("gpost", [1, D])
    gpre2 = din("gpre2", [1, D])
    wf1 = din("wf1", [D, DFF])
    wf2 = din("wf2", [DFF, D])
    gpost2 = din("gpost2", [1, D])
    identf = din("identf", [128, 128])
    kpos = din("kpos", [1, S])
    qpos = din("qpos", [128, NTO])
    y = nc.dram_tensor("y", [TO, D], F32, kind="ExternalOutput").ap()

    dbg = {}

    def dout(name, src, dt=F32):
        if not debug:
            return
        a = nc.dram_tensor("dbg_" + name, list(src.shape), dt, kind="ExternalOutput").ap()
        dbg[name] = (a, src)

    mod_d = dscr("mod_d", [1, cfg.NMOD], F32)
    hT_d = dscr("hT_d", [NTA // 4, 128, KC, 512], BF16)
    kT_d = dscr("kT_d", [NH, 128, S], BF16)
    v_d = dscr("v_d", [S, GW], BF16)
    qT_d = dscr("qT_d", [NH, 128, TO], BF16)
    u_d = dscr("u_d", [TO, GW], BF16)
    vr_d = dscr("vr_d", [TO, GW], F32)
    sga_d = dscr("sga_d", [TO, D], BF16)
    sgb_d = dscr("sgb_d", [TO, D], BF16)
    gm_d = dscr("gm_d", [TO, GW], BF16)
    o_d = dscr("o_d", [TO, GW], BF16)
    t1_d = dscr("t1_d", [TO, D], F32)
    mix_d = dscr("mix_d", [TO, D], BF16)
    mo_d = dscr("mo_d", [TO, D], F32)
    x1_d = dscr("x1_d", [TO, D], F32)
    f_d = dscr("f_d", [TO, DFF], BF16)
    acc_d = dscr("acc_d", [TO, D], F32)

    with ExitStack() as es:
        def sem(name):
            return CS(es.enter_context(nc.semaphore(name)))

        S_pe, S_act, S_dve = sem("s_pe"), sem("s_act"), sem("s_dve")
        S_pool = sem("s_pool")
        S_ld = sem("s_ld")
        S_st = sem("s_st")
        ring_sems = [sem(f"rs{i}") for i in range(24)]
        psf = [es.enter_context(nc.psum_tensor(f"psf{i}", [128, 512], F32)) for i in range(6)]
        psbs = [es.enter_context(nc.psum_tensor(f"psb{i}", [128, 1024], BF16)) for i in range(2)]
        ident_f = es.enter_context(nc.sbuf_tensor("ident_f", [128, 128], F32))
        ident_b = es.enter_context(nc.sbuf_tensor("ident_b", [128, 128], BF16))
        eps_t = es.enter_context(nc.sbuf_tensor("eps_t", [128, 1], F32))

        state = {"rs": 0}
        PSF = {}
        PI = [0]

        def next_bank():
            b = 2 + PI[0] % 4
            PI[0] += 1
            return b

        def new_ring_sem():
            assert state["rs"] < len(ring_sems), "out of ring semaphores in this stage"
            s_ = ring_sems[state["rs"]]
            state["rs"] += 1
            return s_

        stagers = []

        def wait_stagers():
            toks = []
            for s_ in stagers:
                for fr in s_.ring.free:
                    toks.extend(fr)
            P.op("sync", None, toks)
            stagers.clear()

        def end_stage(name):
            wait_stagers()
            P.op("sync", None, [(S_st, S_st.n), (S_ld, S_ld.n)])
            P.flush(name)
            state["rs"] = 0
            if os.environ.get("MK_VERBOSE"):
                print("stage", name, "sem counts pe/act/dve/ld/st", S_pe.n, S_act.n, S_dve.n, S_ld.n, S_st.n,
                      "ring max", max(r.n for r in ring_sems), flush=True)
            return stop_after == name

        def finish():
            for name, (a, src) in dbg.items():
                P.dma("sync", a, src, [], S_st)
            P.op("sync", None, [(S_st, S_st.n)])
            P.flush("fin")
            return nc

        class Ring:
            def __init__(self, st, name, n, shape, dt, sems=True):
                self.bufs = [st.enter_context(nc.sbuf_tensor(f"{name}{i}", list(shape), dt)) for i in range(n)]
                self.sems = [new_ring_sem() if sems else None for _ in range(n)]
                self.free = [[] for _ in range(n)]
                self.i = 0

            def next(self):
                k = self.i % len(self.bufs)
                self.i += 1
                fr = self.free[k]
                self.free[k] = []
                return k, self.bufs[k], self.sems[k], fr

        class Stager:
            def __init__(self, st, name, n, shape, dt):
                self.ring = Ring(st, name, n, shape, dt)

            def get(self):
                k, t, s_, fr = self.ring.next()
                self.k, self.s = k, s_
                return t, fr

            def store(self, dst, src, waits, eng="sync"):
                tok = P.dma(eng, dst, src, waits, self.s)
                self.ring.free[self.k] = [tok]
                return tok

        def mk_stager(st, name, n, shape, dt):
            s_ = Stager(st, name, n, shape, dt)
            stagers.append(s_)
            return s_

        def bcast_load(st, name, src_row, n, dt=F32, waits=()):
            t = st.enter_context(nc.sbuf_tensor(name, [128, n], dt))
            tok = P.dma("sync", t[:], src_row.partition_broadcast(128), list(waits), S_ld)
            return t, tok

        def frontend(st, src_d, ntiles, mb, shb, emit_hT, tag):
            xring = Ring(st, tag + "x", 2, [128, D], F32)
            hring = Ring(st, tag + "h", 2, [128, D], F32)
            junk = st.enter_context(nc.sbuf_tensor(tag + "junk", [128, D], BF16))
            ssq = st.enter_context(nc.sbuf_tensor(tag + "ssq", [128, ntiles], F32))
            std = st.enter_context(nc.sbuf_tensor(tag + "std", [128, ntiles], F32))
            inv = st.enter_context(nc.sbuf_tensor(tag + "inv", [128, ntiles], F32))
            psfree = [None, None]
            pi = [0]

            def phaseA(tt):
                k, xt, xs, xfree = xring.next()
                t_ld = P.dma("sync", xt[:], src_d[tt * 128:(tt + 1) * 128, :], xfree, xs)
                t_sq = P.op("scalar", (lambda e, xt=xt, tt=tt: e.activation(
                    out=junk[:], in_=xt[:], func=AF.Square, accum_out=ssq[:, tt:tt + 1])), [t_ld], S_act)
                t_sd = P.op("scalar", (lambda e, tt=tt: e.activation(
                    out=std[:, tt:tt + 1], in_=ssq[:, tt:tt + 1], func=AF.Sqrt, scale=1.0 / D, bias=eps_t[:, 0:1])),
                    [t_sq], S_act)
                t_iv = P.op("vector", (lambda e, tt=tt: e.reciprocal(out=inv[:, tt:tt + 1], in_=std[:, tt:tt + 1])),
                            [t_sd], S_dve)
                hk, ht, _, hfree = hring.next()
                t_h0 = P.op("vector", (lambda e, xt=xt, ht=ht, tt=tt: e.scalar_tensor_tensor(
                    out=ht[:], in0=xt[:], scalar=inv[:, tt:tt + 1], in1=mb[:], op0=ALU.mult, op1=ALU.mult)),
                    [t_ld, t_iv] + hfree, S_dve)
                t_h = P.op("vector", (lambda e, ht=ht: e.tensor_tensor(out=ht[:], in0=ht[:], in1=shb[:], op=ALU.add)),
                           [t_h0], S_dve)
                xring.free[k] = [t_h, t_sq]
                return dict(hk=hk, ht=ht, t_h=t_h)

            def phaseB(tt, a):
                ht, t_h = a["ht"], a["t_h"]
                t_last = None
                for kc0 in range(0, KC, 4):
                    nk = min(4, KC - kc0)
                    b = pi[0] % 2
                    pi[0] += 1
                    for j in range(nk):
                        t_last = P.op("tensor", (lambda e, b=b, j=j, ht=ht, kc=kc0 + j: e.transpose(
                            out=psf[b][:, j * 128:(j + 1) * 128], in_=ht[:, kc * 128:(kc + 1) * 128], identity=ident_f[:])),
                            [t_h, psfree[b]], S_pe if j == nk - 1 else None)
                    psfree[b] = emit_hT(tt, kc0, nk, psf[b], t_last)
                hring.free[a["hk"]] = [t_last]

            cur_a = phaseA(0)
            for tt in range(ntiles):
                nxt_a = phaseA(tt + 1) if tt + 1 < ntiles else None
                phaseB(tt, cur_a)
                cur_a = nxt_a

        def load_T(st, src_d, K, actT, waits, tag):
            ring = Ring(st, tag + "ld", 2, [128, K], BF16)
            half_free = [None, None]
            hi = 0
            for tt in range(NTO):
                k, t, s_, fr = ring.next()
                t_ld = P.dma("sync", t[:], src_d[tt * 128:(tt + 1) * 128, :], fr + list(waits), s_)
                t_last = None
                for kc0 in range(0, K // 128, 4):
                    nk = min(4, K // 128 - kc0)
                    b = hi % 2
                    hi += 1
                    for j in range(nk):
                        t_last = P.op("tensor", (lambda e, b=b, j=j, t=t, kc=kc0 + j: e.transpose(
                            out=psbs[b][:, j * 128:(j + 1) * 128], in_=t[:, kc * 128:(kc + 1) * 128],
                            identity=ident_b[:])), [t_ld, half_free[b]], S_pe if j == nk - 1 else None)
                    half_free[b] = P.op("scalar", (lambda e, b=b, nk=nk, kc0=kc0, tt=tt: e.activation(
                        out=actT[:, kc0:kc0 + nk, tt * 128:(tt + 1) * 128],
                        in_=psbs[b][:, 0:nk * 128].rearrange("p (k t) -> p k t", k=nk), func=AF.Copy)),
                        [t_last], S_act)
                ring.free[k] = [t_last]

        def gemm(wring, actT, kcn, w_d, row0, col0, ncols, epilogue, cgw=512):
            wv = w_d[row0:row0 + kcn * 128, :].rearrange("(kc p) n -> p kc n", p=128)
            psfree = PSF
            for c0 in range(0, ncols, cgw):
                cw = min(cgw, ncols - c0)
                k, wt, ws_, wfree = wring.next()
                t_w = P.dma("gpsimd", wt[:, :, 0:cw], wv[:, :, col0 + c0:col0 + c0 + cw], wfree, ws_)
                t_last = None
                for tt in range(NTO):
                    b = next_bank()
                    for kc in range(kcn):
                        t_last = P.op("tensor", (lambda e, b=b, kc=kc, wt=wt, cw=cw, tt=tt: e.matmul(
                            psf[b][:, 0:cw], lhsT=actT[:, kc, tt * 128:(tt + 1) * 128], rhs=wt[:, kc, 0:cw],
                            start=(kc == 0), stop=(kc == kcn - 1))),
                            [t_w, psfree.get(b)], S_pe if kc == kcn - 1 else None)
                    psfree[b] = epilogue(tt, c0, cw, psf[b], t_last)
                wring.free[k] = [t_last]

        with ExitStack() as st:
            cvt = st.enter_context(nc.sbuf_tensor("cvt", [128, KC], F32))
            s_bf = st.enter_context(nc.sbuf_tensor("s_bf", [128, KC], BF16))
            P.dma("sync", ident_f[:], identf[:, :], [], S_ld)
            t_ld0 = P.dma("sync", cvt[:], cv[:, :], [], S_ld)
            P.op("vector", lambda e: e.tensor_copy(out=ident_b[:], in_=ident_f[:]), [t_ld0], S_dve)
            P.op("vector", lambda e: e.memset(eps_t[:], EPS), [], S_dve)
            t_s = P.op("scalar", lambda e: e.activation(out=s_bf[:], in_=cvt[:], func=AF.Silu), [t_ld0], S_act)
            wring = Ring(st, "wada", 2, [128, KC, 512], BF16)
            bring = Ring(st, "bada", 2, [1, 512], F32)
            mstg = mk_stager(st, "modst", 2, [1, 512], F32)
            wv = wada.rearrange("(kc p) n -> p kc n", p=128)
            psfree = [None, None]
            for cg in range(cfg.NMOD // 512):
                k, wt, ws_, wfree = wring.next()
                t_w = P.dma("gpsimd", wt[:], wv[:, :, cg * 512:(cg + 1) * 512], wfree, ws_)
                bk, bt, bs_, bfree = bring.next()
                t_b = P.dma("sync", bt[:], bada[0:1, cg * 512:(cg + 1) * 512], bfree, bs_)
                b = cg % 2
                tk = None
                for kc in range(KC):
                    tk = P.op("tensor", (lambda e, b=b, kc=kc, wt=wt: e.matmul(
                        psf[b][0:1, :], lhsT=s_bf[:, kc:kc + 1], rhs=wt[:, kc, :],
                        start=(kc == 0), stop=(kc == KC - 1))), [t_w, t_s, psfree[b]], S_pe if kc == KC - 1 else None)
                wring.free[k] = [tk]
                mt, mfr = mstg.get()
                t_ev = P.op("vector", (lambda e, b=b, mt=mt, bt=bt: e.tensor_tensor(
                    out=mt[0:1, :], in0=psf[b][0:1, :], in1=bt[0:1, :], op=ALU.add)), [tk, t_b] + mfr, S_dve)
                psfree[b] = t_ev
                bring.free[bk] = [t_ev]
                mstg.store(mod_d[0:1, cg * 512:(cg + 1) * 512], mt[0:1, :], [t_ev])
            dout("mod", mod_d)
            if end_stage("s0"):
                return finish()

        def modrow_ap(i):
            return mod_d[0:1, i * D:(i + 1) * D]

        def mod_tiles(st, gsrc, i_sc, i_sh, tag):
            mb, _ = bcast_load(st, tag + "mb", modrow_ap(i_sc), D)
            shb, _ = bcast_load(st, tag + "shb", modrow_ap(i_sh), D)
            gb, t3 = bcast_load(st, tag + "gb", gsrc[0:1, :], D)
            t_m = P.op("vector", lambda e: e.scalar_tensor_tensor(
                out=mb[:], in0=mb[:], scalar=1.0, in1=gb[:], op0=ALU.add, op1=ALU.mult), [t3], S_dve)
            return mb, shb, t_m

        with ExitStack() as st:
            mb, shb, t_m = mod_tiles(st, gpre, 1, 0, "s1")
            stg = mk_stager(st, "s1stg", 2, [128, KC, 512], BF16)
            cur = {}

            def emit1(tt, kc0, nk, ps, ready):
                if tt % 4 == 0 and kc0 == 0:
                    cur["t"], cur["fr"] = stg.get()
                t = cur["t"]
                tok = P.op("scalar", (lambda e, t=t, tt=tt, kc0=kc0, nk=nk, ps=ps: e.activation(
                    out=t[:, kc0:kc0 + nk, (tt % 4) * 128:(tt % 4 + 1) * 128],
                    in_=ps[:, 0:nk * 128].rearrange("p (k t) -> p k t", k=nk), func=AF.Copy)),
                    [ready] + cur["fr"], S_act)
                cur["fr"] = []
                if tt % 4 == 3 and kc0 + nk == KC:
                    stg.store(hT_d[tt // 4], t[:], [tok], eng="scalar")
                return tok

            P.op("vector", None, [t_m])
            frontend(st, xr, NTA, mb, shb, emit1, "s1")
            dout("hT", hT_d, BF16)
            if end_stage("s1"):
                return finish()

        with ExitStack() as st:
            hring = Ring(st, "s2h", 3, [128, KC, 512], BF16)
            wring = Ring(st, "s2w", 2, [128, KC, 512], BF16)
            stg = mk_stager(st, "s2stg", 3, [128, 512], BF16)
            wv = win.rearrange("(kc p) n -> p kc n", p=128)
            kcol0 = 2 * GW + GW
            vcol0 = 2 * GW + 2 * GW
            psfree = PSF
            for isv in (0, 1):
                for c0 in range(0, GW, 512):
                    cw = min(512, GW - c0)
                    wk, wt, ws_, wfree = wring.next()
                    cbase = (vcol0 if isv else kcol0) + c0
                    t_w = P.dma("gpsimd", wt[:, :, 0:cw], wv[:, :, cbase:cbase + cw], wfree, ws_)
                    t_last = None
                    for tg in range(NTA // 4):
                        hk, ht, hs_, hfree = hring.next()
                        t_h = P.dma("sync", ht[:], hT_d[tg], hfree, hs_)
                        if not isv:
                            for hh in range(cw // 128):
                                b = next_bank()
                                for kc in range(KC):
                                    t_last = P.op("tensor", (lambda e, b=b, kc=kc, wt=wt, ht=ht, hh=hh: e.matmul(
                                        psf[b][:, :], lhsT=wt[:, kc, hh * 128:(hh + 1) * 128], rhs=ht[:, kc, :],
                                        start=(kc == 0), stop=(kc == KC - 1))), [t_w, t_h, psfree.get(b)],
                                        S_pe if kc == KC - 1 else None)
                                sg_t, fr = stg.get()
                                tok = P.op("scalar", (lambda e, sg_t=sg_t, b=b: e.activation(
                                    out=sg_t[:], in_=psf[b][:, :], func=AF.Copy)), [t_last] + fr, S_act)
                                psfree[b] = tok
                                head = (c0 // 128) + hh
                                stg.store(kT_d[head][:, tg * 512:(tg + 1) * 512], sg_t[:], [tok], eng="scalar")
                        else:
                            for t4 in range(4):
                                b = next_bank()
                                for kc in range(KC):
                                    t_last = P.op("tensor", (lambda e, b=b, kc=kc, wt=wt, ht=ht, t4=t4, cw=cw: e.matmul(
                                        psf[b][:, 0:cw], lhsT=ht[:, kc, t4 * 128:(t4 + 1) * 128], rhs=wt[:, kc, 0:cw],
                                        start=(kc == 0), stop=(kc == KC - 1))), [t_w, t_h, psfree.get(b)],
                                        S_pe if kc == KC - 1 else None)
                                sg_t, fr = stg.get()
                                tok = P.op("vector", (lambda e, sg_t=sg_t, b=b, cw=cw: e.tensor_copy(
                                    out=sg_t[:, 0:cw], in_=psf[b][:, 0:cw])), [t_last] + fr, S_dve)
                                psfree[b] = tok
                                r0 = tg * 512 + t4 * 128
                                stg.store(v_d[r0:r0 + 128, c0:c0 + cw], sg_t[:, 0:cw], [tok], eng="scalar")
                        hring.free[hk] = [t_last]
                    wring.free[wk] = [t_last]
            dout("kT", kT_d, BF16)
            dout("v", v_d, BF16)
            if end_stage("s2"):
                return finish()

        with ExitStack() as st:
            hT_own = st.enter_context(nc.sbuf_tensor("hT_own", [128, KC, TO], BF16))
            with ExitStack() as st2:
                mb, shb, t_m = mod_tiles(st2, gpre, 1, 0, "s3")

                def emit3(tt, kc0, nk, ps, ready):
                    return P.op("scalar", (lambda e, tt=tt, kc0=kc0, nk=nk, ps=ps: e.activation(
                        out=hT_own[:, kc0:kc0 + nk, tt * 128:(tt + 1) * 128],
                        in_=ps[:, 0:nk * 128].rearrange("p (k t) -> p k t", k=nk), func=AF.Copy)), [ready], S_act)

                P.op("vector", None, [t_m])
                frontend(st2, xo, NTO, mb, shb, emit3, "s3")
                if end_stage("s3a"):
                    return finish()
            stg_b = mk_stager(st, "s3sb", 3, [128, 512], BF16)
            stg_f = mk_stager(st, "s3sf", 3, [128, 512], F32)

            def ep_act(func, dst_d, stg):
                def ep(tt, c0, cw, ps, ready):
                    t, fr = stg.get()
                    tok = P.op("scalar", (lambda e, t=t, ps=ps, cw=cw: e.activation(
                        out=t[:, 0:cw], in_=ps[:, 0:cw], func=func)), [ready] + fr, S_act)
                    stg.store(dst_d[tt * 128:(tt + 1) * 128, c0:c0 + cw], t[:, 0:cw], [tok], eng="scalar")
                    return tok
                return ep

            wring = Ring(st, "s3w", 2, [128, KC, 512], BF16)
            s3n = int(os.environ.get("MK_S3N", "5"))
            if s3n >= 1 and not os.environ.get("MK_SKIPU"):
                gemm(wring, hT_own, KC, win, 0, 0, GW, ep_act(AF.Gelu, u_d, stg_b))
            if s3n >= 2:
                gemm(wring, hT_own, KC, win, 0, GW, GW, ep_act(AF.Gelu, vr_d, stg_f))
            if s3n >= 3:
                gemm(wring, hT_own, KC, win, 0, 2 * GW + 3 * GW, D, ep_act(AF.Sigmoid, sga_d, stg_b))
            if s3n >= 4:
                gemm(wring, hT_own, KC, win, 0, 2 * GW + 3 * GW + D, D, ep_act(AF.Sigmoid, sgb_d, stg_b))
            wv = win.rearrange("(kc p) n -> p kc n", p=128)
            psfree = PSF
            for hh in range(NH if s3n >= 5 else 0):
                wk, wt, ws_, wfree = wring.next()
                cb = 2 * GW + hh * 128
                t_w = P.dma("gpsimd", wt[:, :, 0:128], wv[:, :, cb:cb + 128], wfree, ws_)
                t_last = None
                for t0 in range(0, TO, 512):
                    tw = min(512, TO - t0)
                    b = next_bank()
                    for kc in range(KC):
                        t_last = P.op("tensor", (lambda e, b=b, kc=kc, wt=wt, t0=t0, tw=tw: e.matmul(
                            psf[b][:, 0:tw], lhsT=wt[:, kc, 0:128], rhs=hT_own[:, kc, t0:t0 + tw],
                            start=(kc == 0), stop=(kc == KC - 1))), [t_w, psfree.get(b)],
                            S_pe if kc == KC - 1 else None)
                    t, fr = stg_b.get()
                    tok = P.op("scalar", (lambda e, t=t, b=b, tw=tw: e.activation(
                        out=t[:, 0:tw], in_=psf[b][:, 0:tw], func=AF.Copy, scale=float(1.0 / np.sqrt(128.0)))),
                        [t_last] + fr, S_act)
                    psfree[b] = tok
                    stg_b.store(qT_d[hh][:, t0:t0 + tw], t[:, 0:tw], [tok], eng="scalar")
                wring.free[wk] = [t_last]
            dout("u", u_d, BF16)
            dout("vr", vr_d, F32)
            dout("qT", qT_d, BF16)
            dout("sga", sga_d, BF16)
            if end_stage("s3"):
                return finish()

        with ExitStack() as st:
            gvb, t_g = bcast_load(st, "s4gv", gv[0:1, :], GW)
            wst = st.enter_context(nc.sbuf_tensor("s4ws", [128, NH, 128], F32))
            wsb = st.enter_context(nc.sbuf_tensor("s4wsb", [128, NH, 128], BF16))
            bst = st.enter_context(nc.sbuf_tensor("s4bs", [128, NH], F32))
            P.dma("sync", wst[:], wsT[:, :, :], [], S_ld)
            t_b = P.dma("sync", bst[:], bsr[:, :], [], S_ld)
            t_g = t_b
            t_ms = P.op("vector", lambda e: e.memset(wst[0:64, :, 64:128], 0.0), [t_b], S_dve)
            t_wb = P.op("vector", lambda e: e.tensor_copy(out=wsb[:], in_=wst[:]), [t_ms], S_dve)
            vring = Ring(st, "s4v", 2, [128, GW], F32)
            uring = Ring(st, "s4u", 2, [128, GW], BF16)
            vnr = Ring(st, "s4vn", 2, [128, GW], BF16)
            stg = mk_stager(st, "s4stg", 2, [128, GW], BF16)
            stats = st.enter_context(nc.sbuf_tensor("s4stats", [128, 8 * NTO], F32))
            junk = st.enter_context(nc.sbuf_tensor("s4junk", [128, GW], BF16))
            psfree = PSF
            for tt in range(NTO):
                vk, vt, vs_, vfree = vring.next()
                t_v = P.dma("sync", vt[:], vr_d[tt * 128:(tt + 1) * 128, :], vfree, vs_)
                uk, ut, us_, ufree = uring.next()
                t_u = P.dma("sync", ut[:], u_d[tt * 128:(tt + 1) * 128, :], ufree, us_)
                sc = stats[:, tt * 8:(tt + 1) * 8]
                t_a1 = P.op("scalar", (lambda e, vt=vt, sc=sc: e.activation(
                    out=junk[:], in_=vt[:], func=AF.Copy, scale=1.0 / GW, accum_out=sc[:, 0:1])), [t_v], S_act)
                t_a2 = P.op("scalar", (lambda e, vt=vt, sc=sc: e.activation(
                    out=junk[:], in_=vt[:], func=AF.Square, scale=float(1.0 / np.sqrt(GW)), accum_out=sc[:, 1:2])),
                    [t_a1], S_act)
                t_x = P.op("vector", (lambda e, sc=sc: e.tensor_tensor(
                    out=sc[:, 2:3], in0=sc[:, 0:1], in1=sc[:, 0:1], op=ALU.mult)), [t_a2], S_dve)
                t_var = P.op("vector", (lambda e, sc=sc: e.tensor_tensor(
                    out=sc[:, 3:4], in0=sc[:, 1:2], in1=sc[:, 2:3], op=ALU.subtract)), [t_x], S_dve)
                t_sd = P.op("scalar", (lambda e, sc=sc: e.activation(
                    out=sc[:, 4:5], in_=sc[:, 3:4], func=AF.Sqrt, bias=eps_t[:, 0:1])), [t_var], S_act)
                t_iv = P.op("vector", (lambda e, sc=sc: e.reciprocal(out=sc[:, 5:6], in_=sc[:, 4:5])), [t_sd], S_dve)
                t_n0 = P.op("vector", (lambda e, vt=vt, sc=sc: e.tensor_scalar(
                    out=vt[:], in0=vt[:], scalar1=sc[:, 0:1], scalar2=sc[:, 5:6], op0=ALU.subtract, op1=ALU.mult)),
                    [t_iv], S_dve)
                nk, vn, _, nfree = vnr.next()
                t_vn = P.op("vector", (lambda e, vt=vt, vn=vn: e.tensor_tensor(
                    out=vn[:], in0=vt[:], in1=gvb[:], op=ALU.mult)), [t_g, t_n0] + nfree, S_dve)
                vring.free[vk] = [t_vn]
                gt_, gfr = stg.get()
                t_last = None
                tok = None
                for g in range(NH):
                    b = next_bank()
                    t_last = P.op("tensor", (lambda e, b=b, g=g, vn=vn: e.matmul(
                        psf[b][:, 0:128], lhsT=wsb[:, g, :], rhs=vn[:, g * 128:(g + 1) * 128], start=True, stop=True)),
                        [t_vn, t_wb, psfree.get(b)], S_pe)
                    tok = P.op("vector", (lambda e, b=b, g=g, ut=ut, gt_=gt_: e.scalar_tensor_tensor(
                        out=gt_[:, g * 128:(g + 1) * 128], in0=psf[b][:, 0:128], scalar=bst[:, g:g + 1],
                        in1=ut[:, g * 128:(g + 1) * 128], op0=ALU.add, op1=ALU.mult)), [t_last, t_u] + gfr, S_dve)
                    gfr = []
                    psfree[b] = tok
                vnr.free[nk] = [t_last]
                uring.free[uk] = [tok]
                stg.store(gm_d[tt * 128:(tt + 1) * 128, :], gt_[:], [tok])
            dout("gm", gm_d, BF16)
            if end_stage("s4"):
                return finish()

        with ExitStack() as st:
            G = 4
            kpb, t_kp = bcast_load(st, "s5kp", kpos[0:1, :], S)
            qp = st.enter_context(nc.sbuf_tensor("s5qp", [128, NTO], F32))
            t_qp = P.dma("sync", qp[:], qpos[:, :], [], S_ld)
            t_c = [t_kp, t_qp]
            qring = Ring(st, "s5q", 2, [128, TO], BF16)
            kring = Ring(st, "s5k", 2, [128, S], BF16)
            vring = Ring(st, "s5v", 2, [128, NTA, 128], BF16)
            RD = 2 * G
            ering = Ring(st, "s5e", RD, [128, 512], F32, sems=False)
            lring = Ring(st, "s5l", RD, [128, 512], F32, sems=False)
            cring = Ring(st, "s5c", RD, [128, 512], F32, sems=False)
            wring = Ring(st, "s5w", RD, [128, 512], F32, sems=False)
            aring = Ring(st, "s5a", RD, [128, 512], BF16, sems=False)
            atring = Ring(st, "s5at", RD, [128, 4, 128], BF16, sems=False)
            zeros = st.enter_context(nc.sbuf_tensor("s5z", [128, 512], F32))
            carry0 = st.enter_context(nc.sbuf_tensor("s5carry0", [128, 1], F32))
            P.op("vector", lambda e: e.memset(zeros[:], 0.0), [], S_dve)
            t_z0 = P.op("vector", lambda e: e.memset(carry0[:], 0.0), [], S_dve)
            ostg = mk_stager(st, "s5o", 4, [128, 128], BF16)
            zbanks = [2, 3, 2, 3]
            obanks = [0, 1, 4, 5]
            ofree = {}
            tfree = [None, None]
            tcnt = [0]
            NCH = S // 512
            items = [(hh, qb) for hh in range(NH) for qb in range(NTO)]
            head = {}

            def load_head(hh):
                qk, qt_, qs_, qfree = qring.next()
                t_q = P.dma("sync", qt_[:], qT_d[hh], qfree, qs_)
                kk, kt, ks_, kfree = kring.next()
                t_k = P.dma("sync", kt[:], kT_d[hh], kfree, ks_)
                vk, vt, vs_, vfree = vring.next()
                vsrc = v_d[:, hh * 128:(hh + 1) * 128].rearrange("(b s) d -> s b d", s=128)
                nsp = max(1, NTA // 16)
                t_v = None
                for sp in range(nsp):
                    b0, b1 = sp * NTA // nsp, (sp + 1) * NTA // nsp
                    t_v = P.dma("sync", vt[:, b0:b1, :], vsrc[:, b0:b1, :], vfree, vs_)
                head[hh] = dict(qt=qt_, kt=kt, vt=vt, t_q=t_q, t_k=t_k, t_v=t_v, qk=qk, kk=kk, vk=vk)

            def front(c, ch):
                hd = head[c["hh"]]
                zb = c["zb"]
                qb = c["qb"]
                t_z = P.op("tensor", (lambda e, zb=zb, qb=qb, hd=hd, ch=ch: e.matmul(
                    psf[zb][:, :], lhsT=hd["qt"][:, qb * 128:(qb + 1) * 128], rhs=hd["kt"][:, ch * 512:(ch + 1) * 512],
                    start=True, stop=True)), [hd["t_q"], hd["t_k"], PSF.get(zb)], S_pe)
                c["t_z"] = t_z

            def front2(c, ch):
                zb = c["zb"]
                ek, et, _, efree = ering.next()
                t_e = P.op("scalar", (lambda e, et=et, zb=zb: e.activation(
                    out=et[:], in_=psf[zb][:, :], func=AF.Exp)), [c["t_z"]] + efree, S_act)
                PSF[zb] = t_e
                c["e"] = (ek, et)
                c["t_e"] = t_e

            def front3(c, ch):
                ek, et = c["e"]
                qb = c["qb"]
                c["t_em"] = P.op("vector", (lambda e, et=et, ch=ch, qb=qb: e.scalar_tensor_tensor(
                    out=et[:], in0=kpb[:, ch * 512:(ch + 1) * 512], scalar=qp[:, qb:qb + 1], in1=et[:],
                    op0=ALU.is_gt, op1=ALU.mult)), [c["t_e"]] + t_c, S_dve)

            def front4(c, ch):
                ek, et = c["e"]
                lk, lt, _, lfree = lring.next()
                c["t_l"] = P.op("scalar", (lambda e, et=et, lt=lt: e.activation(
                    out=lt[:], in_=et[:], func=AF.Ln, bias=1.0)), [c["t_em"]] + lfree, S_act)
                c["l"] = (lk, lt)
                c["nxt"] = dict(e=c["e"], l=c["l"], ch=ch, t_em_=c["t_em"])

            def back1(c):
                p = c["pend"]
                lk, lt = p["l"]
                ck, ct, _, cfree = cring.next()
                init = carry0[:, 0:1] if c["prev_c"] is None else c["prev_c"][:, 511:512]
                t_sc = P.op("vector", (lambda e, lt=lt, ct=ct, init=init: e.tensor_tensor_scan(
                    out=ct[:], data0=lt[:], data1=zeros[:], initial=init, op0=ALU.add, op1=ALU.add)),
                    [c["t_l"], c["t_prev_sc"]] + cfree, S_dve)
                c["t_prev_sc"] = t_sc
                lring.free[lk] = [t_sc]
                c["prev_c"] = ct
                p["c"] = (ck, ct)
                p["t_sc"] = t_sc

            def back2(c):
                p = c["pend"]
                ck, ct = p["c"]
                wk, wt, _, wfree = wring.next()
                t_w = P.op("scalar", (lambda e, ct=ct, wt=wt: e.activation(
                    out=wt[:], in_=ct[:], func=AF.Exp, scale=-1.0)), [p["t_sc"]] + wfree, S_act)
                cring.free[ck] = [t_w]
                p["w"] = (wk, wt)
                p["t_w"] = t_w

            def back3(c):
                p = c["pend"]
                ek, et = p["e"]
                wk, wt = p["w"]
                ak, at, _, afree = aring.next()
                if os.environ.get("MK_NOPOOL"):
                    t_a = P.op("vector", (lambda e, et=et, wt=wt, at=at: e.tensor_tensor(
                        out=at[:], in0=et[:], in1=wt[:], op=ALU.mult)), [p["t_w"]] + afree, S_dve)
                else:
                    t_a = P.op("gpsimd", (lambda e, et=et, wt=wt, at=at: e.tensor_tensor(
                        out=at[:], in0=et[:], in1=wt[:], op=ALU.mult)), [p["t_w"], p["t_em_"]] + afree, S_pool)
                ering.free[ek] = [t_a]
                wring.free[wk] = [t_a]
                p["a"] = (ak, at)
                p["t_a"] = t_a

            def back4(c):
                p = c["pend"]
                ak, at = p["a"]
                tb = tcnt[0] % 2
                tcnt[0] += 1
                t_tr = None
                for j in range(4):
                    t_tr = P.op("tensor", (lambda e, tb=tb, j=j, at=at: e.transpose(
                        out=psbs[tb][:, j * 128:(j + 1) * 128], in_=at[:, j * 128:(j + 1) * 128],
                        identity=ident_b[:])), [p["t_a"], tfree[tb]], S_pe if j == 3 else None)
                aring.free[ak] = [t_tr]
                p["tb"] = tb
                p["t_tr"] = t_tr

            def back5(c):
                p = c["pend"]
                tb = p["tb"]
                tk_, att, _, atfree = atring.next()
                t_at = P.op("scalar", (lambda e, tb=tb, att=att: e.activation(
                    out=att[:], in_=psbs[tb][:, 0:512].rearrange("p (j t) -> p j t", j=4),
                    func=AF.Copy)), [p["t_tr"]] + atfree, S_act)
                tfree[tb] = t_at
                p["at"] = (tk_, att)
                p["t_at"] = t_at

            def back6(c):
                p = c["pend"]
                tk_, att = p["at"]
                hd = head[c["hh"]]
                ob = c["ob"]
                ch = p["ch"]
                last = None
                for j in range(4):
                    n_av = c["n_av"]
                    last = P.op("tensor", (lambda e, ob=ob, j=j, att=att, hd=hd, ch=ch, n_av=n_av: e.matmul(
                        psf[ob][:, 0:128], lhsT=att[:, j, :], rhs=hd["vt"][:, ch * 4 + j, :],
                        start=(n_av == 0), stop=(n_av == c["nav_tot"] - 1))),
                        [p["t_at"], hd["t_v"], ofree.get(ob) if n_av == 0 else None],
                        S_pe if (j == 3) else None)
                    c["n_av"] += 1
                atring.free[tk_] = [last]
                c["last_pe"] = last

            loaded = set()
            last_of_head = {}
            for gi in range(0, len(items), G):
                grp = items[gi:gi + G]
                for (hh, qb) in grp:
                    if hh not in loaded:
                        load_head(hh)
                        loaded.add(hh)
                chains = [dict(hh=hh, qb=qb, zb=zbanks[i], ob=obanks[i], prev_c=None, t_prev_sc=t_z0, n_av=0,
                               ch0=(cfg.NC * qb * 128) // 512)
                          for i, (hh, qb) in enumerate(grp)]
                for c in chains:
                    c["nav_tot"] = (NCH - c["ch0"]) * 4
                for i in range(min(c["ch0"] for c in chains), NCH + 1):
                    fr_ = [c for c in chains if c["ch0"] <= i < NCH]
                    bk_ = [c for c in chains if c["ch0"] <= i - 1]
                    for c in bk_:
                        back1(c)
                    for p0 in range(0, len(fr_), 2):
                        for c in fr_[p0:p0 + 2]:
                            front(c, i)
                        for c in fr_[p0:p0 + 2]:
                            front2(c, i)
                    for c in bk_:
                        back2(c)
                    for c in fr_:
                        front3(c, i)
                    for c in bk_:
                        back3(c)
                    for c in fr_:
                        front4(c, i)
                    for c in bk_:
                        back4(c)
                        back5(c)
                    for c in bk_:
                        back6(c)
                    for c in fr_:
                        c["pend"] = c["nxt"]
                for c in chains:
                    ot, ofr = ostg.get()
                    ob = c["ob"]
                    t_o = P.op("vector", (lambda e, ot=ot, ob=ob: e.tensor_copy(out=ot[:], in_=psf[ob][:, 0:128])),
                               [c["last_pe"]] + ofr, S_dve)
                    ofree[ob] = t_o
                    ostg.store(o_d[c["qb"] * 128:(c["qb"] + 1) * 128, c["hh"] * 128:(c["hh"] + 1) * 128], ot[:], [t_o])
                    last_of_head[c["hh"]] = c["last_pe"]
                done_heads = [hh for hh in list(head.keys()) if all((hh, qb) in items[:gi + G] for qb in range(NTO))]
                for hh in done_heads:
                    hd = head.pop(hh)
                    tok = last_of_head[hh]
                    qring.free[hd["qk"]] = [tok]
                    kring.free[hd["kk"]] = [tok]
                    vring.free[hd["vk"]] = [tok]
            dout("o", o_d, BF16)
            if end_stage("s5"):
                return finish()

        for which in (() if os.environ.get('MK_SKIP6') else (0, 1)):
            with ExitStack() as st:
                actT = st.enter_context(nc.sbuf_tensor(f"s6act{which}", [128, GKC, TO], BF16))
                src = gm_d if which == 0 else o_d
                with ExitStack() as st2:
                    load_T(st2, src, GW, actT, [], f"s6l{which}")
                    if end_stage(f"s6l{which}"):
                        return finish()
                sgr = Ring(st, f"s6sg{which}", 3, [128, 512], BF16)
                t1r = Ring(st, f"s6t1{which}", 3, [128, 512], F32)
                stg_f = mk_stager(st, f"s6sf{which}", 3, [128, 512], F32)
                stg_b = mk_stager(st, f"s6sb{which}", 3, [128, 512], BF16)
                sg_src = sga_d if which == 0 else sgb_d

                def ep6(tt, c0, cw, ps, ready, which=which, sgr=sgr, t1r=t1r, stg_f=stg_f, stg_b=stg_b, sg_src=sg_src):
                    ep6m = int(os.environ.get("MK_EP6", "2"))
                    if ep6m == 0:
                        t, fr = stg_f.get()
                        tok = P.op("scalar", (lambda e, t=t, ps=ps, cw=cw: e.activation(
                            out=t[:, 0:cw], in_=ps[:, 0:cw], func=AF.Copy)), [ready] + fr, S_act)
                        stg_f.store(t1_d[tt * 128:(tt + 1) * 128, c0:c0 + cw], t[:, 0:cw], [tok], eng="scalar")
                        return tok
                    gk, gt_, gs_, gfree = sgr.next()
                    t_g = P.dma("sync", gt_[:, 0:cw], sg_src[tt * 128:(tt + 1) * 128, c0:c0 + cw], gfree, gs_)
                    if ep6m == 1:
                        t, fr = stg_f.get()
                        tok = P.op("scalar", (lambda e, t=t, ps=ps, cw=cw: e.activation(
                            out=t[:, 0:cw], in_=ps[:, 0:cw], func=AF.Copy)), [ready, t_g] + fr, S_act)
                        sgr.free[gk] = [tok]
                        stg_f.store(t1_d[tt * 128:(tt + 1) * 128, c0:c0 + cw], t[:, 0:cw], [tok], eng="scalar")
                        return tok
                    if which == 0:
                        t, fr = stg_f.get()
                        tok = P.op("vector", (lambda e, t=t, ps=ps, gt_=gt_, cw=cw: e.tensor_tensor(
                            out=t[:, 0:cw], in0=ps[:, 0:cw], in1=gt_[:, 0:cw], op=ALU.mult)), [ready, t_g] + fr, S_dve)
                        sgr.free[gk] = [tok]
                        stg_f.store(t1_d[tt * 128:(tt + 1) * 128, c0:c0 + cw], t[:, 0:cw], [tok], eng="scalar")
                        return tok
                    k1, t1, s1, f1 = t1r.next()
                    t_1 = P.dma("sync", t1[:, 0:cw], t1_d[tt * 128:(tt + 1) * 128, c0:c0 + cw], f1, s1)
                    t, fr = stg_b.get()
                    tok_a = P.op("vector", (lambda e, ps=ps, gt_=gt_, cw=cw: e.tensor_tensor(
                        out=gt_[:, 0:cw], in0=ps[:, 0:cw], in1=gt_[:, 0:cw], op=ALU.mult)), [ready, t_g], S_dve)
                    tok_b = P.op("vector", (lambda e, t=t, t1=t1, gt_=gt_, cw=cw: e.tensor_tensor(
                        out=t[:, 0:cw], in0=gt_[:, 0:cw], in1=t1[:, 0:cw], op=ALU.add)), [t_1, tok_a] + fr, S_dve)
                    sgr.free[gk] = [tok_b]
                    t1r.free[k1] = [tok_b]
                    stg_b.store(mix_d[tt * 128:(tt + 1) * 128, c0:c0 + cw], t[:, 0:cw], [tok_b], eng="scalar")
                    return tok_a

                wring = Ring(st, f"s6w{which}", 2, [128, GKC, 512], BF16)
                gemm(wring, actT, GKC, wpa if which == 0 else wpb, 0, 0, D, ep6)
                if which == 1:
                    dout("mix", mix_d, BF16)
                if end_stage(f"s6{which}"):
                    return finish()

        with ExitStack() as st:
            actT = st.enter_context(nc.sbuf_tensor("s8act", [128, KC, TO], BF16))
            with ExitStack() as st2:
                load_T(st2, (sga_d if os.environ.get("MK_S8SRC") else mix_d)[:, 0:int(os.environ.get("MK_LTK", D))], int(os.environ.get("MK_LTK", D)), actT, [], "s8l")
                if end_stage("s8l"):
                    return finish()
            stg_f = mk_stager(st, "s8sf", 3, [128, 512], F32)

            def ep8(tt, c0, cw, ps, ready):
                t, fr = stg_f.get()
                tok = P.op("scalar", (lambda e, t=t, ps=ps, cw=cw: e.activation(
                    out=t[:, 0:cw], in_=ps[:, 0:cw], func=AF.Copy)), [ready] + fr, S_act)
                stg_f.store(mo_d[tt * 128:(tt + 1) * 128, c0:c0 + cw], t[:, 0:cw], [tok], eng="scalar")
                return tok

            wring = Ring(st, "s8w", 2, [128, KC, 512], BF16)
            gemm(wring, actT, KC, wo, 0, 0, D, ep8)
            dout("mo", mo_d, F32)
            if end_stage("s8"):
                return finish()

        def post_norm_residual(st, br_d, res_d, gsrc, i_gt, dst_d, tag):
            gtb, _ = bcast_load(st, tag + "gt", modrow_ap(i_gt), D)
            gb, t3 = bcast_load(st, tag + "g", gsrc[0:1, :], D)
            t_gg = P.op("vector", lambda e: e.tensor_tensor(out=gtb[:], in0=gtb[:], in1=gb[:], op=ALU.mult), [t3], S_dve)
            bring = Ring(st, tag + "b", 2, [128, D], F32)
            rring = Ring(st, tag + "r", 2, [128, D], F32)
            junk = st.enter_context(nc.sbuf_tensor(tag + "junk", [128, D], BF16))
            stt = st.enter_context(nc.sbuf_tensor(tag + "st", [128, 4 * NTO], F32))
            stg = mk_stager(st, tag + "o", 2, [128, D], F32)
            for tt in range(NTO):
                bk, bt, bs_, bfree = bring.next()
                t_b = P.dma("sync", bt[:], br_d[tt * 128:(tt + 1) * 128, :], bfree, bs_)
                rk, rt, rs_, rfree = rring.next()
                t_r = P.dma("sync", rt[:], res_d[tt * 128:(tt + 1) * 128, :], rfree, rs_)
                sc = stt[:, tt * 4:(tt + 1) * 4]
                t_q = P.op("scalar", (lambda e, bt=bt, sc=sc: e.activation(
                    out=junk[:], in_=bt[:], func=AF.Square, accum_out=sc[:, 0:1])), [t_b], S_act)
                t_sd = P.op("scalar", (lambda e, sc=sc: e.activation(
                    out=sc[:, 1:2], in_=sc[:, 0:1], func=AF.Sqrt, scale=1.0 / D, bias=eps_t[:, 0:1])), [t_q], S_act)
                t_iv = P.op("vector", (lambda e, sc=sc: e.reciprocal(out=sc[:, 2:3], in_=sc[:, 1:2])), [t_sd], S_dve)
                t_m = P.op("vector", (lambda e, bt=bt, sc=sc: e.scalar_tensor_tensor(
                    out=bt[:], in0=bt[:], scalar=sc[:, 2:3], in1=gtb[:], op0=ALU.mult, op1=ALU.mult)),
                    [t_gg, t_iv], S_dve)
                ot, ofr = stg.get()
                tok = P.op("vector", (lambda e, bt=bt, rt=rt, ot=ot: e.tensor_tensor(
                    out=ot[:], in0=bt[:], in1=rt[:], op=ALU.add)), [t_r, t_m] + ofr, S_dve)
                bring.free[bk] = [tok]
                rring.free[rk] = [tok]
                stg.store(dst_d[tt * 128:(tt + 1) * 128, :], ot[:], [tok])

        with ExitStack() as st:
            post_norm_residual(st, mo_d, xo, gpost, 2, x1_d, "s9")
            dout("x1", x1_d, F32)
            if end_stage("s9"):
                return finish()

        with ExitStack() as st:
            h2T = st.enter_context(nc.sbuf_tensor("h2T", [128, KC, TO], BF16))
            with ExitStack() as st2:
                mb, shb, t_m = mod_tiles(st2, gpre2, 4, 3, "s10")

                def emit10(tt, kc0, nk, ps, ready):
                    return P.op("scalar", (lambda e, tt=tt, kc0=kc0, nk=nk, ps=ps: e.activation(
                        out=h2T[:, kc0:kc0 + nk, tt * 128:(tt + 1) * 128],
                        in_=ps[:, 0:nk * 128].rearrange("p (k t) -> p k t", k=nk), func=AF.Copy)), [ready], S_act)

                P.op("vector", None, [t_m])
                frontend(st2, x1_d, NTO, mb, shb, emit10, "s10")
                if end_stage("s10a"):
                    return finish()
            rr = Ring(st, "s10r", 3, [128, 512], F32)
            stg_b = mk_stager(st, "s10sb", 3, [128, 512], BF16)

            def ep10(tt, c0, cw, ps, ready):
                rk, rt, _, rfree = rr.next()
                tok = P.op("scalar", (lambda e, rt=rt, ps=ps, cw=cw: e.activation(
                    out=rt[:, 0:cw], in_=ps[:, 0:cw], func=AF.Relu)), [ready] + rfree, S_act)
                t, fr = stg_b.get()
                tok2 = P.op("vector", (lambda e, t=t, rt=rt, cw=cw: e.tensor_tensor(
                    out=t[:, 0:cw], in0=rt[:, 0:cw], in1=rt[:, 0:cw], op=ALU.mult)), [tok] + fr, S_dve)
                rr.free[rk] = [tok2]
                stg_b.store(f_d[tt * 128:(tt + 1) * 128, c0:c0 + cw], t[:, 0:cw], [tok2], eng="scalar")
                return tok

            wring = Ring(st, "s10w", 2, [128, KC, 512], BF16)
            gemm(wring, h2T, KC, wf1, 0, 0, DFF, ep10)
            dout("f", f_d, BF16)
            if end_stage("s10"):
                return finish()

        for kb in range(DFF // D):
            with ExitStack() as st:
                actT = st.enter_context(nc.sbuf_tensor(f"s11act{kb}", [128, KC, TO], BF16))
                with ExitStack() as st2:
                    load_T(st2, f_d[:, kb * D:(kb + 1) * D], D, actT, [], f"s11l{kb}")
                    if end_stage(f"s11l{kb}"):
                        return finish()
                ar = Ring(st, f"s11a{kb}", 3, [128, 512], F32)
                stg_f = mk_stager(st, f"s11sf{kb}", 3, [128, 512], F32)

                def ep11(tt, c0, cw, ps, ready, kb=kb, ar=ar, stg_f=stg_f):
                    t, fr = stg_f.get()
                    if kb == 0:
                        tok = P.op("scalar", (lambda e, t=t, ps=ps, cw=cw: e.activation(
                            out=t[:, 0:cw], in_=ps[:, 0:cw], func=AF.Copy)), [ready] + fr, S_act)
                    else:
                        ak, at, as_, afree = ar.next()
                        t_a = P.dma("sync", at[:, 0:cw], acc_d[tt * 128:(tt + 1) * 128, c0:c0 + cw], afree, as_)
                        tok = P.op("vector", (lambda e, t=t, ps=ps, at=at, cw=cw: e.tensor_tensor(
                            out=t[:, 0:cw], in0=ps[:, 0:cw], in1=at[:, 0:cw], op=ALU.add)), [ready, t_a] + fr, S_dve)
                        ar.free[ak] = [tok]
                    stg_f.store(acc_d[tt * 128:(tt + 1) * 128, c0:c0 + cw], t[:, 0:cw], [tok], eng="scalar")
                    return tok

                wring = Ring(st, f"s11w{kb}", 2, [128, KC, 512], BF16)
                gemm(wring, actT, KC, wf2, kb * D, 0, D, ep11)
                if end_stage(f"s11{kb}"):
                    return finish()

        with ExitStack() as st:
            post_norm_residual(st, acc_d, x1_d, gpost2, 5, y, "s12")
            end_stage("s12")
        return finish()


def own_rows(cfg, cid):
    blk = cfg.NC * np.arange(cfg.NTO) + cid
    return (blk[:, None] * 128 + np.arange(128)[None, :]).reshape(-1)


def prep_inputs(cfg, x, c, w_ada, b_ada, g_pre_mix, w_in, g_v, w_s, b_s, w_proj_a, w_proj_b, w_o,
                g_post_mix, g_pre_mlp, w_ff1, w_ff2, g_post_mlp):
    f = lambda a: np.ascontiguousarray(np.asarray(a, dtype=np.float32))
    xr = f(np.asarray(x)[0, ::-1, :])
    common = {
        "xr": xr,
        "cv": f(np.asarray(c)[0].reshape(cfg.KC, 128).T),
        "wada": f(np.asarray(w_ada)[0]),
        "bada": f(np.asarray(b_ada)[0][None, :]),
        "win": f(np.asarray(w_in)[0]),
        "gpre": f(np.asarray(g_pre_mix)[0][None, :]),
        "gv": f(np.asarray(g_v)[0][None, :]),
        "wsT": f(np.asarray(w_s)[0][:, ::-1, ::-1].transpose(2, 0, 1)),
        "bsr": f(np.asarray(b_s)[0][:, ::-1].T),
        "wpa": f(np.asarray(w_proj_a)[0]),
        "wpb": f(np.asarray(w_proj_b)[0]),
        "wo": f(np.asarray(w_o)[0]),
        "gpost": f(np.asarray(g_post_mix)[0][None, :]),
        "gpre2": f(np.asarray(g_pre_mlp)[0][None, :]),
        "wf1": f(np.asarray(w_ff1)[0]),
        "wf2": f(np.asarray(w_ff2)[0]),
        "gpost2": f(np.asarray(g_post_mlp)[0][None, :]),
        "identf": np.eye(128, dtype=np.float32),
        "kpos": np.arange(cfg.S, dtype=np.float32)[None, :],
    }
    in_maps = []
    for cid in range(cfg.NC):
        m = dict(common)
        rows = own_rows(cfg, cid)
        m["xo"] = f(xr[rows])
        m["qpos"] = f(rows.astype(np.float32).reshape(cfg.NTO, 128).T)
        in_maps.append(m)
    return in_maps


_CACHE = {}


def run(cfg, inputs, stop_after=None, debug=False, trace=False):
    key = (cfg.D, cfg.S, cfg.NC, stop_after, debug)
    if key not in _CACHE:
        _CACHE[key] = build_nc(cfg, stop_after, debug)
    nc = _CACHE[key]
    in_maps = prep_inputs(cfg, **inputs)
    return run_bass_kernel_spmd(nc, in_maps, core_ids=list(range(cfg.NC)), trace=trace)


def kernel(**inputs):
    cfg = Cfg()
    res = run(cfg, inputs)
    return assemble(cfg, [np.asarray(r["y"]) for r in res.results])


def assemble(cfg, ys):
    yr = np.empty((cfg.S, cfg.D), np.float32)
    for cid, yc in enumerate(ys):
        yr[own_rows(cfg, cid)] = yc
    return np.ascontiguousarray(yr[::-1])[None].astype(np.float32)
```

```python
import os
from contextlib import ExitStack
import numpy as np
import concourse.bass as bass
import concourse.mybir as mybir
from concourse.bass_utils import run_bass_kernel_spmd

F32 = mybir.dt.float32
BF16 = mybir.dt.bfloat16
AF = mybir.ActivationFunctionType
ALU = mybir.AluOpType
EPS = 1e-6
ENGS = ("sync", "scalar", "gpsimd", "vector", "tensor")


class Cfg:
    def __init__(self, D=4096, S=8192, NC=8):
        self.D, self.S, self.NC = D, S, NC
        self.KC = D // 128
        self.TO = S // NC
        self.NTO = self.TO // 128
        self.NTA = S // 128
        self.GW = D // 2
        self.NH = self.GW // 128
        self.DFF = 4 * D
        self.NMOD = 6 * D
        self.INC = 2 * self.GW + 3 * self.GW + 2 * D


class CS:
    def __init__(self, h):
        self.h = h
        self.n = 0


class Prog:
    def __init__(self, nc):
        self.nc = nc
        self.q = {e: [] for e in ENGS}
        self.waited = {e: {} for e in ENGS}
        self.nblk = 0

    def op(self, eng, fn, waits=(), sig=None, amt=1):
        tok = None
        if sig is not None:
            sig.n += amt
            tok = (sig, sig.n)
        ws = []
        seen = self.waited[eng]
        stack = list(waits)
        while stack:
            w = stack.pop()
            if w is None:
                continue
            if isinstance(w, list):
                stack.extend(w)
                continue
            s, v = w
            if seen.get(id(s), 0) >= v:
                continue
            seen[id(s)] = v
            ws.append((s, v))
        self.q[eng].append((fn, ws, sig, amt))
        return tok

    def dma(self, eng, out, in_, waits=(), sig=None, **kw):
        return self.op(eng, lambda e: e.dma_start(out=out, in_=in_, **kw), waits, sig, 16)

    def flush(self, name=None):
        self.nblk += 1
        name = name or f"blk{self.nblk}"
        with self.nc.Block(name) as block:
            for eng in ENGS:
                items = self.q[eng]
                if not items:
                    continue

                def body(e, items=items):
                    for fn, ws, sig, amt in items:
                        for (s, v) in ws:
                            e.wait_ge(s.h, v)
                        if fn is None:
                            continue
                        ins = fn(e)
                        if sig is not None:
                            ins.then_inc(sig.h, amt)

                getattr(block, eng)(body)
        self.q = {e: [] for e in ENGS}


def build_nc(cfg, stop_after=None, debug=False):
    D, S, KC, TO, NTO, NTA, GW, NH, DFF = cfg.D, cfg.S, cfg.KC, cfg.TO, cfg.NTO, cfg.NTA, cfg.GW, cfg.NH, cfg.DFF
    GKC = GW // 128
    nc = bass.Bass("TRN2", target_bir_lowering=False)
    P = Prog(nc)

    def din(name, shape, dt=F32):
        return nc.dram_tensor(name, list(shape), dt, kind="ExternalInput").ap()

    def dscr(name, shape, dt):
        return nc.dram_tensor(name, list(shape), dt).ap()

    xr = din("xr", [S, D])
    xo = din("xo", [TO, D])
    cv = din("cv", [128, KC])
    wada = din("wada", [D, cfg.NMOD])
    bada = din("bada", [1, cfg.NMOD])
    win = din("win", [D, cfg.INC])
    gpre = din("gpre", [1, D])
    gv = din("gv", [1, GW])
    wsT = din("wsT", [128, NH, 128])
    bsr = din("bsr", [128, NH])
    wpa = din("wpa", [GW, D])
    wpb = din("wpb", [GW, D])
    wo = din("wo", [D, D])
    gpost = din("gpost", [1, D])
    gpre2 = din("gpre2", [1, D])
    wf1 = din("wf1", [D, DFF])
    wf2 = din("wf2", [DFF, D])
    gpost2 = din("gpost2", [1, D])
    identf = din("identf", [128, 128])
    kpos = din("kpos", [1, S])
    qpos = din("qpos", [128, NTO])
    y = nc.dram_tensor("y", [TO, D], F32, kind="ExternalOutput").ap()

    dbg = {}

    def dout(name, src, dt=F32):
        if not debug:
            return
        a = nc.dram_tensor("dbg_" + name, list(src.shape), dt, kind="ExternalOutput").ap()
        dbg[name] = (a, src)

    mod_d = dscr("mod_d", [1, cfg.NMOD], F32)
    hT_d = dscr("hT_d", [NTA // 4, 128, KC, 512], BF16)
    kT_d = dscr("kT_d", [NH, 128, S], BF16)
    v_d = dscr("v_d", [S, GW], BF16)
    qT_d = dscr("qT_d", [NH, 128, TO], BF16)
    u_d = dscr("u_d", [TO, GW], BF16)
    vr_d = dscr("vr_d", [TO, GW], F32)
    sga_d = dscr("sga_d", [TO, D], BF16)
    sgb_d = dscr("sgb_d", [TO, D], BF16)
    gm_d = dscr("gm_d", [TO, GW], BF16)
    o_d = dscr("o_d", [TO, GW], BF16)
    t1_d = dscr("t1_d", [TO, D], F32)
    mix_d = dscr("mix_d", [TO, D], BF16)
    mo_d = dscr("mo_d", [TO, D], F32)
    x1_d = dscr("x1_d", [TO, D], F32)
    f_d = dscr("f_d", [TO, DFF], BF16)
    acc_d = dscr("acc_d", [TO, D], F32)

    with ExitStack() as es:
        def sem(name):
            return CS(es.enter_context(nc.semaphore(name)))

        S_pe, S_act, S_dve = sem("s_pe"), sem("s_act"), sem("s_dve")
        S_pool = sem("s_pool")
        S_ld = sem("s_ld")
        S_st = sem("s_st")
        ring_sems = [sem(f"rs{i}") for i in range(24)]
        psf = [es.enter_context(nc.psum_tensor(f"psf{i}", [128, 512], F32)) for i in range(6)]
        psbs = [es.enter_context(nc.psum_tensor(f"psb{i}", [128, 1024], BF16)) for i in range(2)]
        ident_f = es.enter_context(nc.sbuf_tensor("ident_f", [128, 128], F32))
        ident_b = es.enter_context(nc.sbuf_tensor("ident_b", [128, 128], BF16))
        eps_t = es.enter_context(nc.sbuf_tensor("eps_t", [128, 1], F32))
        s_bf = es.enter_context(nc.sbuf_tensor("s_bf", [128, KC], BF16))

        state = {"rs": 0}
        PSF = {}
        PI = [0]

        def next_bank():
            b = 2 + PI[0] % 4
            PI[0] += 1
            return b

        def new_ring_sem():
            assert state["rs"] < len(ring_sems), "out of ring semaphores in this stage"
            s_ = ring_sems[state["rs"]]
            state["rs"] += 1
            return s_

        stagers = []

        def wait_stagers():
            toks = []
            for s_ in stagers:
                for fr in s_.ring.free:
                    toks.extend(fr)
            P.op("sync", None, toks)
            stagers.clear()

        def end_stage(name):
            wait_stagers()
            P.op("sync", None, [(S_st, S_st.n), (S_ld, S_ld.n)])
            P.flush(name)
            state["rs"] = 0
            if os.environ.get("MK_VERBOSE"):
                print("stage", name, "sem counts pe/act/dve/ld/st", S_pe.n, S_act.n, S_dve.n, S_ld.n, S_st.n,
                      "ring max", max(r.n for r in ring_sems), flush=True)
            return stop_after == name

        def finish():
            for name, (a, src) in dbg.items():
                P.dma("sync", a, src, [], S_st)
            P.op("sync", None, [(S_st, S_st.n)])
            P.flush("fin")
            return nc

        class Ring:
            def __init__(self, st, name, n, shape, dt, sems=True):
                self.bufs = [st.enter_context(nc.sbuf_tensor(f"{name}{i}", list(shape), dt)) for i in range(n)]
                self.sems = [new_ring_sem() if sems else None for _ in range(n)]
                self.free = [[] for _ in range(n)]
                self.i = 0

            def next(self):
                k = self.i % len(self.bufs)
                self.i += 1
                fr = self.free[k]
                self.free[k] = []
                return k, self.bufs[k], self.sems[k], fr

        class Stager:
            def __init__(self, st, name, n, shape, dt):
                self.ring = Ring(st, name, n, shape, dt)

            def get(self):
                k, t, s_, fr = self.ring.next()
                self.k, self.s = k, s_
                return t, fr

            def store(self, dst, src, waits, eng="sync"):
                tok = P.dma(eng, dst, src, waits, self.s)
                self.ring.free[self.k] = [tok]
                return tok

        def mk_stager(st, name, n, shape, dt):
            s_ = Stager(st, name, n, shape, dt)
            stagers.append(s_)
            return s_

        def bcast_load(st, name, src_row, n, dt=F32, waits=()):
            t = st.enter_context(nc.sbuf_tensor(name, [128, n], dt))
            tok = P.dma("sync", t[:], src_row.partition_broadcast(128), list(waits), S_ld)
            return t, tok

        def frontend(st, src_d, ntiles, mb, shb, emit_hT, tag):
            xring = Ring(st, tag + "x", 2, [128, D], F32)
            hring = Ring(st, tag + "h", 2, [128, D], F32)
            junk = st.enter_context(nc.sbuf_tensor(tag + "junk", [128, D], BF16))
            ssq = st.enter_context(nc.sbuf_tensor(tag + "ssq", [128, ntiles], F32))
            std = st.enter_context(nc.sbuf_tensor(tag + "std", [128, ntiles], F32))
            inv = st.enter_context(nc.sbuf_tensor(tag + "inv", [128, ntiles], F32))
            psfree = [None, None]
            pi = [0]

            def phaseA(tt):
                k, xt, xs, xfree = xring.next()
                t_ld = P.dma("sync", xt[:], src_d[tt * 128:(tt + 1) * 128, :], xfree, xs)
                t_sq = P.op("scalar", (lambda e, xt=xt, tt=tt: e.activation(
                    out=junk[:], in_=xt[:], func=AF.Square, accum_out=ssq[:, tt:tt + 1])), [t_ld], S_act)
                t_sd = P.op("scalar", (lambda e, tt=tt: e.activation(
                    out=std[:, tt:tt + 1], in_=ssq[:, tt:tt + 1], func=AF.Sqrt, scale=1.0 / D, bias=eps_t[:, 0:1])),
                    [t_sq], S_act)
                t_iv = P.op("vector", (lambda e, tt=tt: e.reciprocal(out=inv[:, tt:tt + 1], in_=std[:, tt:tt + 1])),
                            [t_sd], S_dve)
                hk, ht, _, hfree = hring.next()
                t_h0 = P.op("vector", (lambda e, xt=xt, ht=ht, tt=tt: e.scalar_tensor_tensor(
                    out=ht[:], in0=xt[:], scalar=inv[:, tt:tt + 1], in1=mb[:], op0=ALU.mult, op1=ALU.mult)),
                    [t_ld, t_iv] + hfree, S_dve)
                t_h = P.op("vector", (lambda e, ht=ht: e.tensor_tensor(out=ht[:], in0=ht[:], in1=shb[:], op=ALU.add)),
                           [t_h0], S_dve)
                xring.free[k] = [t_h, t_sq]
                return dict(hk=hk, ht=ht, t_h=t_h)

            def phaseB(tt, a):
                ht, t_h = a["ht"], a["t_h"]
                t_last = None
                for kc0 in range(0, KC, 4):
                    nk = min(4, KC - kc0)
                    b = pi[0] % 2
                    pi[0] += 1
                    for j in range(nk):
                        t_last = P.op("tensor", (lambda e, b=b, j=j, ht=ht, kc=kc0 + j: e.transpose(
                            out=psf[b][:, j * 128:(j + 1) * 128], in_=ht[:, kc * 128:(kc + 1) * 128], identity=ident_f[:])),
                            [t_h, psfree[b]], S_pe if j == nk - 1 else None)
                    psfree[b] = emit_hT(tt, kc0, nk, psf[b], t_last)
                hring.free[a["hk"]] = [t_last]

            cur_a = phaseA(0)
            for tt in range(ntiles):
                nxt_a = phaseA(tt + 1) if tt + 1 < ntiles else None
                phaseB(tt, cur_a)
                cur_a = nxt_a

        def load_T(st, src_d, K, actT, waits, tag):
            ring = Ring(st, tag + "ld", 2, [128, K], BF16)
            half_free = [None, None]
            hi = 0
            for tt in range(NTO):
                k, t, s_, fr = ring.next()
                t_ld = P.dma("sync", t[:], src_d[tt * 128:(tt + 1) * 128, :], fr + list(waits), s_)
                t_last = None
                for kc0 in range(0, K // 128, 4):
                    nk = min(4, K // 128 - kc0)
                    b = hi % 2
                    hi += 1
                    for j in range(nk):
                        t_last = P.op("tensor", (lambda e, b=b, j=j, t=t, kc=kc0 + j: e.transpose(
                            out=psbs[b][:, j * 128:(j + 1) * 128], in_=t[:, kc * 128:(kc + 1) * 128],
                            identity=ident_b[:])), [t_ld, half_free[b]], S_pe if j == nk - 1 else None)
                    half_free[b] = P.op("scalar", (lambda e, b=b, nk=nk, kc0=kc0, tt=tt: e.activation(
                        out=actT[:, kc0:kc0 + nk, tt * 128:(tt + 1) * 128],
                        in_=psbs[b][:, 0:nk * 128].rearrange("p (k t) -> p k t", k=nk), func=AF.Copy)),
                        [t_last], S_act)
                ring.free[k] = [t_last]

        def gemm(wring, actT, kcn, w_d, row0, col0, ncols, epilogue, cgw=512):
            wv = w_d[row0:row0 + kcn * 128, :].rearrange("(kc p) n -> p kc n", p=128)
            psfree = PSF
            for c0 in range(0, ncols, cgw):
                cw = min(cgw, ncols - c0)
                k, wt, ws_, wfree = wring.next()
                t_w = P.dma("gpsimd", wt[:, :, 0:cw], wv[:, :, col0 + c0:col0 + c0 + cw], wfree, ws_)
                t_last = None
                for tt in range(NTO):
                    b = next_bank()
                    for kc in range(kcn):
                        t_last = P.op("tensor", (lambda e, b=b, kc=kc, wt=wt, cw=cw, tt=tt: e.matmul(
                            psf[b][:, 0:cw], lhsT=actT[:, kc, tt * 128:(tt + 1) * 128], rhs=wt[:, kc, 0:cw],
                            start=(kc == 0), stop=(kc == kcn - 1))),
                            [t_w, psfree.get(b)], S_pe if kc == kcn - 1 else None)
                    psfree[b] = epilogue(tt, c0, cw, psf[b], t_last)
                wring.free[k] = [t_last]

        with ExitStack() as st:
            cvt = st.enter_context(nc.sbuf_tensor("cvt", [128, KC], F32))
            P.dma("sync", ident_f[:], identf[:, :], [], S_ld)
            t_ld0 = P.dma("sync", cvt[:], cv[:, :], [], S_ld)
            P.op("vector", lambda e: e.tensor_copy(out=ident_b[:], in_=ident_f[:]), [t_ld0], S_dve)
            P.op("vector", lambda e: e.memset(eps_t[:], EPS), [], S_dve)
            t_s = P.op("scalar", lambda e: e.activation(out=s_bf[:], in_=cvt[:], func=AF.Silu), [t_ld0], S_act)
            wring = Ring(st, "wada", 2, [128, KC, 512], BF16)
            bring = Ring(st, "bada", 2, [1, 512], F32)
            mstg = mk_stager(st, "modst", 2, [1, 512], F32)
            wv = wada.rearrange("(kc p) n -> p kc n", p=128)
            psfree = [None, None]
            for cg in range(2 * D // 512):
                k, wt, ws_, wfree = wring.next()
                t_w = P.dma("gpsimd", wt[:], wv[:, :, cg * 512:(cg + 1) * 512], wfree, ws_)
                bk, bt, bs_, bfree = bring.next()
                t_b = P.dma("sync", bt[:], bada[0:1, cg * 512:(cg + 1) * 512], bfree, bs_)
                b = cg % 2
                tk = None
                for kc in range(KC):
                    tk = P.op("tensor", (lambda e, b=b, kc=kc, wt=wt: e.matmul(
                        psf[b][0:1, :], lhsT=s_bf[:, kc:kc + 1], rhs=wt[:, kc, :],
                        start=(kc == 0), stop=(kc == KC - 1))), [t_w, t_s, psfree[b]], S_pe if kc == KC - 1 else None)
                wring.free[k] = [tk]
                mt, mfr = mstg.get()
                t_ev = P.op("vector", (lambda e, b=b, mt=mt, bt=bt: e.tensor_tensor(
                    out=mt[0:1, :], in0=psf[b][0:1, :], in1=bt[0:1, :], op=ALU.add)), [tk, t_b] + mfr, S_dve)
                psfree[b] = t_ev
                bring.free[bk] = [t_ev]
                mstg.store(mod_d[0:1, cg * 512:(cg + 1) * 512], mt[0:1, :], [t_ev])
            dout("mod", mod_d)
            if end_stage("s0"):
                return finish()

        def modrow_ap(i):
            return mod_d[0:1, i * D:(i + 1) * D]

        def mod_tiles(st, gsrc, i_sc, i_sh, tag):
            mb, _ = bcast_load(st, tag + "mb", modrow_ap(i_sc), D)
            shb, _ = bcast_load(st, tag + "shb", modrow_ap(i_sh), D)
            gb, t3 = bcast_load(st, tag + "gb", gsrc[0:1, :], D)
            t_m = P.op("vector", lambda e: e.scalar_tensor_tensor(
                out=mb[:], in0=mb[:], scalar=1.0, in1=gb[:], op0=ALU.add, op1=ALU.mult), [t3], S_dve)
            return mb, shb, t_m

        with ExitStack() as st:
            mb, shb, t_m = mod_tiles(st, gpre, 1, 0, "s1")
            stg = mk_stager(st, "s1stg", 2, [128, KC, 512], BF16)
            cur = {}

            def emit1(tt, kc0, nk, ps, ready):
                if tt % 4 == 0 and kc0 == 0:
                    cur["t"], cur["fr"] = stg.get()
                t = cur["t"]
                tok = P.op("scalar", (lambda e, t=t, tt=tt, kc0=kc0, nk=nk, ps=ps: e.activation(
                    out=t[:, kc0:kc0 + nk, (tt % 4) * 128:(tt % 4 + 1) * 128],
                    in_=ps[:, 0:nk * 128].rearrange("p (k t) -> p k t", k=nk), func=AF.Copy)),
                    [ready] + cur["fr"], S_act)
                cur["fr"] = []
                if tt % 4 == 3 and kc0 + nk == KC:
                    stg.store(hT_d[tt // 4], t[:], [tok], eng="scalar")
                return tok

            P.op("vector", None, [t_m])
            frontend(st, xr, NTA, mb, shb, emit1, "s1")
            dout("hT", hT_d, BF16)
            if end_stage("s1"):
                return finish()

        with ExitStack() as st:
            hring = Ring(st, "s2h", 3, [128, KC, 512], BF16)
            wring = Ring(st, "s2w", 2, [128, KC, 512], BF16)
            stg = mk_stager(st, "s2stg", 3, [128, 512], BF16)
            wv = win.rearrange("(kc p) n -> p kc n", p=128)
            kcol0 = 2 * GW + GW
            vcol0 = 2 * GW + 2 * GW
            psfree = PSF
            MW = 256
            mwring = Ring(st, "s2mw", 2, [128, KC, MW], BF16)
            mbring = Ring(st, "s2mb", 2, [1, MW], F32)
            mstg = mk_stager(st, "s2ms", 2, [1, MW], F32)
            wadav = wada.rearrange("(kc p) n -> p kc n", p=128)
            mtasks = list(range(2 * D, cfg.NMOD, MW))
            n_iter = 2 * len(range(0, GW, 512)) * (NTA // 4)
            mstate = {"it": 0, "pend": [], "bank": 0, "bfree": [None, None]}

            def mod_issue():
                it = mstate["it"]
                mstate["it"] += 1
                lo, hi = it * len(mtasks) // n_iter, (it + 1) * len(mtasks) // n_iter
                for c in mtasks[lo:hi]:
                    k, wt, ws_, wfree = mwring.next()
                    t_w = P.dma("gpsimd", wt[:], wadav[:, :, c:c + MW], wfree, ws_)
                    bk, bt, bs_, bfree = mbring.next()
                    t_b = P.dma("sync", bt[:], bada[0:1, c:c + MW], bfree, bs_)
                    mstate["pend"].append((c, k, wt, t_w, bk, bt, t_b))

            def mod_compute():
                for (c, k, wt, t_w, bk, bt, t_b) in mstate["pend"]:
                    b = mstate["bank"] % 2
                    mstate["bank"] += 1
                    tk = None
                    for kc in range(KC):
                        tk = P.op("tensor", (lambda e, b=b, kc=kc, wt=wt: e.matmul(
                            psf[b][0:1, 0:MW], lhsT=s_bf[:, kc:kc + 1], rhs=wt[:, kc, :],
                            start=(kc == 0), stop=(kc == KC - 1))), [t_w, mstate["bfree"][b]],
                            S_pe if kc == KC - 1 else None)
                    mwring.free[k] = [tk]
                    mt, mfr = mstg.get()
                    t_ev = P.op("vector", (lambda e, b=b, mt=mt, bt=bt: e.tensor_tensor(
                        out=mt[0:1, :], in0=psf[b][0:1, 0:MW], in1=bt[0:1, :], op=ALU.add)), [tk, t_b] + mfr, S_dve)
                    mstate["bfree"][b] = t_ev
                    mbring.free[bk] = [t_ev]
                    mstg.store(mod_d[0:1, c:c + MW], mt[0:1, :], [t_ev])
                mstate["pend"] = []

            for isv in (0, 1):
                for c0 in range(0, GW, 512):
                    cw = min(512, GW - c0)
                    wk, wt, ws_, wfree = wring.next()
                    cbase = (vcol0 if isv else kcol0) + c0
                    t_w = P.dma("gpsimd", wt[:, :, 0:cw], wv[:, :, cbase:cbase + cw], wfree, ws_)
                    t_last = None
                    for tg in range(NTA // 4):
                        hk, ht, hs_, hfree = hring.next()
                        t_h = P.dma("sync", ht[:], hT_d[tg], hfree, hs_)
                        mod_compute()
                        mod_issue()
                        if not isv:
                            for hh in range(cw // 128):
                                b = next_bank()
                                for kc in range(KC):
                                    t_last = P.op("tensor", (lambda e, b=b, kc=kc, wt=wt, ht=ht, hh=hh: e.matmul(
                                        psf[b][:, :], lhsT=wt[:, kc, hh * 128:(hh + 1) * 128], rhs=ht[:, kc, :],
                                        start=(kc == 0), stop=(kc == KC - 1))), [t_w, t_h, psfree.get(b)],
                                        S_pe if kc == KC - 1 else None)
                                sg_t, fr = stg.get()
                                tok = P.op("scalar", (lambda e, sg_t=sg_t, b=b: e.activation(
                                    out=sg_t[:], in_=psf[b][:, :], func=AF.Copy)), [t_last] + fr, S_act)
                                psfree[b] = tok
                                head = (c0 // 128) + hh
                                stg.store(kT_d[head][:, tg * 512:(tg + 1) * 512], sg_t[:], [tok], eng="scalar")
                        else:
                            for t4 in range(4):
                                b = next_bank()
                                for kc in range(KC):
                                    t_last = P.op("tensor", (lambda e, b=b, kc=kc, wt=wt, ht=ht, t4=t4, cw=cw: e.matmul(
                                        psf[b][:, 0:cw], lhsT=ht[:, kc, t4 * 128:(t4 + 1) * 128], rhs=wt[:, kc, 0:cw],
                                        start=(kc == 0), stop=(kc == KC - 1))), [t_w, t_h, psfree.get(b)],
                                        S_pe if kc == KC - 1 else None)
                                sg_t, fr = stg.get()
                                tok = P.op("vector", (lambda e, sg_t=sg_t, b=b, cw=cw: e.tensor_copy(
                                    out=sg_t[:, 0:cw], in_=psf[b][:, 0:cw])), [t_last] + fr, S_dve)
                                psfree[b] = tok
                                r0 = tg * 512 + t4 * 128
                                stg.store(v_d[r0:r0 + 128, c0:c0 + cw], sg_t[:, 0:cw], [tok], eng="scalar")
                        hring.free[hk] = [t_last]
                    wring.free[wk] = [t_last]
            mod_compute()
            assert mstate["it"] == n_iter
            dout("kT", kT_d, BF16)
            dout("v", v_d, BF16)
            if end_stage("s2"):
                return finish()

        with ExitStack() as st:
            hT_own = st.enter_context(nc.sbuf_tensor("hT_own", [128, KC, TO], BF16))
            with ExitStack() as st2:
                mb, shb, t_m = mod_tiles(st2, gpre, 1, 0, "s3")

                def emit3(tt, kc0, nk, ps, ready):
                    return P.op("scalar", (lambda e, tt=tt, kc0=kc0, nk=nk, ps=ps: e.activation(
                        out=hT_own[:, kc0:kc0 + nk, tt * 128:(tt + 1) * 128],
                        in_=ps[:, 0:nk * 128].rearrange("p (k t) -> p k t", k=nk), func=AF.Copy)), [ready], S_act)

                P.op("vector", None, [t_m])
                frontend(st2, xo, NTO, mb, shb, emit3, "s3")
                if end_stage("s3a"):
                    return finish()
            stg_b = mk_stager(st, "s3sb", 3, [128, 512], BF16)
            stg_f = mk_stager(st, "s3sf", 3, [128, 512], F32)

            def ep_act(func, dst_d, stg):
                def ep(tt, c0, cw, ps, ready):
                    t, fr = stg.get()
                    tok = P.op("scalar", (lambda e, t=t, ps=ps, cw=cw: e.activation(
                        out=t[:, 0:cw], in_=ps[:, 0:cw], func=func)), [ready] + fr, S_act)
                    stg.store(dst_d[tt * 128:(tt + 1) * 128, c0:c0 + cw], t[:, 0:cw], [tok], eng="scalar")
                    return tok
                return ep

            wring = Ring(st, "s3w", 2, [128, KC, 512], BF16)
            s3n = int(os.environ.get("MK_S3N", "5"))
            if s3n >= 1 and not os.environ.get("MK_SKIPU"):
                gemm(wring, hT_own, KC, win, 0, 0, GW, ep_act(AF.Gelu, u_d, stg_b))
            if s3n >= 2:
                gemm(wring, hT_own, KC, win, 0, GW, GW, ep_act(AF.Gelu, vr_d, stg_f))
            if s3n >= 3:
                gemm(wring, hT_own, KC, win, 0, 2 * GW + 3 * GW, D, ep_act(AF.Sigmoid, sga_d, stg_b))
            if s3n >= 4:
                gemm(wring, hT_own, KC, win, 0, 2 * GW + 3 * GW + D, D, ep_act(AF.Sigmoid, sgb_d, stg_b))
            wv = win.rearrange("(kc p) n -> p kc n", p=128)
            psfree = PSF
            for hh in range(NH if s3n >= 5 else 0):
                wk, wt, ws_, wfree = wring.next()
                cb = 2 * GW + hh * 128
                t_w = P.dma("gpsimd", wt[:, :, 0:128], wv[:, :, cb:cb + 128], wfree, ws_)
                t_last = None
                for t0 in range(0, TO, 512):
                    tw = min(512, TO - t0)
                    b = next_bank()
                    for kc in range(KC):
                        t_last = P.op("tensor", (lambda e, b=b, kc=kc, wt=wt, t0=t0, tw=tw: e.matmul(
                            psf[b][:, 0:tw], lhsT=wt[:, kc, 0:128], rhs=hT_own[:, kc, t0:t0 + tw],
                            start=(kc == 0), stop=(kc == KC - 1))), [t_w, psfree.get(b)],
                            S_pe if kc == KC - 1 else None)
                    t, fr = stg_b.get()
                    tok = P.op("scalar", (lambda e, t=t, b=b, tw=tw: e.activation(
                        out=t[:, 0:tw], in_=psf[b][:, 0:tw], func=AF.Copy, scale=float(1.0 / np.sqrt(128.0)))),
                        [t_last] + fr, S_act)
                    psfree[b] = tok
                    stg_b.store(qT_d[hh][:, t0:t0 + tw], t[:, 0:tw], [tok], eng="scalar")
                wring.free[wk] = [t_last]
            dout("u", u_d, BF16)
            dout("vr", vr_d, F32)
            dout("qT", qT_d, BF16)
            dout("sga", sga_d, BF16)
            if end_stage("s3"):
                return finish()

        with ExitStack() as st:
            gvb, t_g = bcast_load(st, "s4gv", gv[0:1, :], GW)
            wst = st.enter_context(nc.sbuf_tensor("s4ws", [128, NH, 128], F32))
            wsb = st.enter_context(nc.sbuf_tensor("s4wsb", [128, NH, 128], BF16))
            bst = st.enter_context(nc.sbuf_tensor("s4bs", [128, NH], F32))
            P.dma("sync", wst[:], wsT[:, :, :], [], S_ld)
            t_b = P.dma("sync", bst[:], bsr[:, :], [], S_ld)
            t_g = t_b
            t_ms = P.op("vector", lambda e: e.memset(wst[0:64, :, 64:128], 0.0), [t_b], S_dve)
            t_wb = P.op("vector", lambda e: e.tensor_copy(out=wsb[:], in_=wst[:]), [t_ms], S_dve)
            vring = Ring(st, "s4v", 2, [128, GW], F32)
            uring = Ring(st, "s4u", 2, [128, GW], BF16)
            vnr = Ring(st, "s4vn", 2, [128, GW], BF16)
            stg = mk_stager(st, "s4stg", 2, [128, GW], BF16)
            stats = st.enter_context(nc.sbuf_tensor("s4stats", [128, 8 * NTO], F32))
            junk = st.enter_context(nc.sbuf_tensor("s4junk", [128, GW], BF16))
            psfree = PSF
            for tt in range(NTO):
                vk, vt, vs_, vfree = vring.next()
                t_v = P.dma("sync", vt[:], vr_d[tt * 128:(tt + 1) * 128, :], vfree, vs_)
                uk, ut, us_, ufree = uring.next()
                t_u = P.dma("sync", ut[:], u_d[tt * 128:(tt + 1) * 128, :], ufree, us_)
                sc = stats[:, tt * 8:(tt + 1) * 8]
                t_a1 = P.op("scalar", (lambda e, vt=vt, sc=sc: e.activation(
                    out=junk[:], in_=vt[:], func=AF.Copy, scale=1.0 / GW, accum_out=sc[:, 0:1])), [t_v], S_act)
                t_a2 = P.op("scalar", (lambda e, vt=vt, sc=sc: e.activation(
                    out=junk[:], in_=vt[:], func=AF.Square, scale=float(1.0 / np.sqrt(GW)), accum_out=sc[:, 1:2])),
                    [t_a1], S_act)
                t_x = P.op("vector", (lambda e, sc=sc: e.tensor_tensor(
                    out=sc[:, 2:3], in0=sc[:, 0:1], in1=sc[:, 0:1], op=ALU.mult)), [t_a2], S_dve)
                t_var = P.op("vector", (lambda e, sc=sc: e.tensor_tensor(
                    out=sc[:, 3:4], in0=sc[:, 1:2], in1=sc[:, 2:3], op=ALU.subtract)), [t_x], S_dve)
                t_sd = P.op("scalar", (lambda e, sc=sc: e.activation(
                    out=sc[:, 4:5], in_=sc[:, 3:4], func=AF.Sqrt, bias=eps_t[:, 0:1])), [t_var], S_act)
                t_iv = P.op("vector", (lambda e, sc=sc: e.reciprocal(out=sc[:, 5:6], in_=sc[:, 4:5])), [t_sd], S_dve)
                t_n0 = P.op("vector", (lambda e, vt=vt, sc=sc: e.tensor_scalar(
                    out=vt[:], in0=vt[:], scalar1=sc[:, 0:1], scalar2=sc[:, 5:6], op0=ALU.subtract, op1=ALU.mult)),
                    [t_iv], S_dve)
                nk, vn, _, nfree = vnr.next()
                t_vn = P.op("vector", (lambda e, vt=vt, vn=vn: e.tensor_tensor(
                    out=vn[:], in0=vt[:], in1=gvb[:], op=ALU.mult)), [t_g, t_n0] + nfree, S_dve)
                vring.free[vk] = [t_vn]
                gt_, gfr = stg.get()
                t_last = None
                tok = None
                for g in range(NH):
                    b = next_bank()
                    t_last = P.op("tensor", (lambda e, b=b, g=g, vn=vn: e.matmul(
                        psf[b][:, 0:128], lhsT=wsb[:, g, :], rhs=vn[:, g * 128:(g + 1) * 128], start=True, stop=True)),
                        [t_vn, t_wb, psfree.get(b)], S_pe)
                    tok = P.op("vector", (lambda e, b=b, g=g, ut=ut, gt_=gt_: e.scalar_tensor_tensor(
                        out=gt_[:, g * 128:(g + 1) * 128], in0=psf[b][:, 0:128], scalar=bst[:, g:g + 1],
                        in1=ut[:, g * 128:(g + 1) * 128], op0=ALU.add, op1=ALU.mult)), [t_last, t_u] + gfr, S_dve)
                    gfr = []
                    psfree[b] = tok
                vnr.free[nk] = [t_last]
                uring.free[uk] = [tok]
                stg.store(gm_d[tt * 128:(tt + 1) * 128, :], gt_[:], [tok])
            dout("gm", gm_d, BF16)
            if end_stage("s4"):
                return finish()

        with ExitStack() as st:
            G = 4
            kpb, t_kp = bcast_load(st, "s5kp", kpos[0:1, :], S)
            qp = st.enter_context(nc.sbuf_tensor("s5qp", [128, NTO], F32))
            t_qp = P.dma("sync", qp[:], qpos[:, :], [], S_ld)
            t_c = [t_kp, t_qp]
            qring = Ring(st, "s5q", 2, [128, TO], BF16)
            kring = Ring(st, "s5k", 2, [128, S], BF16)
            vring = Ring(st, "s5v", 2, [128, NTA, 128], BF16)
            RD = 2 * G
            ering = Ring(st, "s5e", RD, [128, 512], F32, sems=False)
            lring = Ring(st, "s5l", RD, [128, 512], F32, sems=False)
            cring = Ring(st, "s5c", RD, [128, 512], F32, sems=False)
            wring = Ring(st, "s5w", RD, [128, 512], F32, sems=False)
            aring = Ring(st, "s5a", RD, [128, 512], BF16, sems=False)
            atring = Ring(st, "s5at", RD, [128, 4, 128], BF16, sems=False)
            zeros = st.enter_context(nc.sbuf_tensor("s5z", [128, 512], F32))
            carry0 = st.enter_context(nc.sbuf_tensor("s5carry0", [128, 1], F32))
            P.op("vector", lambda e: e.memset(zeros[:], 0.0), [], S_dve)
            t_z0 = P.op("vector", lambda e: e.memset(carry0[:], 0.0), [], S_dve)
            ostg = mk_stager(st, "s5o", 4, [128, 128], BF16)
            zbanks = [2, 3, 2, 3]
            obanks = [0, 1, 4, 5]
            ofree = {}
            tfree = [None, None]
            tcnt = [0]
            NCH = S // 512
            items = [(hh, qb) for hh in range(NH) for qb in range(NTO)]
            head = {}

            def load_head(hh):
                qk, qt_, qs_, qfree = qring.next()
                t_q = P.dma("sync", qt_[:], qT_d[hh], qfree, qs_)
                kk, kt, ks_, kfree = kring.next()
                t_k = P.dma("sync", kt[:], kT_d[hh], kfree, ks_)
                vk, vt, vs_, vfree = vring.next()
                vsrc = v_d[:, hh * 128:(hh + 1) * 128].rearrange("(b s) d -> s b d", s=128)
                nsp = max(1, NTA // 16)
                t_v = None
                for sp in range(nsp):
                    b0, b1 = sp * NTA // nsp, (sp + 1) * NTA // nsp
                    t_v = P.dma("sync", vt[:, b0:b1, :], vsrc[:, b0:b1, :], vfree, vs_)
                head[hh] = dict(qt=qt_, kt=kt, vt=vt, t_q=t_q, t_k=t_k, t_v=t_v, qk=qk, kk=kk, vk=vk)

            def front(c, ch):
                hd = head[c["hh"]]
                zb = c["zb"]
                qb = c["qb"]
                t_z = P.op("tensor", (lambda e, zb=zb, qb=qb, hd=hd, ch=ch: e.matmul(
                    psf[zb][:, :], lhsT=hd["qt"][:, qb * 128:(qb + 1) * 128], rhs=hd["kt"][:, ch * 512:(ch + 1) * 512],
                    start=True, stop=True)), [hd["t_q"], hd["t_k"], PSF.get(zb)], S_pe)
                c["t_z"] = t_z

            def front2(c, ch):
                zb = c["zb"]
                ek, et, _, efree = ering.next()
                t_e = P.op("scalar", (lambda e, et=et, zb=zb: e.activation(
                    out=et[:], in_=psf[zb][:, :], func=AF.Exp)), [c["t_z"]] + efree, S_act)
                PSF[zb] = t_e
                c["e"] = (ek, et)
                c["t_e"] = t_e

            def front3(c, ch):
                ek, et = c["e"]
                qb = c["qb"]
                c["t_em"] = P.op("vector", (lambda e, et=et, ch=ch, qb=qb: e.scalar_tensor_tensor(
                    out=et[:], in0=kpb[:, ch * 512:(ch + 1) * 512], scalar=qp[:, qb:qb + 1], in1=et[:],
                    op0=ALU.is_gt, op1=ALU.mult)), [c["t_e"]] + t_c, S_dve)

            def front4(c, ch):
                ek, et = c["e"]
                lk, lt, _, lfree = lring.next()
                c["t_l"] = P.op("scalar", (lambda e, et=et, lt=lt: e.activation(
                    out=lt[:], in_=et[:], func=AF.Ln, bias=1.0)), [c["t_em"]] + lfree, S_act)
                c["l"] = (lk, lt)
                c["nxt"] = dict(e=c["e"], l=c["l"], ch=ch, t_em_=c["t_em"])

            def back1(c):
                p = c["pend"]
                lk, lt = p["l"]
                ck, ct, _, cfree = cring.next()
                init = carry0[:, 0:1] if c["prev_c"] is None else c["prev_c"][:, 511:512]
                t_sc = P.op("vector", (lambda e, lt=lt, ct=ct, init=init: e.tensor_tensor_scan(
                    out=ct[:], data0=lt[:], data1=zeros[:], initial=init, op0=ALU.add, op1=ALU.add)),
                    [c["t_l"], c["t_prev_sc"]] + cfree, S_dve)
                c["t_prev_sc"] = t_sc
                lring.free[lk] = [t_sc]
                c["prev_c"] = ct
                p["c"] = (ck, ct)
                p["t_sc"] = t_sc

            def back2(c):
                p = c["pend"]
                ck, ct = p["c"]
                wk, wt, _, wfree = wring.next()
                t_w = P.op("scalar", (lambda e, ct=ct, wt=wt: e.activation(
                    out=wt[:], in_=ct[:], func=AF.Exp, scale=-1.0)), [p["t_sc"]] + wfree, S_act)
                cring.free[ck] = [t_w]
                p["w"] = (wk, wt)
                p["t_w"] = t_w

            def back3(c):
                p = c["pend"]
                ek, et = p["e"]
                wk, wt = p["w"]
                ak, at, _, afree = aring.next()
                if os.environ.get("MK_NOPOOL"):
                    t_a = P.op("vector", (lambda e, et=et, wt=wt, at=at: e.tensor_tensor(
                        out=at[:], in0=et[:], in1=wt[:], op=ALU.mult)), [p["t_w"]] + afree, S_dve)
                else:
                    t_a = P.op("gpsimd", (lambda e, et=et, wt=wt, at=at: e.tensor_tensor(
                        out=at[:], in0=et[:], in1=wt[:], op=ALU.mult)), [p["t_w"], p["t_em_"]] + afree, S_pool)
                ering.free[ek] = [t_a]
                wring.free[wk] = [t_a]
                p["a"] = (ak, at)
                p["t_a"] = t_a

            def back4(c):
                p = c["pend"]
                ak, at = p["a"]
                tb = tcnt[0] % 2
                tcnt[0] += 1
                t_tr = None
                for j in range(4):
                    t_tr = P.op("tensor", (lambda e, tb=tb, j=j, at=at: e.transpose(
                        out=psbs[tb][:, j * 128:(j + 1) * 128], in_=at[:, j * 128:(j + 1) * 128],
                        identity=ident_b[:])), [p["t_a"], tfree[tb]], S_pe if j == 3 else None)
                aring.free[ak] = [t_tr]
                p["tb"] = tb
                p["t_tr"] = t_tr

            def back5(c):
                p = c["pend"]
                tb = p["tb"]
                tk_, att, _, atfree = atring.next()
                t_at = P.op("scalar", (lambda e, tb=tb, att=att: e.activation(
                    out=att[:], in_=psbs[tb][:, 0:512].rearrange("p (j t) -> p j t", j=4),
                    func=AF.Copy)), [p["t_tr"]] + atfree, S_act)
                tfree[tb] = t_at
                p["at"] = (tk_, att)
                p["t_at"] = t_at

            def back6(c):
                p = c["pend"]
                tk_, att = p["at"]
                hd = head[c["hh"]]
                ob = c["ob"]
                ch = p["ch"]
                last = None
                for j in range(4):
                    n_av = c["n_av"]
                    last = P.op("tensor", (lambda e, ob=ob, j=j, att=att, hd=hd, ch=ch, n_av=n_av: e.matmul(
                        psf[ob][:, 0:128], lhsT=att[:, j, :], rhs=hd["vt"][:, ch * 4 + j, :],
                        start=(n_av == 0), stop=(n_av == c["nav_tot"] - 1))),
                        [p["t_at"], hd["t_v"], ofree.get(ob) if n_av == 0 else None],
                        S_pe if (j == 3) else None)
                    c["n_av"] += 1
                atring.free[tk_] = [last]
                c["last_pe"] = last

            loaded = set()
            last_of_head = {}
            for gi in range(0, len(items), G):
                grp = items[gi:gi + G]
                for (hh, qb) in grp:
                    if hh not in loaded:
                        load_head(hh)
                        loaded.add(hh)
                chains = [dict(hh=hh, qb=qb, zb=zbanks[i], ob=obanks[i], prev_c=None, t_prev_sc=t_z0, n_av=0,
                               ch0=(cfg.NC * qb * 128) // 512)
                          for i, (hh, qb) in enumerate(grp)]
                for c in chains:
                    c["nav_tot"] = (NCH - c["ch0"]) * 4
                for i in range(min(c["ch0"] for c in chains), NCH + 1):
                    fr_ = [c for c in chains if c["ch0"] <= i < NCH]
                    bk_ = [c for c in chains if c["ch0"] <= i - 1]
                    for c in bk_:
                        back1(c)
                    for p0 in range(0, len(fr_), 2):
                        for c in fr_[p0:p0 + 2]:
                            front(c, i)
                        for c in fr_[p0:p0 + 2]:
                            front2(c, i)
                    for c in bk_:
                        back2(c)
                    for c in fr_:
                        front3(c, i)
                    for c in bk_:
                        back3(c)
                    for c in fr_:
                        front4(c, i)
                    for c in bk_:
                        back4(c)
                        back5(c)
                    for c in bk_:
                        back6(c)
                    for c in fr_:
                        c["pend"] = c["nxt"]
                for c in chains:
                    ot, ofr = ostg.get()
                    ob = c["ob"]
                    t_o = P.op("vector", (lambda e, ot=ot, ob=ob: e.tensor_copy(out=ot[:], in_=psf[ob][:, 0:128])),
                               [c["last_pe"]] + ofr, S_dve)
                    ofree[ob] = t_o
                    ostg.store(o_d[c["qb"] * 128:(c["qb"] + 1) * 128, c["hh"] * 128:(c["hh"] + 1) * 128], ot[:], [t_o])
                    last_of_head[c["hh"]] = c["last_pe"]
                done_heads = [hh for hh in list(head.keys()) if all((hh, qb) in items[:gi + G] for qb in range(NTO))]
                for hh in done_heads:
                    hd = head.pop(hh)
                    tok = last_of_head[hh]
                    qring.free[hd["qk"]] = [tok]
                    kring.free[hd["kk"]] = [tok]
                    vring.free[hd["vk"]] = [tok]
            dout("o", o_d, BF16)
            if end_stage("s5"):
                return finish()

        for which in (() if os.environ.get('MK_SKIP6') else (0, 1)):
            with ExitStack() as st:
                actT = st.enter_context(nc.sbuf_tensor(f"s6act{which}", [128, GKC, TO], BF16))
                src = gm_d if which == 0 else o_d
                with ExitStack() as st2:
                    load_T(st2, src, GW, actT, [], f"s6l{which}")
                    if end_stage(f"s6l{which}"):
                        return finish()
                sgr = Ring(st, f"s6sg{which}", 3, [128, 512], BF16)
                t1r = Ring(st, f"s6t1{which}", 3, [128, 512], F32)
                stg_f = mk_stager(st, f"s6sf{which}", 3, [128, 512], F32)
                stg_b = mk_stager(st, f"s6sb{which}", 3, [128, 512], BF16)
                sg_src = sga_d if which == 0 else sgb_d

                def ep6(tt, c0, cw, ps, ready, which=which, sgr=sgr, t1r=t1r, stg_f=stg_f, stg_b=stg_b, sg_src=sg_src):
                    ep6m = int(os.environ.get("MK_EP6", "2"))
                    if ep6m == 0:
                        t, fr = stg_f.get()
                        tok = P.op("scalar", (lambda e, t=t, ps=ps, cw=cw: e.activation(
                            out=t[:, 0:cw], in_=ps[:, 0:cw], func=AF.Copy)), [ready] + fr, S_act)
                        stg_f.store(t1_d[tt * 128:(tt + 1) * 128, c0:c0 + cw], t[:, 0:cw], [tok], eng="scalar")
                        return tok
                    gk, gt_, gs_, gfree = sgr.next()
                    t_g = P.dma("sync", gt_[:, 0:cw], sg_src[tt * 128:(tt + 1) * 128, c0:c0 + cw], gfree, gs_)
                    if ep6m == 1:
                        t, fr = stg_f.get()
                        tok = P.op("scalar", (lambda e, t=t, ps=ps, cw=cw: e.activation(
                            out=t[:, 0:cw], in_=ps[:, 0:cw], func=AF.Copy)), [ready, t_g] + fr, S_act)
                        sgr.free[gk] = [tok]
                        stg_f.store(t1_d[tt * 128:(tt + 1) * 128, c0:c0 + cw], t[:, 0:cw], [tok], eng="scalar")
                        return tok
                    if which == 0:
                        t, fr = stg_f.get()
                        tok = P.op("vector", (lambda e, t=t, ps=ps, gt_=gt_, cw=cw: e.tensor_tensor(
                            out=t[:, 0:cw], in0=ps[:, 0:cw], in1=gt_[:, 0:cw], op=ALU.mult)), [ready, t_g] + fr, S_dve)
                        sgr.free[gk] = [tok]
                        stg_f.store(t1_d[tt * 128:(tt + 1) * 128, c0:c0 + cw], t[:, 0:cw], [tok], eng="scalar")
                        return tok
                    k1, t1, s1, f1 = t1r.next()
                    t_1 = P.dma("sync", t1[:, 0:cw], t1_d[tt * 128:(tt + 1) * 128, c0:c0 + cw], f1, s1)
                    t, fr = stg_b.get()
                    tok_a = P.op("vector", (lambda e, ps=ps, gt_=gt_, cw=cw: e.tensor_tensor(
                        out=gt_[:, 0:cw], in0=ps[:, 0:cw], in1=gt_[:, 0:cw], op=ALU.mult)), [ready, t_g], S_dve)
                    tok_b = P.op("vector", (lambda e, t=t, t1=t1, gt_=gt_, cw=cw: e.tensor_tensor(
                        out=t[:, 0:cw], in0=gt_[:, 0:cw], in1=t1[:, 0:cw], op=ALU.add)), [t_1, tok_a] + fr, S_dve)
                    sgr.free[gk] = [tok_b]
                    t1r.free[k1] = [tok_b]
                    stg_b.store(mix_d[tt * 128:(tt + 1) * 128, c0:c0 + cw], t[:, 0:cw], [tok_b], eng="scalar")
                    return tok_a

                wring = Ring(st, f"s6w{which}", 2, [128, GKC, 512], BF16)
                gemm(wring, actT, GKC, wpa if which == 0 else wpb, 0, 0, D, ep6)
                if which == 1:
                    dout("mix", mix_d, BF16)
                if end_stage(f"s6{which}"):
                    return finish()

        with ExitStack() as st:
            actT = st.enter_context(nc.sbuf_tensor("s8act", [128, KC, TO], BF16))
            with ExitStack() as st2:
                load_T(st2, (sga_d if os.environ.get("MK_S8SRC") else mix_d)[:, 0:int(os.environ.get("MK_LTK", D))], int(os.environ.get("MK_LTK", D)), actT, [], "s8l")
                if end_stage("s8l"):
                    return finish()
            stg_f = mk_stager(st, "s8sf", 3, [128, 512], F32)

            def ep8(tt, c0, cw, ps, ready):
                t, fr = stg_f.get()
                tok = P.op("scalar", (lambda e, t=t, ps=ps, cw=cw: e.activation(
                    out=t[:, 0:cw], in_=ps[:, 0:cw], func=AF.Copy)), [ready] + fr, S_act)
                stg_f.store(mo_d[tt * 128:(tt + 1) * 128, c0:c0 + cw], t[:, 0:cw], [tok], eng="scalar")
                return tok

            wring = Ring(st, "s8w", 2, [128, KC, 512], BF16)
            gemm(wring, actT, KC, wo, 0, 0, D, ep8)
            dout("mo", mo_d, F32)
            if end_stage("s8"):
                return finish()

        def post_norm_residual(st, br_d, res_d, gsrc, i_gt, dst_d, tag):
            gtb, _ = bcast_load(st, tag + "gt", modrow_ap(i_gt), D)
            gb, t3 = bcast_load(st, tag + "g", gsrc[0:1, :], D)
            t_gg = P.op("vector", lambda e: e.tensor_tensor(out=gtb[:], in0=gtb[:], in1=gb[:], op=ALU.mult), [t3], S_dve)
            bring = Ring(st, tag + "b", 2, [128, D], F32)
            rring = Ring(st, tag + "r", 2, [128, D], F32)
            junk = st.enter_context(nc.sbuf_tensor(tag + "junk", [128, D], BF16))
            stt = st.enter_context(nc.sbuf_tensor(tag + "st", [128, 4 * NTO], F32))
            stg = mk_stager(st, tag + "o", 2, [128, D], F32)
            for tt in range(NTO):
                bk, bt, bs_, bfree = bring.next()
                t_b = P.dma("sync", bt[:], br_d[tt * 128:(tt + 1) * 128, :], bfree, bs_)
                rk, rt, rs_, rfree = rring.next()
                t_r = P.dma("sync", rt[:], res_d[tt * 128:(tt + 1) * 128, :], rfree, rs_)
                sc = stt[:, tt * 4:(tt + 1) * 4]
                t_q = P.op("scalar", (lambda e, bt=bt, sc=sc: e.activation(
                    out=junk[:], in_=bt[:], func=AF.Square, accum_out=sc[:, 0:1])), [t_b], S_act)
                t_sd = P.op("scalar", (lambda e, sc=sc: e.activation(
                    out=sc[:, 1:2], in_=sc[:, 0:1], func=AF.Sqrt, scale=1.0 / D, bias=eps_t[:, 0:1])), [t_q], S_act)
                t_iv = P.op("vector", (lambda e, sc=sc: e.reciprocal(out=sc[:, 2:3], in_=sc[:, 1:2])), [t_sd], S_dve)
                t_m = P.op("vector", (lambda e, bt=bt, sc=sc: e.scalar_tensor_tensor(
                    out=bt[:], in0=bt[:], scalar=sc[:, 2:3], in1=gtb[:], op0=ALU.mult, op1=ALU.mult)),
                    [t_gg, t_iv], S_dve)
                ot, ofr = stg.get()
                tok = P.op("vector", (lambda e, bt=bt, rt=rt, ot=ot: e.tensor_tensor(
                    out=ot[:], in0=bt[:], in1=rt[:], op=ALU.add)), [t_r, t_m] + ofr, S_dve)
                bring.free[bk] = [tok]
                rring.free[rk] = [tok]
                stg.store(dst_d[tt * 128:(tt + 1) * 128, :], ot[:], [tok])

        with ExitStack() as st:
            post_norm_residual(st, mo_d, xo, gpost, 2, x1_d, "s9")
            dout("x1", x1_d, F32)
            if end_stage("s9"):
                return finish()

        with ExitStack() as st:
            h2T = st.enter_context(nc.sbuf_tensor("h2T", [128, KC, TO], BF16))
            with ExitStack() as st2:
                mb, shb, t_m = mod_tiles(st2, gpre2, 4, 3, "s10")

                def emit10(tt, kc0, nk, ps, ready):
                    return P.op("scalar", (lambda e, tt=tt, kc0=kc0, nk=nk, ps=ps: e.activation(
                        out=h2T[:, kc0:kc0 + nk, tt * 128:(tt + 1) * 128],
                        in_=ps[:, 0:nk * 128].rearrange("p (k t) -> p k t", k=nk), func=AF.Copy)), [ready], S_act)

                P.op("vector", None, [t_m])
                frontend(st2, x1_d, NTO, mb, shb, emit10, "s10")
                if end_stage("s10a"):
                    return finish()
            rr = Ring(st, "s10r", 3, [128, 512], F32)
            stg_b = mk_stager(st, "s10sb", 3, [128, 512], BF16)

            def ep10(tt, c0, cw, ps, ready):
                rk, rt, _, rfree = rr.next()
                tok = P.op("scalar", (lambda e, rt=rt, ps=ps, cw=cw: e.activation(
                    out=rt[:, 0:cw], in_=ps[:, 0:cw], func=AF.Relu)), [ready] + rfree, S_act)
                t, fr = stg_b.get()
                tok2 = P.op("vector", (lambda e, t=t, rt=rt, cw=cw: e.tensor_tensor(
                    out=t[:, 0:cw], in0=rt[:, 0:cw], in1=rt[:, 0:cw], op=ALU.mult)), [tok] + fr, S_dve)
                rr.free[rk] = [tok2]
                stg_b.store(f_d[tt * 128:(tt + 1) * 128, c0:c0 + cw], t[:, 0:cw], [tok2], eng="scalar")
                return tok

            wring = Ring(st, "s10w", 2, [128, KC, 512], BF16)
            gemm(wring, h2T, KC, wf1, 0, 0, DFF, ep10)
            dout("f", f_d, BF16)
            if end_stage("s10"):
                return finish()

        for kb in range(DFF // D):
            with ExitStack() as st:
                actT = st.enter_context(nc.sbuf_tensor(f"s11act{kb}", [128, KC, TO], BF16))
                with ExitStack() as st2:
                    load_T(st2, f_d[:, kb * D:(kb + 1) * D], D, actT, [], f"s11l{kb}")
                    if end_stage(f"s11l{kb}"):
                        return finish()
                ar = Ring(st, f"s11a{kb}", 3, [128, 512], F32)
                stg_f = mk_stager(st, f"s11sf{kb}", 3, [128, 512], F32)

                def ep11(tt, c0, cw, ps, ready, kb=kb, ar=ar, stg_f=stg_f):
                    t, fr = stg_f.get()
                    if kb == 0:
                        tok = P.op("scalar", (lambda e, t=t, ps=ps, cw=cw: e.activation(
                            out=t[:, 0:cw], in_=ps[:, 0:cw], func=AF.Copy)), [ready] + fr, S_act)
                    else:
                        ak, at, as_, afree = ar.next()
                        t_a = P.dma("sync", at[:, 0:cw], acc_d[tt * 128:(tt + 1) * 128, c0:c0 + cw], afree, as_)
                        tok = P.op("vector", (lambda e, t=t, ps=ps, at=at, cw=cw: e.tensor_tensor(
                            out=t[:, 0:cw], in0=ps[:, 0:cw], in1=at[:, 0:cw], op=ALU.add)), [ready, t_a] + fr, S_dve)
                        ar.free[ak] = [tok]
                    stg_f.store(acc_d[tt * 128:(tt + 1) * 128, c0:c0 + cw], t[:, 0:cw], [tok], eng="scalar")
                    return tok

                wring = Ring(st, f"s11w{kb}", 2, [128, KC, 512], BF16)
                gemm(wring, actT, KC, wf2, kb * D, 0, D, ep11)
                if end_stage(f"s11{kb}"):
                    return finish()

        with ExitStack() as st:
            post_norm_residual(st, acc_d, x1_d, gpost2, 5, y, "s12")
            end_stage("s12")
        return finish()


def own_rows(cfg, cid):
    blk = cfg.NC * np.arange(cfg.NTO) + cid
    return (blk[:, None] * 128 + np.arange(128)[None, :]).reshape(-1)


def prep_inputs(cfg, x, c, w_ada, b_ada, g_pre_mix, w_in, g_v, w_s, b_s, w_proj_a, w_proj_b, w_o,
                g_post_mix, g_pre_mlp, w_ff1, w_ff2, g_post_mlp):
    f = lambda a: np.ascontiguousarray(np.asarray(a, dtype=np.float32))
    xr = f(np.asarray(x)[0, ::-1, :])
    common = {
        "xr": xr,
        "cv": f(np.asarray(c)[0].reshape(cfg.KC, 128).T),
        "wada": f(np.asarray(w_ada)[0]),
        "bada": f(np.asarray(b_ada)[0][None, :]),
        "win": f(np.asarray(w_in)[0]),
        "gpre": f(np.asarray(g_pre_mix)[0][None, :]),
        "gv": f(np.asarray(g_v)[0][None, :]),
        "wsT": f(np.asarray(w_s)[0][:, ::-1, ::-1].transpose(2, 0, 1)),
        "bsr": f(np.asarray(b_s)[0][:, ::-1].T),
        "wpa": f(np.asarray(w_proj_a)[0]),
        "wpb": f(np.asarray(w_proj_b)[0]),
        "wo": f(np.asarray(w_o)[0]),
        "gpost": f(np.asarray(g_post_mix)[0][None, :]),
        "gpre2": f(np.asarray(g_pre_mlp)[0][None, :]),
        "wf1": f(np.asarray(w_ff1)[0]),
        "wf2": f(np.asarray(w_ff2)[0]),
        "gpost2": f(np.asarray(g_post_mlp)[0][None, :]),
        "identf": np.eye(128, dtype=np.float32),
        "kpos": np.arange(cfg.S, dtype=np.float32)[None, :],
    }
    in_maps = []
    for cid in range(cfg.NC):
        m = dict(common)
        rows = own_rows(cfg, cid)
        m["xo"] = f(xr[rows])
        m["qpos"] = f(rows.astype(np.float32).reshape(cfg.NTO, 128).T)
        in_maps.append(m)
    return in_maps


_CACHE = {}


def run(cfg, inputs, stop_after=None, debug=False, trace=False):
    key = (cfg.D, cfg.S, cfg.NC, stop_after, debug)
    if key not in _CACHE:
        _CACHE[key] = build_nc(cfg, stop_after, debug)
    nc = _CACHE[key]
    in_maps = prep_inputs(cfg, **inputs)
    return run_bass_kernel_spmd(nc, in_maps, core_ids=list(range(cfg.NC)), trace=trace)


def kernel(**inputs):
    cfg = Cfg()
    res = run(cfg, inputs)
    return assemble(cfg, [np.asarray(r["y"]) for r in res.results])


def assemble(cfg, ys):
    yr = np.empty((cfg.S, cfg.D), np.float32)
    for cid, yc in enumerate(ys):
        yr[own_rows(cfg, cid)] = yc
    return np.ascontiguousarray(yr[::-1])[None].astype(np.float32)
```
